# Optimizing a Trainium2 kernel written in Bass

```python
import math
import jax
import jax.numpy as jnp
from jax import lax
import numpy as np

D_MODEL = 2048
BATCH = 8
SEQ = 4096
DEPTH = 4

CTX_LEN = 256
GRID_W = 64
N_MIXERS = 3
MIXER_OF_LAYER = tuple(i % N_MIXERS for i in range(DEPTH))
N_SSD_LAYERS = MIXER_OF_LAYER.count(0)
N_ATTN_LAYERS = MIXER_OF_LAYER.count(1)
N_HYENA_LAYERS = MIXER_OF_LAYER.count(2)

ALPHA = (2.0 * DEPTH) ** 0.25
BETA = (8.0 * DEPTH) ** -0.25
LN_EPS = 1e-5
RMS_EPS = 1e-5
N_MOD = 6
MLP_HIDDEN = 4 * D_MODEL

SSM_EXPAND = 2
D_INNER = SSM_EXPAND * D_MODEL
SSM_HEAD_DIM = 64
SSM_HEADS = D_INNER // SSM_HEAD_DIM
SSM_GROUPS = 8
SSM_HEADS_PER_GROUP = SSM_HEADS // SSM_GROUPS
SSM_STATE = 128
SSM_CONV_W = 5
SSM_CHUNK = 128
SSM_XBC = D_INNER + 2 * SSM_GROUPS * SSM_STATE
SSM_IN = D_INNER + SSM_XBC + 2 * SSM_HEADS

ATTN_HEAD_DIM = 64
ATTN_Q_HEADS = D_MODEL // ATTN_HEAD_DIM
ATTN_KV_HEADS = 4
ATTN_GROUP = ATTN_Q_HEADS // ATTN_KV_HEADS
ATTN_QKV = (ATTN_Q_HEADS + 2 * ATTN_KV_HEADS) * ATTN_HEAD_DIM
WINDOW = 128
ATTN_BLOCK = 128
ROPE_BASE = 10000.0

HYENA_ORDER = 2
HYENA_SHORT_W = 3
HYENA_EMB = 33
HYENA_FILTER_W = 64
HYENA_DECAY_FAST = 0.3
HYENA_DECAY_SLOW = 1.5
HYENA_DECAY_TARGET = 1e-2

kernel_name = 'hybrid_ssd_swa_hyena_dit_trunk'

F32 = jnp.float32


def layer_norm(x, g, b):
    xf = x.astype(F32)
    mu = jnp.mean(xf, axis=-1, keepdims=True)
    var = jnp.mean(jnp.square(xf - mu), axis=-1, keepdims=True)
    y = (xf - mu) * lax.rsqrt(var + LN_EPS) * g.astype(F32) + b.astype(F32)
    return y.astype(x.dtype)


def dwconv_centred(x, w, b):
    width = w.shape[0]
    y = lax.conv_general_dilated(
        x, w[:, None, :].astype(x.dtype), window_strides=(1,),
        padding=[(width // 2, width // 2)],
        dimension_numbers=('NWC', 'WIO', 'NWC'), feature_group_count=x.shape[-1])
    return y + b


def sq_relu_mlp(u, w1, w2):
    return jnp.square(jax.nn.relu(u @ w1)) @ w2


def ssd_chunk_scan(xs, dt, a, bm, cm, h0):
    b, L = xs.shape[0], xs.shape[1]
    nc = L // SSM_CHUNK

    def to_chunks(t):
        return jnp.swapaxes(t.reshape((b, nc, SSM_CHUNK) + t.shape[2:]), 0, 1)

    lower = jnp.tril(jnp.ones((SSM_CHUNK, SSM_CHUNK), dtype=bool))[None, :, :, None, None]

    def step(h, inp):
        x_c, dt_c, b_c, c_c = inp
        x_c = x_c.astype(F32)
        b_c = b_c.astype(F32)
        c_c = c_c.astype(F32)
        cum = jnp.cumsum(dt_c * a, axis=1)
        decay = jnp.exp(jnp.where(lower, cum[:, :, None] - cum[:, None, :], -jnp.inf))
        cb = jnp.einsum('bign,bjgn->bijg', c_c, b_c)
        y = jnp.einsum('bijgh,bjghp->bighp', cb[..., None] * decay, x_c * dt_c[..., None])
        y = y + jnp.einsum('bign,bghpn->bighp', c_c, h) * jnp.exp(cum)[..., None]
        w = jnp.exp(cum[:, -1:] - cum) * dt_c
        h = h * jnp.exp(cum[:, -1])[..., None, None] + jnp.einsum('bjgh,bjghp,bjgn->bghpn', w, x_c, b_c)
        return h, y

    h, ys = lax.scan(step, h0, (to_chunks(xs), to_chunks(dt), to_chunks(bm), to_chunks(cm)))
    return jnp.swapaxes(ys, 0, 1).reshape(xs.shape), h


def ssd_mixer(u_lat, u_ctx, w_in, conv_w, conv_b, dt_bias, a_log, d_skip, norm_g, w_out, ctx_out):
    G, Hg, P, N = SSM_GROUPS, SSM_HEADS_PER_GROUP, SSM_HEAD_DIM, SSM_STATE
    a = -jnp.exp(a_log.astype(F32)).reshape(2, G, Hg)

    def project(u):
        b, n, _ = u.shape
        z, xbc, dt = jnp.split(u @ w_in, [D_INNER, D_INNER + SSM_XBC], axis=-1)
        xbc = jax.nn.silu(dwconv_centred(xbc, conv_w, conv_b))
        xs, bm, cm = jnp.split(xbc, [D_INNER, D_INNER + G * N], axis=-1)
        dt = jax.nn.softplus(dt.astype(F32).reshape(b, n, 2, SSM_HEADS) + dt_bias.astype(F32))
        return (z, xs.reshape(b, n, G, Hg, P), bm.reshape(b, n, G, N), cm.reshape(b, n, G, N),
                dt.reshape(b, n, 2, G, Hg))

    def rev(t):
        return jnp.flip(t, axis=1)

    def bidir(xs, bm, cm, dt, h_f, h_b):
        y_f, s_f = ssd_chunk_scan(xs, dt[:, :, 0], a[0], bm, cm, h_f)
        y_b, s_b = ssd_chunk_scan(rev(xs), rev(dt[:, :, 1]), a[1], rev(bm), rev(cm), h_b)
        return y_f + rev(y_b), s_f, s_b

    def finish(y, xs, z):
        b, n = y.shape[0], y.shape[1]
        y = y + d_skip.astype(F32).reshape(G, Hg)[..., None] * xs.astype(F32)
        y = y.reshape(b, n, D_INNER) * jax.nn.silu(z.astype(F32))
        yg = y.reshape(b, n, G, D_INNER // G)
        yg = yg * lax.rsqrt(jnp.mean(jnp.square(yg), axis=-1, keepdims=True) + RMS_EPS)
        y = (yg.reshape(b, n, D_INNER) * norm_g.astype(F32)).astype(u_lat.dtype)
        return y @ w_out

    zc, xc, bc, cc, dtc = project(u_ctx)
    zl, xl, bl, cl, dtl = project(u_lat)
    h0 = jnp.zeros((u_lat.shape[0], G, Hg, P, N), F32)
    yc, hc_f, hc_b = bidir(xc, bc, cc, dtc, h0, h0)
    yl, _, _ = bidir(xl, bl, cl, dtl, hc_f, hc_b)
    y_lat = finish(yl, xl, zl)
    y_ctx = finish(yc, xc, zc) if ctx_out else None
    return y_lat, y_ctx


def axial_rope_angles(L):
    rows = L // GRID_W
    row_id = jnp.broadcast_to(jnp.arange(rows, dtype=F32)[:, None], (rows, GRID_W)).reshape(-1)
    col_id = jnp.broadcast_to(jnp.arange(GRID_W, dtype=F32)[None, :], (rows, GRID_W)).reshape(-1)
    pairs_per_axis = ATTN_HEAD_DIM // 4
    inv = ROPE_BASE ** (-jnp.arange(pairs_per_axis, dtype=F32) / pairs_per_axis)
    ang = jnp.concatenate([row_id[:, None] * inv, col_id[:, None] * inv], axis=-1)
    return jnp.cos(ang), jnp.sin(ang)


def apply_rope(x, cos, sin):
    xf = x.astype(F32).reshape(x.shape[:-1] + (x.shape[-1] // 2, 2))
    x1, x2 = xf[..., 0], xf[..., 1]
    out = jnp.stack([x1 * cos - x2 * sin, x1 * sin + x2 * cos], axis=-1)
    return out.reshape(x.shape).astype(x.dtype)


def windowed_gqa_mixer(u_lat, u_ctx, w_qkv, sink, w_o, ctx_out):
    b, L, _ = u_lat.shape
    KV, G, Dh, BLK = ATTN_KV_HEADS, ATTN_GROUP, ATTN_HEAD_DIM, ATTN_BLOCK
    scale = Dh ** -0.5

    def project(u):
        n = u.shape[1]
        q, k, v = jnp.split(u @ w_qkv, [ATTN_Q_HEADS * Dh, (ATTN_Q_HEADS + KV) * Dh], axis=-1)
        return (q.reshape(b, n, KV, G, Dh) * scale, k.reshape(b, n, KV, Dh), v.reshape(b, n, KV, Dh))

    q_l, k_l, v_l = project(u_lat)
    cos, sin = axial_rope_angles(L)
    q_l = apply_rope(q_l, cos[:, None, None], sin[:, None, None])
    k_l = apply_rope(k_l, cos[:, None], sin[:, None])
    q_c, k_c, v_c = project(u_ctx)
    sink_b = sink.astype(F32).reshape(KV, G)[:, :, None, None]

    nblk = L // BLK
    qb = jnp.swapaxes(q_l.reshape(b, nblk, BLK, KV, G, Dh), 0, 1)

    def windows(t):
        tp = jnp.pad(t, ((0, 0), (BLK, BLK), (0, 0), (0, 0))).reshape(b, nblk + 2, BLK, KV, Dh)
        w = jnp.concatenate([tp[:, :-2], tp[:, 1:-1], tp[:, 2:]], axis=2)
        return jnp.swapaxes(w, 0, 1)

    kw, vw = windows(k_l), windows(v_l)
    r_idx = jnp.arange(BLK)[:, None]
    s_idx = jnp.arange(3 * BLK)[None, :]
    band = jnp.abs(s_idx - BLK - r_idx) <= WINDOW

    def block_attend(args):
        q, k, v, n = args
        kpos = (n - 1) * BLK + s_idx
        mask = band & (kpos >= 0) & (kpos < L)
        s_loc = jnp.einsum('bqhgd,bshd->bhgqs', q, k).astype(F32)
        s_loc = jnp.where(mask, s_loc, -jnp.inf)
        s_ctx = jnp.einsum('bqhgd,bchd->bhgqc', q, k_c).astype(F32)
        logits = jnp.concatenate(
            [s_loc, s_ctx, jnp.broadcast_to(sink_b, s_loc.shape[:-1] + (1,))], axis=-1)
        p = jax.nn.softmax(logits, axis=-1).astype(v.dtype)
        o = jnp.einsum('bhgqs,bshd->bqhgd', p[..., :3 * BLK], v)
        o = o + jnp.einsum('bhgqc,bchd->bqhgd', p[..., 3 * BLK:-1], v_c)
        return o

    o = lax.map(block_attend, (qb, kw, vw, jnp.arange(nblk)))
    y_lat = jnp.swapaxes(o, 0, 1).reshape(b, L, ATTN_Q_HEADS * Dh) @ w_o
    y_ctx = None
    if ctx_out:
        s_c = jnp.einsum('bqhgd,bchd->bhgqc', q_c, k_c).astype(F32)
        logits = jnp.concatenate([s_c, jnp.broadcast_to(sink_b, s_c.shape[:-1] + (1,))], axis=-1)
        p = jax.nn.softmax(logits, axis=-1)[..., :-1].astype(v_c.dtype)
        o_c = jnp.einsum('bhgqc,bchd->bqhgd', p, v_c)
        y_ctx = o_c.reshape(b, u_ctx.shape[1], ATTN_Q_HEADS * Dh) @ w_o
    return y_lat, y_ctx


def hyena_filter_spectra(L, w1, b1, w2, b2, w3, b3, w4, freq):
    t = jnp.linspace(0.0, 1.0, L, dtype=F32)[:, None]
    bands = (HYENA_EMB - 1) // 2
    w = 2.0 * math.pi * jnp.arange(L, dtype=F32)[:, None] / L
    f = jnp.linspace(1e-4, bands - 1, bands, dtype=F32)[None, :]
    z = jnp.concatenate([t, jnp.cos(f * w), -jnp.sin(f * w)], axis=-1)
    fr = freq.astype(F32)
    h = jnp.sin(fr * (z @ w1.astype(F32) + b1.astype(F32)))
    h = jnp.sin(fr * (h @ w2.astype(F32) + b2.astype(F32)))
    h = jnp.sin(fr * (h @ w3.astype(F32) + b3.astype(F32)))
    h = (h @ w4.astype(F32)).reshape(L, HYENA_ORDER, 2, D_MODEL)
    max_decay = math.log(HYENA_DECAY_TARGET) / HYENA_DECAY_FAST
    min_decay = math.log(HYENA_DECAY_TARGET) / HYENA_DECAY_SLOW
    deltas = jnp.linspace(min_decay, max_decay, D_MODEL, dtype=F32)
    h = h * jnp.exp(-t * jnp.abs(deltas))[:, None, None, :]
    fwd = h[:, :, 0]
    bwd = jnp.flip(h[1:, :, 1], axis=0)
    full = jnp.concatenate([fwd, jnp.zeros((1, HYENA_ORDER, D_MODEL), F32), bwd], axis=0)
    return jnp.fft.rfft(full, axis=0)


def fft_long_conv(u, spec, bias):
    L = u.shape[1]
    uf = u.astype(F32)
    y = jnp.fft.irfft(jnp.fft.rfft(uf, n=2 * L, axis=1) * spec, n=2 * L, axis=1)[:, :L]
    return (y + uf * bias.astype(F32)).astype(u.dtype)


def hyena_mixer(u_lat, u_ctx, w_in, conv_w, conv_b, f_w1, f_b1, f_w2, f_b2, f_w3, f_b3, f_w4,
                f_freq, f_bias, w_out, ctx_out):
    def run(u):
        spec = hyena_filter_spectra(u.shape[1], f_w1, f_b1, f_w2, f_b2, f_w3, f_b3, f_w4, f_freq)
        x1, x2, v = jnp.split(dwconv_centred(u @ w_in, conv_w, conv_b), 3, axis=-1)
        z = x1 * fft_long_conv(v, spec[:, 0], f_bias[0])
        y = x2 * fft_long_conv(z, spec[:, 1], f_bias[1])
        return y @ w_out

    return run(u_lat), (run(u_ctx) if ctx_out else None)


def setup_inputs(seed: int = 0) -> dict:
    key = jax.random.key(seed)
    ks = iter(list(jax.random.split(key, 48)))

    def nrm(shape, scale=1.0):
        return jax.random.normal(next(ks), shape, F32) * scale

    D = D_MODEL
    NA, NB, NC = N_SSD_LAYERS, N_ATTN_LAYERS, N_HYENA_LAYERS
    dt0 = jnp.exp(jax.random.uniform(next(ks), (NA, 2, SSM_HEADS), F32,
                                     minval=math.log(1e-3), maxval=math.log(1e-1)))
    dt_bias = dt0 + jnp.log(-jnp.expm1(-dt0))
    a_log = jnp.log(jax.random.uniform(next(ks), (NA, 2, SSM_HEADS), F32, minval=1.0, maxval=16.0))
    return {
        'x': nrm((BATCH, SEQ, D)),
        'c': nrm((BATCH, D)),
        'ctx': nrm((BATCH, CTX_LEN, D)),
        'c_ctx': nrm((D,)),
        'ada_w': nrm((DEPTH, D, N_MOD * D), D ** -0.5),
        'ada_b': nrm((DEPTH, N_MOD * D), 0.02),
        'ln_g': 1.0 + nrm((DEPTH, 2, D), 0.02),
        'ln_b': nrm((DEPTH, 2, D), 0.02),
        'mlp_w1': nrm((DEPTH, D, MLP_HIDDEN), D ** -0.5),
        'mlp_w2': nrm((DEPTH, MLP_HIDDEN, D), BETA * MLP_HIDDEN ** -0.5),
        'ssd_w_in': nrm((NA, D, SSM_IN), D ** -0.5),
        'ssd_conv_w': nrm((NA, SSM_CONV_W, SSM_XBC), SSM_CONV_W ** -0.5),
        'ssd_conv_b': nrm((NA, SSM_XBC), 0.01),
        'ssd_dt_bias': dt_bias,
        'ssd_a_log': a_log,
        'ssd_d': 1.0 + nrm((NA, SSM_HEADS), 0.02),
        'ssd_norm_g': 1.0 + nrm((NA, D_INNER), 0.02),
        'ssd_w_out': nrm((NA, D_INNER, D), BETA * D_INNER ** -0.5),
        'attn_w_qkv': nrm((NB, D, ATTN_QKV), D ** -0.5),
        'attn_sink': nrm((NB, ATTN_Q_HEADS), 0.5),
        'attn_w_o': nrm((NB, ATTN_Q_HEADS * ATTN_HEAD_DIM, D), BETA * (ATTN_Q_HEADS * ATTN_HEAD_DIM) ** -0.5),
        'hy_w_in': nrm((NC, D, 3 * D), D ** -0.5),
        'hy_conv_w': nrm((NC, HYENA_SHORT_W, 3 * D), HYENA_SHORT_W ** -0.5),
        'hy_conv_b': nrm((NC, 3 * D), 0.01),
        'hy_f_w1': nrm((NC, HYENA_EMB, HYENA_FILTER_W), HYENA_EMB ** -0.5),
        'hy_f_b1': nrm((NC, HYENA_FILTER_W), 0.02),
        'hy_f_w2': nrm((NC, HYENA_FILTER_W, HYENA_FILTER_W), HYENA_FILTER_W ** -0.5),
        'hy_f_b2': nrm((NC, HYENA_FILTER_W), 0.02),
        'hy_f_w3': nrm((NC, HYENA_FILTER_W, HYENA_FILTER_W), HYENA_FILTER_W ** -0.5),
        'hy_f_b3': nrm((NC, HYENA_FILTER_W), 0.02),
        'hy_f_w4': nrm((NC, HYENA_FILTER_W, HYENA_ORDER * 2 * D), 0.1 * HYENA_FILTER_W ** -0.5),
        'hy_f_freq': 1.0 + nrm((NC, HYENA_FILTER_W), 0.01),
        'hy_f_bias': nrm((NC, HYENA_ORDER, D), 0.5),
        'hy_w_out': nrm((NC, D, D), BETA * D ** -0.5),
    }


def reference(x, c, ctx, c_ctx, ada_w, ada_b, ln_g, ln_b, mlp_w1, mlp_w2,
              ssd_w_in, ssd_conv_w, ssd_conv_b, ssd_dt_bias, ssd_a_log, ssd_d, ssd_norm_g, ssd_w_out,
              attn_w_qkv, attn_sink, attn_w_o,
              hy_w_in, hy_conv_w, hy_conv_b, hy_f_w1, hy_f_b1, hy_f_w2, hy_f_b2, hy_f_w3, hy_f_b3,
              hy_f_w4, hy_f_freq, hy_f_bias, hy_w_out):
    xl, xc = x, ctx
    cond_l = jax.nn.silu(c.astype(F32)).astype(x.dtype)
    cond_c = jax.nn.silu(c_ctx.astype(F32)).astype(x.dtype)
    for i in range(DEPTH):
        last = i == DEPTH - 1
        kind = MIXER_OF_LAYER[i]
        j = MIXER_OF_LAYER[:i].count(kind)
        mod_l = (cond_l @ ada_w[i] + ada_b[i])[:, None, :]
        mod_c = cond_c @ ada_w[i] + ada_b[i]
        sh1, sc1, g1, sh2, sc2, g2 = jnp.split(mod_l, N_MOD, axis=-1)
        csh1, csc1, cg1, csh2, csc2, cg2 = jnp.split(mod_c, N_MOD, axis=-1)
        ul = xl * (1.0 + sc1) + sh1
        uc = xc * (1.0 + csc1) + csh1
        if kind == 0:
            yl, yc = ssd_mixer(ul, uc, ssd_w_in[j], ssd_conv_w[j], ssd_conv_b[j], ssd_dt_bias[j],
                               ssd_a_log[j], ssd_d[j], ssd_norm_g[j], ssd_w_out[j], not last)
        elif kind == 1:
            yl, yc = windowed_gqa_mixer(ul, uc, attn_w_qkv[j], attn_sink[j], attn_w_o[j], not last)
        else:
            yl, yc = hyena_mixer(ul, uc, hy_w_in[j], hy_conv_w[j], hy_conv_b[j], hy_f_w1[j], hy_f_b1[j],
                                 hy_f_w2[j], hy_f_b2[j], hy_f_w3[j], hy_f_b3[j], hy_f_w4[j],
                                 hy_f_freq[j], hy_f_bias[j], hy_w_out[j], not last)
        xl = layer_norm(ALPHA * xl + g1 * yl, ln_g[i, 0], ln_b[i, 0])
        ml = sq_relu_mlp(xl * (1.0 + sc2) + sh2, mlp_w1[i], mlp_w2[i])
        xl = layer_norm(ALPHA * xl + g2 * ml, ln_g[i, 1], ln_b[i, 1])
        if not last:
            xc = layer_norm(ALPHA * xc + cg1 * yc, ln_g[i, 0], ln_b[i, 0])
            mc = sq_relu_mlp(xc * (1.0 + csc2) + csh2, mlp_w1[i], mlp_w2[i])
            xc = layer_norm(ALPHA * xc + cg2 * mc, ln_g[i, 1], ln_b[i, 1])
    return xl
```

```python
import math
import os
from contextlib import ExitStack

import numpy as np
import ml_dtypes

import concourse.bass as bass
import concourse.mybir as mybir
from concourse.bass_utils import run_bass_kernel_spmd

F32 = mybir.dt.float32
BF16 = mybir.dt.bfloat16
AF = mybir.ActivationFunctionType
ALU = mybir.AluOpType
AX = mybir.AxisListType

D = 2048
SEQ = 4096
CTX = 256
NTOK = SEQ + CTX
DEPTH = 4
ALPHA = (2.0 * DEPTH) ** 0.25
LN_EPS = 1e-5
RMS_EPS = 1e-5
HID = 4 * D
DC = D // 128
MIXER = (0, 1, 2, 0)
BLOCKS = [(0, CTX, True)] + [(CTX + i * 512, 512, False) for i in range(SEQ // 512)]

D_INNER = 2 * D
SSM_HEADS = 64
SSM_G = 8
SSM_N = 128
SSM_XBC = D_INNER + 2 * SSM_G * SSM_N
SSM_IN = D_INNER + SSM_XBC + 2 * SSM_HEADS

ENGS = ("pe", "act", "dve", "pool", "sp")


class Buf:
    __slots__ = ("name", "w", "r")

    def __init__(self, name=""):
        self.name = name
        self.w = None
        self.r = []


class TT:
    def __init__(self, t, name="", nb=1):
        self.t = t
        self.b = Buf(name)
        self.bs = [Buf(name + str(i)) for i in range(nb)] if nb > 1 else [self.b]

    def __getitem__(self, k):
        return self.t[k]


class Sch:
    RINGS = {"sp": 16, "pool": 8, "act": 4}

    def __init__(self, nc):
        self.nc = nc
        self.q = {e: [] for e in ENGS}
        self.cnt = {e: 0 for e in ENGS}
        self.seen = {e: {} for e in ENGS}
        self.esem = {}
        self.dsem = []
        self.dval = []
        self.ring = {}
        self.rpos = {}
        self.mute = False

    def setup(self, stack):
        nc = self.nc
        for e in ENGS:
            self.esem[e] = stack.enter_context(nc.semaphore("s_" + e))
        for e, n in self.RINGS.items():
            self.ring[e] = []
            self.rpos[e] = 0
            for i in range(n):
                self.ring[e].append(len(self.dsem))
                self.dsem.append(stack.enter_context(nc.semaphore("d_%s%d" % (e, i))))
                self.dval.append(0)

    def _wait(self, eng, ev):
        if ev is None:
            return
        kind, a, v = ev
        if kind == "e" and a == eng and eng == "pe":
            return
        key = (kind, a)
        if self.seen[eng].get(key, 0) >= v:
            return
        self.seen[eng][key] = v
        sem = self.esem[a] if kind == "e" else self.dsem[a]
        self.q[eng].append(lambda E, sem=sem, v=v: E.wait_ge(sem, v))

    def _deps(self, eng, reads, writes):
        for b in reads:
            self._wait(eng, b.w)
        for b in writes:
            self._wait(eng, b.w)
            for ev in b.r:
                self._wait(eng, ev)

    def _commit(self, ev, reads, writes):
        for b in reads:
            b.r.append(ev)
            if len(b.r) > 48:
                last = {}
                for e in b.r:
                    k = (e[0], e[1])
                    if k not in last or last[k][2] < e[2]:
                        last[k] = e
                b.r = list(last.values())
        for b in writes:
            b.w = ev
            b.r = []

    def op(self, eng, fn, reads=(), writes=()):
        if self.mute:
            return
        self._deps(eng, reads, writes)
        self.cnt[eng] += 1
        idx = self.cnt[eng]
        sem = self.esem[eng]
        self.q[eng].append(lambda E, fn=fn, sem=sem: fn(E).then_inc(sem, 1))
        self._commit(("e", eng, idx), reads, writes)

    def group(self, eng, fns, reads=(), writes=()):
        if self.mute:
            return
        self._deps(eng, reads, writes)
        self.cnt[eng] += 1
        idx = self.cnt[eng]
        sem = self.esem[eng]
        for f in fns[:-1]:
            self.q[eng].append(lambda E, f=f: f(E))
        self.q[eng].append(lambda E, f=fns[-1], sem=sem: f(E).then_inc(sem, 1))
        self._commit(("e", eng, idx), reads, writes)

    def dma(self, eng, out, in_, reads=(), writes=(), **kw):
        if self.mute:
            return
        self._deps(eng, reads, writes)
        ring = self.ring[eng]
        si = ring[self.rpos[eng] % len(ring)]
        self.rpos[eng] += 1
        if self.dval[si] > 0:
            self._wait(eng, ("d", si, self.dval[si]))
        self.dval[si] += 16
        v = self.dval[si]
        sem = self.dsem[si]
        self.q[eng].append(
            lambda E, out=out, in_=in_, sem=sem, kw=kw: E.dma_start(out=out, in_=in_, **kw).then_inc(sem, 16))
        self._commit(("d", si, v), reads, writes)

    def barrier(self):
        for e in ENGS:
            for e2 in ENGS:
                if e2 != e and self.cnt[e2] > 0:
                    self._wait(e, ("e", e2, self.cnt[e2]))
            for si in range(len(self.dsem)):
                if self.dval[si] > 0:
                    self._wait(e, ("d", si, self.dval[si]))

    def emit(self):
        nc = self.nc
        self.barrier()
        q = self.q
        with nc.Block() as block:
            @block.tensor
            def _(E):
                for f in q["pe"]:
                    f(E)

            @block.scalar
            def _(E):
                for f in q["act"]:
                    f(E)

            @block.vector
            def _(E):
                for f in q["dve"]:
                    f(E)

            @block.gpsimd
            def _(E):
                for f in q["pool"]:
                    f(E)

            @block.sync
            def _(E):
                for f in q["sp"]:
                    f(E)


def bl(ts):
    out = []
    for t in ts:
        if t is None:
            continue
        if isinstance(t, Buf):
            out.append(t)
        elif isinstance(t, TT):
            out.extend(t.bs)
        else:
            out.extend(bl(t))
    return out


class Ring:
    def __init__(self, items):
        self.items = items
        self.i = 0

    def next(self):
        t = self.items[self.i % len(self.items)]
        self.i += 1
        return t


class KB:
    def __init__(self, nc, stack):
        self.nc = nc
        self.st = stack
        self.S = Sch(nc)
        self.S.setup(stack)
        self.uid = 0

    def nm(self, p):
        self.uid += 1
        return "%s_%d" % (p, self.uid)

    def sb(self, shape, dt, name="sb", stack=None, nb=1):
        n = self.nm(name)
        t = (stack or self.st).enter_context(self.nc.sbuf_tensor(n, list(shape), dt))
        return TT(t, n, nb)

    def ps(self, shape, dt=F32, name="ps", stack=None):
        n = self.nm(name)
        t = (stack or self.st).enter_context(self.nc.psum_tensor(n, list(shape), dt))
        return TT(t, n)

    def dram(self, shape, dt, name="dr", nb=1):
        n = self.nm(name)
        t = self.nc.dram_tensor(n, list(shape), dt).ap()
        return TT(t, n, nb)

    def act(self, out, in_, func, reads, writes, **kw):
        self.S.op("act", lambda E: E.activation(out=out, in_=in_, func=func, **kw), bl(reads), bl(writes))

    def tcopy(self, eng, out, in_, reads, writes):
        self.S.op(eng, lambda E: E.tensor_copy(out=out, in_=in_), bl(reads), bl(writes))

    def tt(self, eng, out, in0, in1, op, reads, writes):
        self.S.op(eng, lambda E: E.tensor_tensor(out=out, in0=in0, in1=in1, op=op), bl(reads), bl(writes))

    def ts(self, eng, out, in0, s1, s2, op0, op1, reads, writes):
        if s2 is None:
            self.S.op(eng, lambda E: E.tensor_scalar(out=out, in0=in0, scalar1=s1, scalar2=None, op0=op0),
                      bl(reads), bl(writes))
        else:
            self.S.op(eng, lambda E: E.tensor_scalar(out=out, in0=in0, scalar1=s1, scalar2=s2, op0=op0, op1=op1),
                      bl(reads), bl(writes))

    def stt(self, eng, out, in0, scalar, in1, op0, op1, reads, writes):
        self.S.op(eng, lambda E: E.scalar_tensor_tensor(out=out, in0=in0, scalar=scalar, in1=in1, op0=op0, op1=op1),
                  bl(reads), bl(writes))

    def memset(self, eng, ap, val, writes):
        self.S.op(eng, lambda E: E.memset(ap, val), [], bl(writes))

    def dma(self, eng, out, in_, reads, writes, **kw):
        self.S.dma(eng, out, in_, bl(reads), bl(writes), **kw)

    def mm(self, specs, reads, writes):
        fns = []
        for (o, l, r, s0, s1) in specs:
            fns.append(lambda E, o=o, l=l, r=r, s0=s0, s1=s1: E.matmul(o, lhsT=l, rhs=r, start=s0, stop=s1))
        self.S.group("pe", fns, bl(reads), bl(writes))

    def tr(self, specs, reads, writes):
        fns = []
        for (o, i, idn) in specs:
            fns.append(lambda E, o=o, i=i, idn=idn: E.transpose(out=o, in_=i, identity=idn))
        self.S.group("pe", fns, bl(reads), bl(writes))

    def barrier(self):
        self.S.barrier()


def host_consts():
    c = {}
    c["identf"] = np.eye(128, dtype=np.float32)
    c["onesf"] = np.ones((128, 128), dtype=np.float32)
    j = np.arange(128)
    c["triu"] = (j[:, None] <= j[None, :]).astype(np.float32)
    c["tril"] = (j[:, None] >= j[None, :]).astype(np.float32)
    rot = np.zeros((128, 128), np.float32)
    for i in range(64):
        rot[2 * i + 1, 2 * i] = -1.0
        rot[2 * i, 2 * i + 1] = 1.0
    c["rotT"] = rot
    t = np.arange(SEQ)
    inv = (10000.0 ** (-np.arange(16, dtype=np.float32) / 16)).astype(np.float32)
    ang = np.concatenate([(t // 64).astype(np.float32)[:, None] * inv, (t % 64).astype(np.float32)[:, None] * inv], -1)
    dd = (np.arange(128) % 64) // 2
    c["ropec"] = np.ascontiguousarray(np.cos(ang)[:, dd].T.astype(np.float32))
    c["ropes"] = np.ascontiguousarray(np.sin(ang)[:, dd].T.astype(np.float32))
    NEG = -30000.0
    am = np.zeros((3, 128, 640), np.float32)
    left = (c["triu"] - 1.0) * -NEG
    right = (c["tril"] - 1.0) * -NEG
    for v in range(3):
        am[v, :, 256:384] = left if v != 0 else NEG
        am[v, :, 512:640] = right if v != 2 else NEG
    c["amask"] = am
    bf = ml_dtypes.bfloat16
    max_decay = math.log(1e-2) / 0.3
    min_decay = math.log(1e-2) / 1.5
    deltas = np.linspace(min_decay, max_decay, D, dtype=np.float32)
    c["habsd"] = np.abs(deltas)[None, :].astype(np.float32)
    for gname, L in (("lat", SEQ), ("ctx", CTX)):
        nt = L // 128
        nf = nt + 1
        N = 2 * L
        t = np.linspace(0.0, 1.0, L, dtype=np.float32)[:, None]
        w = (np.float32(2.0 * math.pi) * np.arange(L, dtype=np.float32)[:, None] / np.float32(L)).astype(np.float32)
        f = np.linspace(1e-4, 15, 16, dtype=np.float32)[None, :]
        z = np.concatenate([t, np.cos(f * w), -np.sin(f * w)], axis=-1).astype(np.float32)
        c["hz_" + gname] = np.ascontiguousarray(z.T)
        c["hnt_" + gname] = np.ascontiguousarray((-t[:, 0]).reshape(nt, 128).T.astype(np.float32))
        a = np.arange(L, dtype=np.int64)
        fr = np.arange(nf * 128, dtype=np.int64)
        ang = (2.0 * np.pi / N) * ((a[:, None] * fr[None, :]) % N).astype(np.float64)
        valid = (fr <= L)[None, :]
        Cm = np.where(valid, np.cos(ang), 0.0)
        Sm = np.where(valid, np.sin(ang), 0.0)
        c["hC_" + gname] = np.ascontiguousarray(Cm.reshape(nt, 128, nf, 128).transpose(2, 1, 0, 3)).astype(bf)
        c["hS_" + gname] = np.ascontiguousarray(Sm.reshape(nt, 128, nf, 128).transpose(2, 1, 0, 3)).astype(bf)
        wf = np.where((fr == 0) | (fr == L), 1.0, 2.0) / N
        wf = np.where(fr <= L, wf, 0.0)
        Ci = (Cm * wf[None, :]).T
        Si = (Sm * wf[None, :]).T
        c["hCw_" + gname] = np.ascontiguousarray(Ci.reshape(nf, 128, nt, 128).transpose(2, 1, 0, 3)).astype(bf)
        c["hSw_" + gname] = np.ascontiguousarray(Si.reshape(nf, 128, nt, 128).transpose(2, 1, 0, 3)).astype(bf)
    return c


_HC_CACHE = {}


def host_consts_cached():
    if "c" not in _HC_CACHE:
        _HC_CACHE["c"] = host_consts()
    return _HC_CACHE["c"]


WEIGHT_SHAPES = {
    "ada_w": [DEPTH, D, 6 * D], "ada_b": [DEPTH, 6 * D], "ln_g": [DEPTH, 2, D], "ln_b": [DEPTH, 2, D],
    "mlp_w1": [DEPTH, D, HID], "mlp_w2": [DEPTH, HID, D],
    "ssd_w_in": [2, D, SSM_IN], "ssd_conv_w": [2, 5, SSM_XBC], "ssd_conv_b": [2, SSM_XBC],
    "ssd_dt_bias": [2, 2, 64], "ssd_a_log": [2, 2, 64], "ssd_d": [2, 64], "ssd_norm_g": [2, D_INNER],
    "ssd_w_out": [2, D_INNER, D],
    "attn_w_qkv": [1, D, 2560], "attn_sink": [1, 32], "attn_w_o": [1, D, D],
    "hy_w_in": [1, D, 3 * D], "hy_conv_w": [1, 3, 3 * D], "hy_conv_b": [1, 3 * D],
    "hy_f_w1": [1, 33, 64], "hy_f_b1": [1, 64], "hy_f_w2": [1, 64, 64], "hy_f_b2": [1, 64],
    "hy_f_w3": [1, 64, 64], "hy_f_b3": [1, 64], "hy_f_w4": [1, 64, 4 * D], "hy_f_freq": [1, 64],
    "hy_f_bias": [1, 2, D], "hy_w_out": [1, D, D],
}


class Prog(KB):
    def __init__(self, nc, stack, cfg=None):
        super().__init__(nc, stack)
        self.cfg = cfg or {}
        nc_ = nc
        self.inp = {}
        ein = lambda n, s: nc_.dram_tensor(n, list(s), F32, kind="ExternalInput").ap()
        self.inp["x"] = ein("x", [SEQ, D])
        self.inp["c"] = ein("c", [1, D])
        self.inp["ctx"] = ein("ctx", [CTX, D])
        self.inp["c_ctx"] = ein("c_ctx", [1, D])
        self.nl = self.cfg.get("nl", DEPTH)
        for n, s in WEIGHT_SHAPES.items():
            s = list(s)
            if n in ("ada_w", "ada_b", "ln_g", "ln_b", "mlp_w1", "mlp_w2"):
                s[0] = self.nl
            if n in self.cfg.get("skip_inputs", ()):
                continue
            self.inp[n] = ein(n, s)
        self.hc = host_consts_cached()
        self.blocks = [(i, BLOCKS[i]) for i in self.cfg.get('blocks', range(len(BLOCKS)))]
        for n, a in self.hc.items():
            if n.startswith("h") and self.cfg.get("no_hyena"):
                continue
            dt_ = BF16 if a.dtype == ml_dtypes.bfloat16 else F32
            self.inp[n] = nc_.dram_tensor(n, list(a.shape), dt_, kind="ExternalInput").ap()
        self.out = TT(nc_.dram_tensor("out", [SEQ, D], F32, kind="ExternalOutput").ap(), "out")
        self.XT = self.dram([D, NTOK], F32, "XT", nb=len(BLOCKS))
        self.xt_v = self.XT.t.rearrange("(k p) t -> p k t", p=128)
        self.w16 = {}
        self.pending = []
        self.tick = 0
        self.drip_every = 4
        self.setup_consts()

    def setup_consts(self):
        self.identf = self.sb([128, 128], F32, "identf")
        self.onesf = self.sb([128, 128], F32, "onesf")
        self.identb = self.sb([128, 128], BF16, "identb")
        self.dma("sp", self.identf[:], self.inp["identf"], [], [self.identf])
        self.dma("sp", self.onesf[:], self.inp["onesf"], [], [self.onesf])
        self.tcopy("dve", self.identb[:], self.identf[:], [self.identf], [self.identb])
        self.MOD = self.sb([128, DEPTH, 96, 2], F32, "MOD")
        self.LNG = self.sb([128, DEPTH * 2 * DC], F32, "LNG")
        self.LNB = self.sb([128, DEPTH * 2 * DC], F32, "LNB")

    def load_vec_fm(self, rows_ap, n, out_ap, out_tt, st):
        if not hasattr(st, "_lv"):
            st._lv = (self.sb([128, 128], F32, "lv", st), self.ps([128, 128], F32, "lvp", st))
        tmp, pst = st._lv
        self.dma("sp", tmp[0:n, :], rows_ap, [], [tmp])
        self.tr([(pst[:, 0:n], tmp[0:n, :], self.identf[0:n, 0:n])], [tmp, self.identf], [pst])
        self.tcopy("dve", out_ap, pst[:, 0:n], [pst], [out_tt])

    def stage_mod(self):
        with ExitStack() as st:
            ADAB = self.sb([128, DEPTH * 96], F32, "ADAB", st)
            ab = self.inp["ada_b"].rearrange("l (m p) -> (l m) p", p=128)
            nrow = self.nl * 96
            for r0 in range(0, nrow, 128):
                rn = min(128, nrow - r0)
                self.load_vec_fm(ab[r0:r0 + rn, :], rn, ADAB[:, r0:r0 + rn], ADAB, st)
            nr = self.nl * 32
            self.load_vec_fm(self.inp["ln_g"].rearrange("l s (k p) -> (l s k) p", p=128), nr, self.LNG[:, 0:nr], self.LNG, st)
            self.load_vec_fm(self.inp["ln_b"].rearrange("l s (k p) -> (l s k) p", p=128), nr, self.LNB[:, 0:nr], self.LNB, st)
            c32 = self.sb([32, 128], F32, "c32", st)
            self.dma("sp", c32[0:16, :], self.inp["c"].rearrange("o (k p) -> (o k) p", p=128), [], [c32])
            self.dma("sp", c32[16:32, :], self.inp["c_ctx"].rearrange("o (k p) -> (o k) p", p=128), [], [c32])
            self.act(c32[:], c32[:], AF.Silu, [c32], [c32])
            cps = self.ps([128, 32], F32, "cps", st)
            self.tr([(cps[:], c32[:], self.identf[0:32, 0:32])], [c32, self.identf], [cps])
            condT = self.sb([128, 2, 16], F32, "condT", st)
            self.tcopy("dve", condT[:].rearrange("p a k -> p (a k)"), cps[:], [cps], [condT])
            wr = Ring([self.sb([128, DC, 512], F32, "adaw", st) for _ in range(4)])
            pr = Ring([self.ps([128, 4, 2], F32, "modp", st) for _ in range(2)])
            pq = Ring([self.ps([2, 512], F32, "modq", st) for _ in range(2)])
            sqr = Ring([self.sb([2, 512], F32, "modsq", st) for _ in range(2)])
            for i in range(self.nl):
                wv = self.inp["ada_w"][i].rearrange("(k p) m -> p k m", p=128)
                for mg in range(24):
                    wt = wr.next()
                    self.dma("sp" if mg % 2 == 0 else "act", wt[:], wv[:, :, mg * 512:(mg + 1) * 512], [], [wt])
                    q = pq.next()
                    self.mm([(q[0:2, :], condT[:, :, kc], wt[:, kc, :], kc == 0, kc == DC - 1) for kc in range(DC)],
                            [wt, condT], [q])
                    sq = sqr.next()
                    self.tcopy("dve", sq[:], q[0:2, :], [q], [sq])
                    pt = pr.next()
                    self.tr([(pt[:, j, :], sq[0:2, j * 128:(j + 1) * 128], self.identf[0:2, 0:2]) for j in range(4)],
                            [sq, self.identf], [pt])
                    self.tt("dve", self.MOD[:, i, mg * 4:(mg + 1) * 4, :], pt[:],
                            ADAB[:, i * 96 + mg * 4:i * 96 + mg * 4 + 4].unsqueeze(2).broadcast_to([128, 4, 2]),
                            ALU.add, [pt, ADAB], [self.MOD])
            for i in range(self.nl):
                for grp in (1, 4):
                    v = self.MOD[:, i, grp * 16:(grp + 1) * 16, :]
                    self.ts("dve", v, v, 1.0, None, ALU.add, None, [self.MOD], [self.MOD])
                for grp in (2, 5):
                    v = self.MOD[:, i, grp * 16:(grp + 1) * 16, :]
                    self.ts("dve", v, v, 1.0 / ALPHA, None, ALU.mult, None, [self.MOD], [self.MOD])
            self.barrier()

    def precast(self, name, idx, K, M, rows=256, defer=False):
        src = self.inp[name][idx]
        dst = self.dram([K, M], BF16, "w16_" + name, nb=K // rows)
        for s in range(K // rows):
            job = (dst.t[s * rows:(s + 1) * rows, :], src[s * rows:(s + 1) * rows, :], dst.bs[s])
            if defer:
                self.pending.append(job)
            else:
                self.dma("pool", job[0], job[1], [], [job[2]])
        dst.rows = rows
        self.w16[(name, idx)] = dst
        return dst

    def drip(self, n=1):
        while n > 0 and self.pending:
            o, i, b = self.pending.pop(0)
            self.dma("pool", o, i, [], [b])
            n -= 1

    def drip_tick(self, pace_ev=None):
        self.tick += 1
        if self.pending and self.tick % self.drip_every == 0:
            if pace_ev is not None and not self.S.mute:
                self.S._wait("pool", pace_ev)
            self.drip(1)

    def stage_transpose_in(self):
        with ExitStack() as st:
            xin = Ring([self.sb([128, D], F32, "xin", st) for _ in range(2)])
            stg = Ring([self.sb([128, DC, 512], F32, "xstg", st) for _ in range(2)])
            pr = Ring([self.ps([128, 4, 128], F32, "tip", st) for _ in range(4)])
            for bi, (t0, nb, isctx) in self.blocks:
                sg = stg.next()
                for tt_ in range(nb // 128):
                    xi = xin.next()
                    if isctx:
                        src = self.inp["ctx"][tt_ * 128:(tt_ + 1) * 128, :]
                    else:
                        r0 = t0 - CTX + tt_ * 128
                        src = self.inp["x"][r0:r0 + 128, :]
                    self.dma("sp", xi[:], src, [], [xi])
                    for kg in range(4):
                        pt = pr.next()
                        self.tr([(pt[:, j, :], xi[:, (kg * 4 + j) * 128:(kg * 4 + j + 1) * 128], self.identf[:])
                                 for j in range(4)], [xi, self.identf], [pt])
                        eng = "dve" if kg % 2 == 0 else "act"
                        o = sg[:, kg * 4:(kg + 1) * 4, tt_ * 128:(tt_ + 1) * 128]
                        if eng == "dve":
                            self.tcopy("dve", o, pt[:], [pt], [sg])
                        else:
                            self.act(o, pt[:], AF.Copy, [pt], [sg])
                self.dma("sp", self.xt_v[:, :, t0:t0 + nb], sg[:, :, 0:nb], [sg], [self.XT.bs[bi]])
            self.barrier()

    def stage_transpose_out(self):
        with ExitStack() as st:
            xin = Ring([self.sb([128, DC, 512], F32, "oin", st) for _ in range(2)])
            stg = Ring([self.sb([128, D], F32, "ostg", st) for _ in range(2)])
            pr = Ring([self.ps([128, 4, 128], F32, "top", st) for _ in range(4)])
            for bi, (t0, nb, isctx) in self.blocks:
                if isctx:
                    continue
                xi = xin.next()
                self.dma("sp", xi[:, :, 0:nb], self.xt_v[:, :, t0:t0 + nb], [self.XT.bs[bi]], [xi])
                for tt_ in range(nb // 128):
                    sg = stg.next()
                    for kg in range(4):
                        pt = pr.next()
                        self.tr([(pt[:, j, :], xi[:, kg * 4 + j, tt_ * 128:(tt_ + 1) * 128], self.identf[:])
                                 for j in range(4)], [xi, self.identf], [pt])
                        o = sg[:, kg * 512:(kg + 1) * 512]
                        if kg % 2 == 0:
                            self.tcopy("dve", o, pt[:].rearrange("p a b -> p (a b)"), [pt], [sg])
                        else:
                            self.act(o, pt[:].rearrange("p a b -> p (a b)"), AF.Copy, [pt], [sg])
                    r0 = t0 - CTX + tt_ * 128
                    self.dma("sp", self.out.t[r0:r0 + 128, :], sg[:], [sg], [self.out])
            self.barrier()

    def wview(self, W, kt, m0, mw):
        return W.t.rearrange("(kk p) m -> p kk m", p=128)[:, kt * 16:(kt + 1) * 16, m0:m0 + mw]

    def wstrips(self, W, kt):
        n = 2048 // W.rows
        return W.bs[kt * n:(kt + 1) * n]

    def linear_fm(self, W, K, M, rhs_fn, rhs_reads, nb, epi, wring, pring, mstart=0):
        KT = K // 2048
        m0 = mstart
        while m0 < M:
            mw = min(256, M - m0)
            nj = mw // 128
            banks = [pring.next() for _ in range(nj)]
            for kt in range(KT):
                wt = wring.next()
                self.dma("sp", wt[:, :, 0:mw], self.wview(W, kt, m0, mw), self.wstrips(W, kt), [wt])
                for j in range(nj):
                    specs = [(banks[j][:, 0:nb], wt[:, kc, j * 128:(j + 1) * 128], rhs_fn(kt * 16 + kc),
                              kt == 0 and kc == 0, kt == KT - 1 and kc == 15) for kc in range(16)]
                    self.mm(specs, [wt] + rhs_reads, [banks[j]])
            for j in range(nj):
                epi((m0 - mstart) // 128 + j, banks[j])
            m0 += mw
            self.drip_tick(banks[-1].b.w)

    def linear_tm(self, W, K, M, lhs_fn, lhs_reads, nb, epi, wring, pring, mstart=0, mwmax=256):
        KT = K // 2048
        ntt = nb // 128
        m0 = mstart
        while m0 < M:
            mw = min(mwmax, M - m0)
            banks = [pring.next() for _ in range(ntt)]
            for kt in range(KT):
                wt = wring.next()
                self.dma("sp", wt[:, :, 0:mw], self.wview(W, kt, m0, mw), self.wstrips(W, kt), [wt])
                for tt_ in range(ntt):
                    specs = [(banks[tt_][:, 0:mw], lhs_fn(kt * 16 + kc, tt_), wt[:, kc, 0:mw],
                              kt == 0 and kc == 0, kt == KT - 1 and kc == 15) for kc in range(16)]
                    self.mm(specs, [wt] + lhs_reads, [banks[tt_]])
            for tt_ in range(ntt):
                epi(m0 - mstart, mw, tt_, banks[tt_])
            m0 += mw

    def ln_epi(self, li, which, col, XTb, nb, S1, S2, rsq_ring):
        gbase = (2 if which == 0 else 5) * 16

        def epi(m, bank):
            xs = XTb[:, m, 0:nb]
            self.stt("dve", xs, bank[:, 0:nb], self.MOD[:, li, gbase + m, col:col + 1], xs, ALU.mult, ALU.add,
                     [bank, self.MOD, XTb], [XTb])
            rs = rsq_ring.next()
            self.act(rs[:, 0:nb], xs, AF.Square, [XTb], [rs])
            self.mm([(S1[:, 0:nb], self.onesf[:], xs, m == 0, m == DC - 1)], [self.onesf, XTb], [S1])
            self.mm([(S2[:, 0:nb], self.onesf[:], rs[:, 0:nb], m == 0, m == DC - 1)], [self.onesf, rs], [S2])
        return epi

    def ln_finish(self, li, which, col, XTb, nb, S1, S2, mean, rstd, tmp, U=None, unext=None):
        self.act(mean[:, 0:nb], S1[:, 0:nb], AF.Copy, [S1], [mean], scale=1.0 / D)
        self.act(rstd[:, 0:nb], S2[:, 0:nb], AF.Copy, [S2], [rstd], scale=1.0 / D)
        self.tt("dve", tmp[:, 0:nb], mean[:, 0:nb], mean[:, 0:nb], ALU.mult, [mean], [tmp])
        self.tt("dve", rstd[:, 0:nb], rstd[:, 0:nb], tmp[:, 0:nb], ALU.subtract, [rstd, tmp], [rstd])
        self.ts("dve", rstd[:, 0:nb], rstd[:, 0:nb], LN_EPS / (ALPHA * ALPHA), None, ALU.add, None, [rstd], [rstd])
        self.act(rstd[:, 0:nb], rstd[:, 0:nb], AF.Ln, [rstd], [rstd])
        self.act(rstd[:, 0:nb], rstd[:, 0:nb], AF.Exp, [rstd], [rstd], scale=-0.5)
        xa = XTb[:, :, 0:nb]
        self.tt("dve", xa, xa, mean[:, 0:nb].unsqueeze(1).broadcast_to([128, DC, nb]), ALU.subtract, [XTb, mean], [XTb])
        self.tt("dve", xa, xa, rstd[:, 0:nb].unsqueeze(1).broadcast_to([128, DC, nb]), ALU.mult, [XTb, rstd], [XTb])
        lc = (li * 2 + which) * DC
        for m in range(DC):
            xs = XTb[:, m, 0:nb]
            self.ts("dve", xs, xs, self.LNG[:, lc + m:lc + m + 1], self.LNB[:, lc + m:lc + m + 1], ALU.mult, ALU.add,
                    [XTb, self.LNG, self.LNB], [XTb])
            if U is not None:
                l2, shb, scb = unext
                self.act(U[:, m, 0:nb], xs, AF.Identity, [XTb, self.MOD], [U],
                         scale=self.MOD[:, l2, scb * 16 + m, col:col + 1], bias=self.MOD[:, l2, shb * 16 + m, col:col + 1])

    def stage_post(self, li, YN, KY, Wout, W1, W2, last=False):
        with ExitStack() as st:
            XTb = self.sb([128, DC, 512], F32, "XTb", st)
            U = self.sb([128, DC, 512], BF16, "U", st)
            BIG = self.sb([128, HID // 128, 512], BF16, "BIG", st)
            wring = Ring([self.sb([128, 16, 256], BF16, "wt", st) for _ in range(4)])
            pring = Ring([self.ps([128, 512], F32, "mmp", st) for _ in range(4)])
            S1 = self.ps([128, 512], F32, "S1", st)
            S2 = self.ps([128, 512], F32, "S2", st)
            rsq = Ring([self.sb([128, 512], F32, "rsq", st) for _ in range(2)])
            rel = Ring([self.sb([128, 512], F32, "rel", st) for _ in range(2)])
            mean = self.sb([128, 512], F32, "mean", st)
            rstd = self.sb([128, 512], F32, "rstd", st)
            tmp = self.sb([128, 512], F32, "lntmp", st)
            ynv = YN.t.rearrange("(kk p) t -> p kk t", p=128)
            for bi, (t0, nb, isctx) in self.blocks:
                if last and isctx:
                    continue
                col = 1 if isctx else 0
                self.dma("sp", XTb[:, :, 0:nb], self.xt_v[:, :, t0:t0 + nb], [self.XT.bs[bi]], [XTb])
                self.dma("sp", BIG[:, 0:KY // 128, 0:nb], ynv[:, :, t0:t0 + nb], [YN], [BIG])
                self.linear_fm(Wout, KY, D, lambda kc: BIG[:, kc, 0:nb], [BIG], nb,
                               self.ln_epi(li, 0, col, XTb, nb, S1, S2, rsq), wring, pring)
                self.ln_finish(li, 0, col, XTb, nb, S1, S2, mean, rstd, tmp, U, (li, 3, 4))

                def relu2(m, bank):
                    r = rel.next()
                    self.act(r[:, 0:nb], bank[:, 0:nb], AF.Relu, [bank], [r])
                    self.tt("dve", BIG[:, m, 0:nb], r[:, 0:nb], r[:, 0:nb], ALU.mult, [r], [BIG])
                self.linear_fm(W1, D, HID, lambda kc: U[:, kc, 0:nb], [U], nb, relu2, wring, pring)
                self.linear_fm(W2, HID, D, lambda kc: BIG[:, kc, 0:nb], [BIG], nb,
                               self.ln_epi(li, 1, col, XTb, nb, S1, S2, rsq), wring, pring)
                self.ln_finish(li, 1, col, XTb, nb, S1, S2, mean, rstd, tmp)
                self.dma("sp", self.xt_v[:, :, t0:t0 + nb], XTb[:, :, 0:nb], [XTb], [self.XT.bs[bi]])
            self.barrier()

    def build_U(self, li, col, XTb, U, nb, shg, scg):
        for m in range(DC):
            self.act(U[:, m, 0:nb], XTb[:, m, 0:nb], AF.Identity, [XTb, self.MOD], [U],
                     scale=self.MOD[:, li, scg * 16 + m, col:col + 1], bias=self.MOD[:, li, shg * 16 + m, col:col + 1])

    def bcast_load(self, dst_tt, dst_ap, src_row_ap):
        self.dma("pool", dst_ap, src_row_ap.partition_broadcast(128), [], [dst_tt])

    def ssd_mixer(self, li, j, Win, ctx_out):
        NCH = NTOK // 128
        YN = self.dram([D_INNER, NTOK], BF16, "YN")
        RAWT = self.dram([SSM_XBC, NTOK], F32, "RAWT")
        ZTOK = self.dram([NTOK, D_INNER], BF16, "ZTOK")
        DTRAW = self.dram([NTOK, 128], F32, "DTRAW")
        XSTOK = self.dram([NTOK, 5120], BF16, "XSTOK")
        BCT = self.dram([2048, NTOK], BF16, "BCT")
        CUMT = self.dram([128, NTOK], F32, "CUMT")
        STD = self.dram([2, NCH, 128, D_INNER], BF16, "STD")
        chunks = sorted(set(c for _, (t0, nb, _) in self.blocks for c in range(t0 // 128, (t0 + nb) // 128)))

        stages = self.cfg.get("ssd_stages", (1, 2, 3, 4, 5))
        self.S.mute = 1 not in stages
        with ExitStack() as st:
            XTb = self.sb([128, DC, 512], F32, "XTb", st)
            U = self.sb([128, DC, 512], BF16, "U", st)
            wring = Ring([self.sb([128, 16, 256], BF16, "wt", st) for _ in range(4)])
            pring = Ring([self.ps([128, 512], F32, "mmp", st) for _ in range(4)])
            zst = Ring([self.sb([128, 512], BF16, "zst", st) for _ in range(4)])
            wring2 = Ring([self.sb([128, 16, 512], BF16, "wt2", st) for _ in range(3)])
            rst = Ring([self.sb([128, 512], F32, "rst", st) for _ in range(4)])
            for bi, (t0, nb, isctx) in self.blocks:
                col = 1 if isctx else 0
                self.dma("sp", XTb[:, :, 0:nb], self.xt_v[:, :, t0:t0 + nb], [self.XT.bs[bi]], [XTb])
                self.build_U(li, col, XTb, U, nb, 0, 1)

                def zepi(m0r, mw, tt_, bank):
                    z = zst.next()
                    self.act(z[:, 0:mw], bank[:, 0:mw], AF.Silu, [bank], [z])
                    self.dma("pool", ZTOK.t[t0 + tt_ * 128:t0 + (tt_ + 1) * 128, m0r:m0r + mw], z[:, 0:mw], [z], [ZTOK])
                self.linear_tm(Win, D, D_INNER, lambda kc, tt_: U[:, kc, tt_ * 128:(tt_ + 1) * 128], [U], nb, zepi,
                               wring2, pring, mstart=0, mwmax=512)

                def xepi(m, bank):
                    r = rst.next()
                    if m % 2 == 0:
                        self.tcopy("dve", r[:, 0:nb], bank[:, 0:nb], [bank], [r])
                    else:
                        self.act(r[:, 0:nb], bank[:, 0:nb], AF.Copy, [bank], [r])
                    self.dma("pool", RAWT.t[m * 128:(m + 1) * 128, t0:t0 + nb], r[:, 0:nb], [r], [RAWT])
                self.linear_fm(Win, D, D_INNER + SSM_XBC, lambda kc: U[:, kc, 0:nb], [U], nb, xepi, wring, pring,
                               mstart=D_INNER)

                def depi(m0r, mw, tt_, bank):
                    r = rst.next()
                    self.tcopy("dve", r[:, 0:mw], bank[:, 0:mw], [bank], [r])
                    self.dma("pool", DTRAW.t[t0 + tt_ * 128:t0 + (tt_ + 1) * 128, :], r[:, 0:mw], [r], [DTRAW])
                self.linear_tm(Win, D, SSM_IN, lambda kc, tt_: U[:, kc, tt_ * 128:(tt_ + 1) * 128], [U], nb, depi,
                               wring, pring, mstart=D_INNER + SSM_XBC)
            self.barrier()
        if self.cfg.get("ssd_stop") == 1:
            return YN

        self.S.mute = 2 not in stages
        with ExitStack() as st:
            CW = self.sb([128, 240], F32, "CW", st)
            CBv = self.sb([128, 48], F32, "CBv", st)
            cwv = self.inp["ssd_conv_w"][j].rearrange("k (m p) -> (k m) p", p=128)
            self.load_vec_fm(cwv[0:128, :], 128, CW[:, 0:128], CW, st)
            self.load_vec_fm(cwv[128:240, :], 112, CW[:, 128:240], CW, st)
            self.load_vec_fm(self.inp["ssd_conv_b"][j:j + 1, :].rearrange("o (m p) -> (o m) p", p=128), 48, CBv[:], CBv, st)
            PADW = 4 + CTX + 4 + SEQ
            raws = [self.sb([128, PADW], F32, "raw", st) for _ in range(2)]
            for r in raws:
                self.memset("dve", r[:], 0.0, [r])
            rawr = Ring(raws)
            accr = Ring([self.sb([128, NTOK], F32, "acc", st) for _ in range(2)])
            xbr = Ring([self.sb([128, NTOK], BF16, "xb", st) for _ in range(2)])
            tpr = Ring([self.ps([128, 8, 128], BF16, "ctp", st) for _ in range(2)])
            tsr = Ring([self.sb([128, NCH, 128], BF16, "cts", st) for _ in range(2)])
            xsv = XSTOK.t.rearrange("(c p) f -> p c f", p=128)
            for m2 in range(0, 48, 2):
                pair = []
                for m in (m2, m2 + 1):
                    raw = rawr.next()
                    self.dma("sp", raw[:, 2:2 + CTX], RAWT.t[m * 128:(m + 1) * 128, 0:CTX], [RAWT], [raw])
                    self.dma("sp", raw[:, CTX + 6:CTX + 6 + SEQ], RAWT.t[m * 128:(m + 1) * 128, CTX:NTOK], [RAWT], [raw])
                    pair.append((m, raw, accr.next()))
                for (oi, oo, n) in ((0, 0, CTX), (CTX + 4, CTX, SEQ)):
                    for (m, raw, acc) in pair:
                        self.ts("dve", acc[:, oo:oo + n], raw[:, oi:oi + n], CW[:, m:m + 1], None, ALU.mult, None,
                                [raw, CW], [acc])
                    for k in range(1, 5):
                        for (m, raw, acc) in pair:
                            self.stt("dve", acc[:, oo:oo + n], raw[:, oi + k:oi + k + n], CW[:, k * 48 + m:k * 48 + m + 1],
                                     acc[:, oo:oo + n], ALU.mult, ALU.add, [raw, CW, acc], [acc])
                for (m, raw, acc) in pair:
                    xb = xbr.next()
                    self.act(xb[:], acc[:], AF.Silu, [acc, CBv], [xb], bias=CBv[:, m:m + 1])
                    if m >= 32:
                        self.dma("pool", BCT.t[(m - 32) * 128:(m - 31) * 128, :], xb[:], [xb], [BCT])
                    if m < 40:
                        tsb = tsr.next()
                        for c0 in range(0, NCH, 8):
                            cn = min(8, NCH - c0)
                            tp = tpr.next()
                            self.tr([(tp[:, i, :], xb[:, (c0 + i) * 128:(c0 + i + 1) * 128], self.identb[:]) for i in range(cn)],
                                    [xb, self.identb], [tp])
                            self.tcopy("dve", tsb[:, c0:c0 + cn, :], tp[:, 0:cn, :], [tp], [tsb])
                        for c0 in range(0, NCH, 9):
                            cn = min(9, NCH - c0)
                            self.dma("pool", xsv[:, c0:c0 + cn, m * 128:(m + 1) * 128], tsb[:, c0:c0 + cn, :], [tsb], [XSTOK])
            self.barrier()
        if self.cfg.get("ssd_stop") == 2:
            return YN

        self.S.mute = 3 not in stages
        with ExitStack() as sst:
            DT = self.sb([128, NCH, 128], F32, "DT", sst)
            CUM = self.sb([128, NCH, 128], F32, "CUM", sst)
            triu = self.sb([128, 128], F32, "triu", sst)
            tril = self.sb([128, 128], F32, "tril", sst)
            self.dma("sp", triu[:], self.inp["triu"], [], [triu])
            self.dma("sp", tril[:], self.inp["tril"], [], [tril])
            DSK = self.sb([128, 64], F32, "DSK", sst)
            self.bcast_load(DSK, DSK[:], self.inp["ssd_d"][j:j + 1, :])
            NGfm = self.sb([128, 32], F32, "NGfm", sst)
            with ExitStack() as st:
                self.load_vec_fm(self.inp["ssd_norm_g"][j:j + 1, :].rearrange("o (m p) -> (o m) p", p=128), 32,
                                 NGfm[:], NGfm, st)
                self.barrier()
            with ExitStack() as s2:
                Wd = self.sb([128, NCH, 128], F32, "Wd", s2)
                DEC = self.sb([128, NCH, 128], F32, "DEC", s2)
                with ExitStack() as st:
                    DTB = self.sb([128, 128], F32, "DTB", st)
                    AB = self.sb([128, 128], F32, "AB", st)
                    self.bcast_load(DTB, DTB[:], self.inp["ssd_dt_bias"][j:j + 1].rearrange("o a h -> o (a h)"))
                    self.bcast_load(AB, AB[:], self.inp["ssd_a_log"][j:j + 1].rearrange("o a h -> o (a h)"))
                    self.act(AB[:], AB[:], AF.Exp, [AB], [AB])
                    self.ts("dve", AB[:], AB[:], -1.0, None, ALU.mult, None, [AB], [AB])
                    import os
                    CUT = int(os.environ.get("S3CUT", "0"))
                    if CUT == 1:
                        self.S.mute = True
                    dtv = DTRAW.t.rearrange("(c p) h -> p c h", p=128)
                    for c0 in range(0, NCH, 8):
                        cn = min(8, NCH - c0)
                        self.dma("sp", DT[:, c0:c0 + cn, :], dtv[:, c0:c0 + cn, :], [DTRAW], [DT])
                    self.tt("dve", DT[:], DT[:], DTB[:].unsqueeze(1).broadcast_to([128, NCH, 128]), ALU.add, [DT, DTB], [DT])
                    self.act(DT[:], DT[:], AF.Exp, [DT], [DT])
                    self.act(DT[:], DT[:], AF.Ln, [DT], [DT], bias=1.0)
                    if CUT == 2:
                        self.S.mute = True
                    dtA = self.sb([128, NCH, 128], F32, "dtA", st)
                    self.tt("dve", dtA[:], DT[:], AB[:].unsqueeze(1).broadcast_to([128, NCH, 128]), ALU.mult, [DT, AB], [dtA])
                    CTs = self.sb([128, NTOK], F32, "CTs", st)
                    pcr = Ring([self.ps([128, 128], F32, "pc", st) for _ in range(2)])
                    ptr_ = Ring([self.ps([128, 128], F32, "pt", st) for _ in range(2)])
                    pxr = Ring([self.ps([128, 128], F32, "px", st) for _ in range(2)])
                    for c in range(NCH):
                        pc = pcr.next()
                        self.mm([(pc[:, 0:64], triu[:], dtA[:, c, 0:64], True, True)], [triu, dtA], [pc])
                        self.mm([(pc[:, 64:128], tril[:], dtA[:, c, 64:128], True, True)], [tril, dtA], [pc])
                        self.tcopy("dve", CUM[:, c, :], pc[:], [pc], [CUM])
                        if CUT == 4:
                            continue
                        pt = ptr_.next()
                        self.mm([(pt[:], self.onesf[:], dtA[:, c, :], True, True)], [self.onesf, dtA], [pt])
                        if CUT != 7:
                            self.tcopy("dve", DEC[:, c, :], pt[:], [pt], [DEC])
                        if CUT != 6:
                            self.tt("dve", Wd[:, c, :], pt[:], CUM[:, c, :], ALU.subtract, [pt, CUM], [Wd])
                        if CUT in (6, 7, 8):
                            continue
                        if CUT == 5:
                            continue
                        px = pxr.next()
                        self.tr([(px[:], CUM[:, c, :], self.identf[:])], [CUM, self.identf], [px])
                        self.act(CTs[:, c * 128:(c + 1) * 128], px[:], AF.Copy, [px], [CTs])
                    if CUT == 3:
                        self.S.mute = True
                    self.act(DEC[:], DEC[:], AF.Exp, [DEC], [DEC])
                    self.act(Wd[:], Wd[:], AF.Exp, [Wd], [Wd])
                    self.tt("dve", Wd[:], Wd[:], DT[:], ALU.mult, [Wd, DT], [Wd])
                    self.dma("sp", CUMT.t, CTs[:], [CTs], [CUMT])
                    self.barrier()
                if self.cfg.get("ssd_stop") == 3:
                    return YN

                self.S.mute = 4 not in stages
                with ExitStack() as st:
                    Sst = [self.sb([128, D_INNER], F32, "Sst", st) for _ in range(2)]
                    for s_ in Sst:
                        self.memset("dve", s_[:], 0.0, [s_])
                    xr = Ring([self.sb([128, 5120], BF16, "Xs", st) for _ in range(3)])
                    xwr = Ring([self.sb([128, D_INNER], BF16, "xw", st) for _ in range(2)])
                    sbr = Ring([self.sb([128, D_INNER], BF16, "sbf", st) for _ in range(2)])
                    pnr = Ring([self.ps([128, 512], F32, "pn", st) for _ in range(4)])
                    order = [list(range(NCH)), [1, 0] + list(range(NCH - 1, 1, -1))]
                    for step in range(NCH):
                        for dr in (0, 1):
                            c = order[dr][step]
                            X = xr.next()
                            self.dma("sp", X[:], XSTOK.t[c * 128:(c + 1) * 128, :], [XSTOK], [X])
                            xw = xwr.next()
                            self.tt("dve", xw[:].rearrange("p (h q) -> p h q", q=64),
                                    X[:, 0:D_INNER].rearrange("p (h q) -> p h q", q=64),
                                    Wd[:, c, dr * 64:(dr + 1) * 64].unsqueeze(2).broadcast_to([128, 64, 64]), ALU.mult,
                                    [X, Wd], [xw])
                            sbf = sbr.next()
                            self.act(sbf[:], Sst[dr][:], AF.Copy, [Sst[dr]], [sbf])
                            self.dma("pool", STD.t[dr, c], sbf[:], [sbf], [STD])
                            for g in range(8):
                                pn = pnr.next()
                                self.mm([(pn[:], X[:, D_INNER + g * 128:D_INNER + (g + 1) * 128],
                                          xw[:, g * 512:(g + 1) * 512], True, True)], [X, xw], [pn])
                                sv = Sst[dr][:, g * 512:(g + 1) * 512].rearrange("p (h q) -> p h q", q=64)
                                self.tt("dve", sv, sv,
                                        DEC[:, c, dr * 64 + g * 8:dr * 64 + g * 8 + 8].unsqueeze(2).broadcast_to([128, 8, 64]),
                                        ALU.mult, [Sst[dr], DEC], [Sst[dr]])
                                self.tt("dve", Sst[dr][:, g * 512:(g + 1) * 512], Sst[dr][:, g * 512:(g + 1) * 512], pn[:],
                                        ALU.add, [Sst[dr], pn], [Sst[dr]])
                    self.barrier()
                if self.cfg.get("ssd_stop") == 4:
                    return YN

            self.S.mute = 5 not in stages
            with ExitStack() as st:
                xr = Ring([self.sb([128, 5120], BF16, "Xs", st) for _ in range(2)])
                bcr = Ring([self.sb([128, 16, 128], BF16, "bct", st) for _ in range(2)])
                XDT = [self.sb([128, D_INNER], BF16, "xdt", st) for _ in range(2)]
                XD = self.sb([128, D_INNER], BF16, "xd", st)
                Zt = self.sb([128, D_INNER], BF16, "zt", st)
                YS = self.sb([128, D_INNER], F32, "ys", st)
                yn = self.sb([128, D_INNER], BF16, "yn", st)
                junk = self.sb([128, 512], F32, "junk", st)
                YNT = self.sb([128, 32, 128], BF16, "ynt", st)
                STt = [self.sb([128, D_INNER], BF16, "stt", st) for _ in range(2)]
                EC = self.sb([128, 128], F32, "ec", st)
                ss = self.sb([128, 8], F32, "ss", st)
                cbr = Ring([self.sb([128, 8, 128], F32, "cbt", st) for _ in range(4)])
                argr = Ring([self.sb([128, 8, 128], F32, "arg", st) for _ in range(4)])
                mtr = Ring([self.sb([128, 8, 128], BF16, "mt", st) for _ in range(3)])
                cbmr = Ring([self.sb([128, 128], F32, "cbm", st) for _ in range(4)])
                y2r = Ring([self.sb([128, 512], BF16, "y2s", st) for _ in range(3)])
                pcb = Ring([self.ps([128, 128], F32, "pcb", st) for _ in range(2)])
                pyr = Ring([self.ps([128, 512], F32, "py", st) for _ in range(2)])
                pcs = Ring([self.ps([128, 512], F32, "pcs", st) for _ in range(2)])
                ptp = Ring([self.ps([128, 8, 128], BF16, "ptp", st) for _ in range(2)])
                masks = [triu, tril]
                bcv = BCT.t.rearrange("(k p) t -> p k t", p=128)
                ynv = YN.t.rearrange("(k p) t -> p k t", p=128)
                for c in chunks:
                    if c < 2 and not ctx_out:
                        continue
                    X = xr.next()
                    self.dma("sp", X[:], XSTOK.t[c * 128:(c + 1) * 128, :], [XSTOK], [X])
                    bct = bcr.next()
                    self.dma("sp", bct[:], bcv[:, :, c * 128:(c + 1) * 128], [BCT], [bct])
                    self.dma("sp", Zt[:], ZTOK.t[c * 128:(c + 1) * 128, :], [ZTOK], [Zt])
                    for dr in (0, 1):
                        self.dma("sp", STt[dr][:], STD.t[dr, c], [STD], [STt[dr]])
                        self.tt("dve", XDT[dr][:].rearrange("p (h q) -> p h q", q=64),
                                X[:, 0:D_INNER].rearrange("p (h q) -> p h q", q=64),
                                DT[:, c, dr * 64:(dr + 1) * 64].unsqueeze(2).broadcast_to([128, 64, 64]), ALU.mult,
                                [X, DT], [XDT[dr]])
                    self.tt("dve", XD[:].rearrange("p (h q) -> p h q", q=64),
                            X[:, 0:D_INNER].rearrange("p (h q) -> p h q", q=64),
                            DSK[:].unsqueeze(2).broadcast_to([128, 64, 64]), ALU.mult, [X, DSK], [XD])
                    self.act(EC[:], CUM[:, c, :], AF.Exp, [CUM], [EC])
                    def part_a(g):
                        pc = pcb.next()
                        self.mm([(pc[:], bct[:, g, :], bct[:, 8 + g, :], True, True)], [bct], [pc])
                        res = []
                        for dr in (0, 1):
                            cbm = cbmr.next()
                            self.tt("dve", cbm[:], pc[:], masks[dr][:], ALU.mult, [pc, masks[dr]], [cbm])
                            cbt = cbr.next()
                            r0 = dr * 64 + g * 8
                            self.dma("pool", cbt[:], CUMT.t[r0:r0 + 8, c * 128:(c + 1) * 128].partition_broadcast(128), [CUMT], [cbt])
                            arg = argr.next()
                            self.tt("dve", arg[:], cbt[:], CUM[:, c, r0:r0 + 8].unsqueeze(2).broadcast_to([128, 8, 128]),
                                    ALU.subtract, [cbt, CUM], [arg])
                            res.append((cbm, arg, r0))
                        for (cbm, arg, r0) in res:
                            self.act(arg[:], arg[:], AF.Exp, [arg], [arg])
                        return g, res

                    def part_b(g, res):
                        py = pyr.next()
                        self.mm([(py[:], self.identb[:], XD[:, g * 512:(g + 1) * 512], True, False)], [self.identb, XD], [py])
                        for dr in (0, 1):
                            cbm, arg, r0 = res[dr]
                            mt = mtr.next()
                            self.stt("dve", mt[:], arg[:], 1.0, cbm[:].unsqueeze(1).broadcast_to([128, 8, 128]),
                                     ALU.min, ALU.mult, [arg, cbm], [mt])
                            self.mm([(py[:, h * 64:(h + 1) * 64], mt[:, h, :],
                                      XDT[dr][:, (g * 8 + h) * 64:(g * 8 + h + 1) * 64], False, False) for h in range(8)],
                                    [mt, XDT[dr]], [py])
                            pq = pcs.next()
                            self.mm([(pq[:], bct[:, 8 + g, :], STt[dr][:, g * 512:(g + 1) * 512], True, True)],
                                    [bct, STt[dr]], [pq])
                            y2 = y2r.next()
                            self.tt("dve", y2[:].rearrange("p (h q) -> p h q", q=64),
                                    pq[:].rearrange("p (h q) -> p h q", q=64),
                                    EC[:, r0:r0 + 8].unsqueeze(2).broadcast_to([128, 8, 64]), ALU.mult, [pq, EC], [y2])
                            self.mm([(py[:], self.identb[:], y2[:], False, dr == 1)], [self.identb, y2], [py])
                        self.tt("dve", YS[:, g * 512:(g + 1) * 512], py[:], Zt[:, g * 512:(g + 1) * 512], ALU.mult,
                                [py, Zt], [YS])
                        self.act(junk[:], YS[:, g * 512:(g + 1) * 512], AF.Square, [YS], [junk, ss], accum_out=ss[:, g:g + 1])

                    pend = None
                    for g in range(8):
                        cur = part_a(g)
                        if pend is not None:
                            part_b(*pend)
                        pend = cur
                    part_b(*pend)
                    self.ts("dve", ss[:], ss[:], 1.0 / 512, RMS_EPS, ALU.mult, ALU.add, [ss], [ss])
                    self.act(ss[:], ss[:], AF.Ln, [ss], [ss])
                    self.act(ss[:], ss[:], AF.Exp, [ss], [ss], scale=-0.5)
                    for g in range(8):
                        self.act(yn[:, g * 512:(g + 1) * 512], YS[:, g * 512:(g + 1) * 512], AF.Copy, [YS, ss], [yn],
                                 scale=ss[:, g:g + 1])
                    for k0 in range(0, 32, 8):
                        tp = ptp.next()
                        self.tr([(tp[:, i, :], yn[:, (k0 + i) * 128:(k0 + i + 1) * 128], self.identb[:]) for i in range(8)],
                                [yn, self.identb], [tp])
                        self.tt("dve", YNT[:, k0:k0 + 8, :], tp[:], NGfm[:, k0:k0 + 8].unsqueeze(2).broadcast_to([128, 8, 128]),
                                ALU.mult, [tp, NGfm], [YNT])
                    self.dma("sp", ynv[:, :, c * 128:(c + 1) * 128], YNT[:], [YNT], [YN])
                self.barrier()
        self.S.mute = False
        return YN

    def attn_mixer(self, li, Wqkv, ctx_out):
        NCH = NTOK // 128
        YN = self.dram([D, NTOK], BF16, "AYN")
        QT = self.dram([D, NTOK], BF16, "QT")
        KT2 = self.dram([4, 128, NTOK + 128], BF16, "KT2")
        VT = self.dram([NTOK + 128, 256], BF16, "VT")
        SCALE = 0.125
        with ExitStack() as st:
            XTb = self.sb([128, DC, 512], F32, "XTb", st)
            U = self.sb([128, DC, 512], BF16, "U", st)
            wring = Ring([self.sb([128, 16, 256], BF16, "wt", st) for _ in range(4)])
            pring = Ring([self.ps([128, 512], F32, "mmp", st) for _ in range(4)])
            psw = Ring([self.ps([128, 512], F32, "psw", st) for _ in range(2)])
            rotT = self.sb([128, 128], F32, "rotT", st)
            self.dma("sp", rotT[:], self.inp["rotT"], [], [rotT])
            COS = self.sb([128, 512], F32, "COS", st)
            SIN = self.sb([128, 512], F32, "SIN", st)
            x32r = Ring([self.sb([128, 512], F32, "x32", st) for _ in range(3)])
            t1r = Ring([self.sb([128, 512], F32, "t1", st) for _ in range(2)])
            t2r = Ring([self.sb([128, 512], F32, "t2", st) for _ in range(2)])
            qor = Ring([self.sb([128, 512], BF16, "qo", st) for _ in range(3)])
            vor = Ring([self.sb([128, 256], BF16, "vo", st) for _ in range(3)])
            zer = self.sb([128, 256], BF16, "zer", st)
            self.memset("dve", zer[:], 0.0, [zer])
            for kv in range(4):
                self.dma("sp", KT2.t[kv, :, NTOK:NTOK + 128], zer[:, 0:128], [zer], [KT2])
            self.dma("sp", VT.t[NTOK:NTOK + 128, :], zer[:], [zer], [VT])
            for bi, (t0, nb, isctx) in self.blocks:
                col = 1 if isctx else 0
                self.dma("sp", XTb[:, :, 0:nb], self.xt_v[:, :, t0:t0 + nb], [self.XT.bs[bi]], [XTb])
                self.build_U(li, col, XTb, U, nb, 0, 1)
                if not isctx:
                    self.dma("sp", COS[:, 0:nb], self.inp["ropec"][:, t0 - CTX:t0 - CTX + nb], [], [COS])
                    self.dma("sp", SIN[:, 0:nb], self.inp["ropes"][:, t0 - CTX:t0 - CTX + nb], [], [SIN])

                def qkepi(m, bank):
                    qo = qor.next()
                    if isctx:
                        self.act(qo[:, 0:nb], bank[:, 0:nb], AF.Copy, [bank], [qo])
                    else:
                        x32 = x32r.next()
                        self.act(x32[:, 0:nb], bank[:, 0:nb], AF.Copy, [bank], [x32])
                        pw = psw.next()
                        self.mm([(pw[:, 0:nb], rotT[:], x32[:, 0:nb], True, True)], [rotT, x32], [pw])
                        t1 = t1r.next()
                        t2 = t2r.next()
                        self.tt("dve", t1[:, 0:nb], x32[:, 0:nb], COS[:, 0:nb], ALU.mult, [x32, COS], [t1])
                        self.tt("dve", t2[:, 0:nb], pw[:, 0:nb], SIN[:, 0:nb], ALU.mult, [pw, SIN], [t2])
                        self.tt("dve", qo[:, 0:nb], t1[:, 0:nb], t2[:, 0:nb], ALU.add, [t1, t2], [qo])
                    if m < 16:
                        self.dma("pool", QT.t[m * 128:(m + 1) * 128, t0:t0 + nb], qo[:, 0:nb], [qo], [QT])
                    else:
                        for hh in range(2):
                            kv = (m - 16) * 2 + hh
                            for dup in range(2):
                                self.dma("pool", KT2.t[kv, dup * 64:(dup + 1) * 64, t0:t0 + nb],
                                         qo[hh * 64:(hh + 1) * 64, 0:nb], [qo], [KT2])
                self.linear_fm(Wqkv, D, 2304, lambda kc: U[:, kc, 0:nb], [U], nb, qkepi, wring, pring, mstart=0)

                def vepi(m0r, mw, tt_, bank):
                    vo = vor.next()
                    self.act(vo[:, 0:mw], bank[:, 0:mw], AF.Copy, [bank], [vo])
                    self.dma("pool", VT.t[t0 + tt_ * 128:t0 + (tt_ + 1) * 128, :], vo[:, 0:mw], [vo], [VT])
                self.linear_tm(Wqkv, D, 2560, lambda kc, tt_: U[:, kc, tt_ * 128:(tt_ + 1) * 128], [U], nb, vepi,
                               wring, pring, mstart=2304)
            self.barrier()

        with ExitStack() as st:
            SINK = self.sb([128, 32], F32, "SINK", st)
            self.bcast_load(SINK, SINK[:], self.inp["attn_sink"][0:1, :])
            SINKS = self.sb([128, 32], F32, "SINKS", st)
            self.ts("dve", SINKS[:], SINK[:], 1.0 / SCALE, None, ALU.mult, None, [SINK], [SINKS])
            AM = self.sb([128, 3, 640], F32, "AM", st)
            self.dma("sp", AM[:], self.inp["amask"].rearrange("v p s -> p v s"), [], [AM])
            KC = self.sb([128, 4, 256], BF16, "KC", st)
            self.dma("sp", KC[:], KT2.t[:, :, 0:256].rearrange("k p t -> p k t"), [KT2], [KC])
            VC = self.sb([128, 2, 256], BF16, "VC", st)
            self.dma("sp", VC[:], VT.t[0:256, :].rearrange("(c p) f -> p c f", p=128), [VT], [VC])
            qr = Ring([self.sb([128, 16, 128], BF16, "Qb", st) for _ in range(2)])
            klr = Ring([self.sb([128, 4, 384], BF16, "Kl", st) for _ in range(2)])
            vlr = Ring([self.sb([128, 3, 256], BF16, "Vl", st) for _ in range(2)])
            ssr = Ring([self.sb([128, 640], F32, "Ssb", st) for _ in range(3)])
            ppr = Ring([self.sb([128, 640], BF16, "Pb", st) for _ in range(3)])
            ptr_ = Ring([self.sb([128, 5, 128], BF16, "PT", st) for _ in range(3)])
            smr = Ring([self.sb([128, 8], F32, "sm", st) for _ in range(4)])
            Ot = self.sb([128, D], BF16, "Ot", st)
            OT = self.sb([128, 16, 128], BF16, "OTt", st)
            psA = Ring([self.ps([128, 512], F32, "psA", st) for _ in range(2)])
            psB = Ring([self.ps([128, 128], F32, "psB", st) for _ in range(2)])
            psT = Ring([self.ps([128, 5, 128], BF16, "psT", st) for _ in range(2)])
            psO = Ring([self.ps([128, 64], F32, "psO", st) for _ in range(2)])
            qtv = QT.t.rearrange("(k p) t -> p k t", p=128)
            ynv = YN.t.rearrange("(k p) t -> p k t", p=128)
            vtv = VT.t.rearrange("(c p) f -> p c f", p=128)
            chunks = sorted(set(c for _, (t0, nb, _) in self.blocks for c in range(t0 // 128, (t0 + nb) // 128)))
            for c in chunks:
                isctx = c < 2
                if isctx and not ctx_out:
                    continue
                Qb = qr.next()
                self.dma("sp", Qb[:], qtv[:, :, c * 128:(c + 1) * 128], [QT], [Qb])
                if not isctx:
                    Kl = klr.next()
                    self.dma("sp", Kl[:], KT2.t[:, :, (c - 1) * 128:(c + 2) * 128].rearrange("k p t -> p k t"), [KT2], [Kl])
                    Vl = vlr.next()
                    self.dma("sp", Vl[:], vtv[:, c - 1:c + 2, :], [VT], [Vl])
                    W = 640
                    mv = 0 if c == 2 else (2 if c == NCH - 1 else 1)
                else:
                    W = 256
                nkc = W // 128
                def part_a(h):
                    kv = h // 8
                    pb = (h % 2) * 64
                    qap = Qb[pb:pb + 64, h // 2, :]
                    pa = psA.next()
                    self.mm([(pa[:, 0:256], qap, KC[pb:pb + 64, kv, :], True, True)], [Qb, KC], [pa])
                    Ssb = ssr.next()
                    if not isctx:
                        self.mm([(pa[:, 256:512], qap, Kl[pb:pb + 64, kv, 0:256], True, True)], [Qb, Kl], [pa])
                        pbk = psB.next()
                        self.mm([(pbk[:], qap, Kl[pb:pb + 64, kv, 256:384], True, True)], [Qb, Kl], [pbk])
                        self.tt("dve", Ssb[:, 0:512], pa[:], AM[:, mv, 0:512], ALU.add, [pa, AM], [Ssb])
                        self.tt("dve", Ssb[:, 512:640], pbk[:], AM[:, mv, 512:640], ALU.add, [pbk, AM], [Ssb])
                    else:
                        self.tcopy("dve", Ssb[:, 0:256], pa[:, 0:256], [pa], [Ssb])
                    sm = smr.next()
                    self.S.op("dve", lambda E, o=sm[:, 0:1], i=Ssb[:, 0:W]: E.reduce_max(out=o, in_=i, axis=AX.X),
                              bl([Ssb]), bl([sm]))
                    self.tt("dve", sm[:, 1:2], sm[:, 0:1], SINKS[:, h:h + 1], ALU.max, [sm, SINKS], [sm])
                    self.ts("dve", sm[:, 1:2], sm[:, 1:2], -SCALE, None, ALU.mult, None, [sm], [sm])
                    Pb = ppr.next()
                    self.act(Pb[:, 0:W], Ssb[:, 0:W], AF.Exp, [Ssb, sm], [Pb, sm], scale=SCALE, bias=sm[:, 1:2],
                             accum_out=sm[:, 2:3])
                    self.act(sm[:, 3:4], SINK[:, h:h + 1], AF.Exp, [SINK, sm], [sm], bias=sm[:, 1:2])
                    self.tt("dve", sm[:, 4:5], sm[:, 2:3], sm[:, 3:4], ALU.add, [sm], [sm])
                    self.S.op("dve", lambda E, o=sm[:, 4:5], i=sm[:, 4:5]: E.reciprocal(out=o, in_=i), bl([sm]), bl([sm]))
                    return (h, kv, Pb, sm)

                def part_b(h, kv, Pb, sm):
                    pt = psT.next()
                    self.tr([(pt[:, k, :], Pb[:, k * 128:(k + 1) * 128], self.identb[:]) for k in range(nkc)],
                            [Pb, self.identb], [pt])
                    PT = ptr_.next()
                    self.act(PT[:, 0:nkc, :], pt[:, 0:nkc, :], AF.Copy, [pt], [PT])
                    po = psO.next()
                    specs = []
                    for k in range(nkc):
                        vsrc = VC[:, k, kv * 64:(kv + 1) * 64] if k < 2 else Vl[:, k - 2, kv * 64:(kv + 1) * 64]
                        specs.append((po[:], PT[:, k, :], vsrc, k == 0, k == nkc - 1))
                    self.mm(specs, [PT, VC] + ([Vl] if not isctx else []), [po])
                    self.act(Ot[:, h * 64:(h + 1) * 64], po[:], AF.Copy, [po, sm], [Ot], scale=sm[:, 4:5])

                pend = None
                for h in range(32):
                    cur = part_a(h)
                    if pend is not None:
                        part_b(*pend)
                    pend = cur
                part_b(*pend)
                for k0 in range(0, 16, 5):
                    kn = min(5, 16 - k0)
                    pt = psT.next()
                    self.tr([(pt[:, i, :], Ot[:, (k0 + i) * 128:(k0 + i + 1) * 128], self.identb[:]) for i in range(kn)],
                            [Ot, self.identb], [pt])
                    self.tcopy("dve", OT[:, k0:k0 + kn, :], pt[:, 0:kn, :], [pt], [OT])
                self.dma("sp", ynv[:, :, c * 128:(c + 1) * 128], OT[:], [OT], [YN])
            self.barrier()
        return YN

    def precast_layer(self, li, kind, j, defer=False):
        w = {}
        if kind == 0:
            w["in"] = self.precast("ssd_w_in", j, D, SSM_IN, defer=defer)
            w["out"] = self.precast("ssd_w_out", j, D_INNER, D, defer=defer)
        elif kind == 1:
            w["in"] = self.precast("attn_w_qkv", j, D, 2560, defer=defer)
            w["out"] = self.precast("attn_w_o", j, D, D, defer=defer)
        else:
            w["in"] = self.precast("hy_w_in", j, D, 3 * D, defer=defer)
            w["out"] = self.precast("hy_w_out", j, D, D, defer=defer)
        w["w1"] = self.precast("mlp_w1", li, D, HID, defer=defer)
        w["w2"] = self.precast("mlp_w2", li, HID, D, defer=defer)
        return w

    def run_all(self):
        layers = self.cfg.get("layers", [(0, 0), (1, 0), (2, 0), (0, 1)])
        full = "layers" not in self.cfg
        w = self.precast_layer(0, *layers[0])
        self.stage_mod()
        self.stage_transpose_in()
        for li, (kind, j) in enumerate(layers):
            last = full and li == len(layers) - 1
            if kind == 0:
                YN = self.ssd_mixer(li, j, w["in"], not last)
                KY = D_INNER
            elif kind == 1:
                YN = self.attn_mixer(li, w["in"], not last)
                KY = D
            else:
                YN = self.hyena_mixer(li, w["in"], not last)
                KY = D
            wn = self.precast_layer(li + 1, *layers[li + 1], defer=True) if li + 1 < len(layers) else None
            self.stage_post(li, YN, KY, w["out"], w["w1"], w["w2"], last)
            self.drip(10 ** 6)
            w = wn
        self.stage_transpose_out()


def build_program(cfg=None):
    nc = bass.Bass("TRN2", target_bir_lowering=False)
    with ExitStack() as st:
        P = Prog(nc, st, cfg)
        P.run_all()
        if cfg and cfg.get("dump_xt"):
            dbg = nc.dram_tensor("dbg_xt", [D, NTOK], F32, kind="ExternalOutput").ap()
            P.dma("sp", dbg, P.XT.t, [P.XT], [])
        P.S.emit()
    return nc, P


TWO_PI = 2.0 * math.pi
MAGIC = 12582912.0


def _hy_geoms():
    return {"lat": dict(L=SEQ, nt=SEQ // 128, nf=SEQ // 128 + 1, r0=CTX),
            "ctx": dict(L=CTX, nt=CTX // 128, nf=CTX // 128 + 1, r0=0)}


def hyena_mixer(self, li, Win, ctx_out):
    NCH = NTOK // 128
    G = _hy_geoms()
    geoms = ["lat"] + (["ctx"] if ctx_out else [])
    YN = self.dram([D, NTOK], BF16, "HYN")
    RAWT = self.dram([3 * D, NTOK], F32, "HRAWT")
    X3 = self.dram([NTOK, 3 * D], F32, "HX3")
    VB = self.dram([NTOK, D], BF16, "HVB")
    Z32 = self.dram([NTOK, D], F32, "HZ32")
    ZB = self.dram([NTOK, D], BF16, "HZB")
    FA = {g: self.dram([2, G[g]["L"], D], BF16, "HFA" + g) for g in geoms}
    FB = {g: self.dram([2, G[g]["L"], D], BF16, "HFB" + g) for g in geoms}
    KRE = {g: self.dram([2, G[g]["nf"] * 128, D], F32, "HKRE" + g) for g in geoms}
    KIM = {g: self.dram([2, G[g]["nf"] * 128, D], F32, "HKIM" + g) for g in geoms}
    YRE = {g: self.dram([G[g]["nf"] * 128, D], BF16, "HYRE" + g) for g in geoms}
    YIM = {g: self.dram([G[g]["nf"] * 128, D], BF16, "HYIM" + g) for g in geoms}

    hst = self.cfg.get('hy_stages', (1, 2, 3, 4, 5))
    self.S.mute = 1 not in hst
    with ExitStack() as st:
        XTb = self.sb([128, DC, 512], F32, "XTb", st)
        U = self.sb([128, DC, 512], BF16, "U", st)
        wring = Ring([self.sb([128, 16, 256], BF16, "wt", st) for _ in range(4)])
        pring = Ring([self.ps([128, 512], F32, "mmp", st) for _ in range(4)])
        rst = Ring([self.sb([128, 512], F32, "rst", st) for _ in range(4)])
        for bi, (t0, nb, isctx) in self.blocks:
            if isctx and not ctx_out:
                continue
            col = 1 if isctx else 0
            self.dma("sp", XTb[:, :, 0:nb], self.xt_v[:, :, t0:t0 + nb], [self.XT.bs[bi]], [XTb])
            self.build_U(li, col, XTb, U, nb, 0, 1)

            def xepi(m, bank):
                r = rst.next()
                if m % 2 == 0:
                    self.tcopy("dve", r[:, 0:nb], bank[:, 0:nb], [bank], [r])
                else:
                    self.act(r[:, 0:nb], bank[:, 0:nb], AF.Copy, [bank], [r])
                self.dma("pool", RAWT.t[m * 128:(m + 1) * 128, t0:t0 + nb], r[:, 0:nb], [r], [RAWT])
            self.linear_fm(Win, D, 3 * D, lambda kc: U[:, kc, 0:nb], [U], nb, xepi, wring, pring)
        self.barrier()

    self.S.mute = 2 not in hst
    with ExitStack() as st:
        CW = self.sb([128, 144], F32, "HCW", st)
        CBv = self.sb([128, 48], F32, "HCB", st)
        cwv = self.inp["hy_conv_w"][0].rearrange("k (m p) -> (k m) p", p=128)
        self.load_vec_fm(cwv[0:128, :], 128, CW[:, 0:128], CW, st)
        self.load_vec_fm(cwv[128:144, :], 16, CW[:, 128:144], CW, st)
        self.load_vec_fm(self.inp["hy_conv_b"][0:1, :].rearrange("o (m p) -> (o m) p", p=128), 48, CBv[:], CBv, st)
        PADW = 2 + CTX + 2 + SEQ
        raws = [self.sb([128, PADW], F32, "hraw", st) for _ in range(2)]
        for r in raws:
            self.memset("dve", r[:], 0.0, [r])
        rawr = Ring(raws)
        accr = Ring([self.sb([128, NTOK], F32, "hacc", st) for _ in range(2)])
        tpr = Ring([self.ps([128, 4, 128], F32, "htp", st) for _ in range(4)])
        tsr = Ring([self.sb([128, NCH, 128], F32, "hts", st) for _ in range(2)])
        tbr = Ring([self.sb([128, NCH, 128], BF16, "htb", st) for _ in range(2)])
        x3v = X3.t.rearrange("(c p) f -> p c f", p=128)
        vbv = VB.t.rearrange("(c p) f -> p c f", p=128)
        c_lo = 0 if ctx_out else 2
        for m in range(48):
            raw = rawr.next()
            if ctx_out:
                self.dma("sp", raw[:, 1:1 + CTX], RAWT.t[m * 128:(m + 1) * 128, 0:CTX], [RAWT], [raw])
            self.dma("sp", raw[:, CTX + 3:CTX + 3 + SEQ], RAWT.t[m * 128:(m + 1) * 128, CTX:NTOK], [RAWT], [raw])
            acc = accr.next()
            segs = ([(0, 0, CTX)] if ctx_out else []) + [(CTX + 2, CTX, SEQ)]
            for (oi, oo, n) in segs:
                self.ts("dve", acc[:, oo:oo + n], raw[:, oi:oi + n], CW[:, m:m + 1], CBv[:, m:m + 1], ALU.mult, ALU.add,
                        [raw, CW, CBv], [acc])
                for k in range(1, 3):
                    self.stt("dve", acc[:, oo:oo + n], raw[:, oi + k:oi + k + n], CW[:, k * 48 + m:k * 48 + m + 1],
                             acc[:, oo:oo + n], ALU.mult, ALU.add, [raw, CW, acc], [acc])
            tsb = tsr.next()
            tbb = tbr.next() if m >= 32 else None
            for c0 in range(c_lo, NCH, 4):
                cn = min(4, NCH - c0)
                tp = tpr.next()
                self.tr([(tp[:, i, :], acc[:, (c0 + i) * 128:(c0 + i + 1) * 128], self.identf[:]) for i in range(cn)],
                        [acc, self.identf], [tp])
                self.tcopy("dve", tsb[:, c0:c0 + cn, :], tp[:, 0:cn, :], [tp], [tsb])
                if tbb is not None:
                    self.tcopy("dve", tbb[:, c0:c0 + cn, :], tp[:, 0:cn, :], [tp], [tbb])
            for c0 in range(c_lo, NCH, 8):
                cn = min(8, NCH - c0)
                self.dma("pool", x3v[:, c0:c0 + cn, m * 128:(m + 1) * 128], tsb[:, c0:c0 + cn, :], [tsb], [X3])
                if tbb is not None:
                    self.dma("pool", vbv[:, c0:c0 + cn, (m - 32) * 128:(m - 31) * 128], tbb[:, c0:c0 + cn, :], [tbb], [VB])
        self.barrier()

    self.S.mute = 3 not in hst
    with ExitStack() as st:
        ABSD = self.sb([128, D], F32, "ABSD", st)
        self.bcast_load(ABSD, ABSD[:], self.inp["habsd"])
        W1 = self.sb([33, 64], F32, "HW1", st)
        W2 = self.sb([64, 64], F32, "HW2", st)
        W3 = self.sb([64, 64], F32, "HW3", st)
        W4 = self.sb([64, 4 * D], F32, "HW4", st)
        self.dma("sp", W1[:], self.inp["hy_f_w1"][0], [], [W1])
        self.dma("sp", W2[:], self.inp["hy_f_w2"][0], [], [W2])
        self.dma("sp", W3[:], self.inp["hy_f_w3"][0], [], [W3])
        self.dma("sp", W4[:], self.inp["hy_f_w4"][0], [], [W4])
        FRB = self.sb([64, 4], F32, "FRB", st)
        tmp4 = self.sb([4, 64], F32, "tmp4", st)
        self.dma("sp", tmp4[0:1, :], self.inp["hy_f_freq"][0:1, :], [], [tmp4])
        self.dma("sp", tmp4[1:2, :], self.inp["hy_f_b1"][0:1, :], [], [tmp4])
        self.dma("sp", tmp4[2:3, :], self.inp["hy_f_b2"][0:1, :], [], [tmp4])
        self.dma("sp", tmp4[3:4, :], self.inp["hy_f_b3"][0:1, :], [], [tmp4])
        p4 = self.ps([64, 4], F32, "p4", st)
        self.tr([(p4[:], tmp4[:], self.identf[0:4, 0:4])], [tmp4, self.identf], [p4])
        self.tcopy("dve", FRB[:], p4[:], [p4], [FRB])
        self.tt("dve", FRB[:, 1:4], FRB[:, 1:4], FRB[:, 0:1].broadcast_to([64, 3]), ALU.mult, [FRB], [FRB])
        pm = Ring([self.ps([128, 512], F32, "hpm", st) for _ in range(4)])
        ar = Ring([self.sb([64, 512], F32, "har", st) for _ in range(2)])
        a2 = Ring([self.sb([64, 512], F32, "ha2", st) for _ in range(2)])
        WIN = self.sb([128, D], F32, "WIN", st)
        hw = Ring([self.sb([128, 512], F32, "hw", st) for _ in range(4)])
        abr = Ring([self.sb([128, 512], BF16, "hab", st) for _ in range(4)])
        for g in geoms:
            L, nt = G[g]["L"], G[g]["nt"]
            zT = self.sb([33, L], F32, "zT" + g, st)
            self.dma("sp", zT[:], self.inp["hz_" + g], [], [zT])
            NT = self.sb([128, nt], F32, "NT" + g, st)
            self.dma("sp", NT[:], self.inp["hnt_" + g], [], [NT])
            Hs = [self.sb([64, L], F32, "H%d%s" % (i, g), st) for i in range(2)]
            src, srcK, Wl = zT, 33, [W1, W2, W3]
            for layer in range(3):
                dst = Hs[layer % 2]
                nbk = min(512, L)
                for cb in range(L // nbk):
                    p = pm.next()
                    self.mm([(p[0:64, 0:nbk], Wl[layer][0:srcK, :], src[0:srcK, cb * nbk:(cb + 1) * nbk], True, True)],
                            [Wl[layer], src], [p])
                    a = ar.next()
                    self.ts("dve", a[:, 0:nbk], p[0:64, 0:nbk], FRB[:, 0:1], FRB[:, layer + 1:layer + 2], ALU.mult, ALU.add,
                            [p, FRB], [a])
                    b = a2.next()
                    self.ts("dve", b[:, 0:nbk], a[:, 0:nbk], 1.0 / TWO_PI, MAGIC, ALU.mult, ALU.add, [a], [b])
                    self.ts("dve", b[:, 0:nbk], b[:, 0:nbk], -MAGIC, -TWO_PI, ALU.add, ALU.mult, [b], [b])
                    self.tt("dve", a[:, 0:nbk], a[:, 0:nbk], b[:, 0:nbk], ALU.add, [a, b], [a])
                    self.ts("dve", a[:, 0:nbk], a[:, 0:nbk], math.pi, -math.pi, ALU.min, ALU.max, [a], [a])
                    self.act(dst[:, cb * nbk:(cb + 1) * nbk], a[:, 0:nbk], AF.Sin, [a], [dst])
                src, srcK = dst, 64
            H3 = src
            for lc in range(nt):
                self.act(WIN[:], ABSD[:], AF.Exp, [ABSD, NT], [WIN], scale=NT[:, lc:lc + 1])
                for o in range(2):
                    for db in range(4):
                        hws = []
                        for dr in range(2):
                            cb = o * 8 + dr * 4 + db
                            p = pm.next()
                            self.mm([(p[:], H3[:, lc * 128:(lc + 1) * 128], W4[:, cb * 512:(cb + 1) * 512], True, True)],
                                    [H3, W4], [p])
                            h = hw.next()
                            self.tt("dve", h[:], p[:], WIN[:, db * 512:(db + 1) * 512], ALU.mult, [p, WIN], [h])
                            hws.append(h)
                        if lc == 0:
                            self.memset("dve", hws[1][0:1, :], 0.0, [hws[1]])
                        A = abr.next()
                        B = abr.next()
                        self.tt("dve", A[:], hws[0][:], hws[1][:], ALU.add, hws, [A])
                        self.tt("dve", B[:], hws[0][:], hws[1][:], ALU.subtract, hws, [B])
                        self.dma("pool", FA[g].t[o, lc * 128:(lc + 1) * 128, db * 512:(db + 1) * 512], A[:], [A], [FA[g]])
                        self.dma("pool", FB[g].t[o, lc * 128:(lc + 1) * 128, db * 512:(db + 1) * 512], B[:], [B], [FB[g]])
        self.barrier()

    def dft_fwd(g, src_ap, src_tt, use_cos, use_sin, epi, st):
        L, nt, nf = G[g]["L"], G[g]["nt"], G[g]["nf"]
        X = self.sb([128, nt, 1024], BF16, "dfX", st)
        cr = Ring([self.sb([128, nt, 128], BF16, "dfC", st) for _ in range(2)])
        sr = Ring([self.sb([128, nt, 128], BF16, "dfS", st) for _ in range(2)])
        pre = Ring([self.ps([128, 512], F32, "dfpr", st) for _ in range(4)])
        pim = Ring([self.ps([128, 512], F32, "dfpi", st) for _ in range(4)])
        sv = src_ap.rearrange("(c p) d -> p c d", p=128)
        for dh in range(2):
            for c0 in range(0, nt, 8):
                cn = min(8, nt - c0)
                self.dma("sp", X[:, c0:c0 + cn, :], sv[:, c0:c0 + cn, dh * 1024:(dh + 1) * 1024], [src_tt], [X])
            for fc in range(nf):
                fs = 128 if fc < nf - 1 else 1
                Ct = St = None
                if use_cos:
                    Ct = cr.next()
                    self.dma("sp", Ct[:], self.inp["hC_" + g][fc], [], [Ct])
                if use_sin:
                    St = sr.next()
                    self.dma("sp", St[:], self.inp["hS_" + g][fc], [], [St])
                for d2 in range(2):
                    dq = dh * 2 + d2
                    xs_ = slice(d2 * 512, (d2 + 1) * 512)
                    pr_ = pi_ = None
                    if use_cos:
                        pr_ = pre.next()
                        self.mm([(pr_[0:fs, :], Ct[:, ac, 0:fs], X[:, ac, xs_], ac == 0, ac == nt - 1) for ac in range(nt)],
                                [Ct, X], [pr_])
                    if use_sin:
                        pi_ = pim.next()
                        self.mm([(pi_[0:fs, :], St[:, ac, 0:fs], X[:, ac, xs_], ac == 0, ac == nt - 1) for ac in range(nt)],
                                [St, X], [pi_])
                    epi(fc, fs, dq, pr_, pi_)

    def dft_inv(g, epi, st):
        L, nt, nf = G[g]["L"], G[g]["nt"], G[g]["nf"]
        YR = self.sb([128, nf, 1024], BF16, "diR", st)
        YI = self.sb([128, nf, 1024], BF16, "diI", st)
        cr = Ring([self.sb([128, nf, 128], BF16, "diC", st) for _ in range(2)])
        sr = Ring([self.sb([128, nf, 128], BF16, "diS", st) for _ in range(2)])
        py = Ring([self.ps([128, 512], F32, "dipy", st) for _ in range(4)])
        yrv = YRE[g].t.rearrange("(c p) d -> p c d", p=128)
        yiv = YIM[g].t.rearrange("(c p) d -> p c d", p=128)
        for dh in range(2):
            for c0 in range(0, nf, 8):
                cn = min(8, nf - c0)
                self.dma("sp", YR[:, c0:c0 + cn, :], yrv[:, c0:c0 + cn, dh * 1024:(dh + 1) * 1024], [YRE[g]], [YR])
                self.dma("sp", YI[:, c0:c0 + cn, :], yiv[:, c0:c0 + cn, dh * 1024:(dh + 1) * 1024], [YIM[g]], [YI])
            for ic in range(nt):
                Ct = cr.next()
                St = sr.next()
                self.dma("sp", Ct[:], self.inp["hCw_" + g][ic], [], [Ct])
                self.dma("sp", St[:], self.inp["hSw_" + g][ic], [], [St])
                for d2 in range(2):
                    dq = dh * 2 + d2
                    xs_ = slice(d2 * 512, (d2 + 1) * 512)
                    p = py.next()
                    specs = []
                    for fc in range(nf):
                        fs = 128 if fc < nf - 1 else 1
                        specs.append((p[:], Ct[0:fs, fc, :], YR[0:fs, fc, xs_], fc == 0, False))
                        specs.append((p[:], St[0:fs, fc, :], YI[0:fs, fc, xs_], False, fc == nf - 1))
                    self.mm(specs, [Ct, St, YR, YI], [p])
                    epi(ic, dq, p)

    self.S.mute = 4 not in hst
    for g in geoms:
        for o in range(2):
            for (use_cos, src, dstK) in ((True, FA[g], KRE[g]), (False, FB[g], KIM[g])):
                with ExitStack() as st:
                    kst = Ring([self.sb([128, 512], F32, "kst", st) for _ in range(3)])

                    def kepi(fc, fs, dq, pr_, pi_, dstK=dstK, o=o, kst=kst):
                        p = pr_ if pr_ is not None else pi_
                        k = kst.next()
                        self.act(k[0:fs, :], p[0:fs, :], AF.Copy, [p], [k])
                        self.dma("pool", dstK.t[o, fc * 128:fc * 128 + fs, dq * 512:(dq + 1) * 512], k[0:fs, :], [k], [dstK])
                    dft_fwd(g, src.t[o], src, use_cos, not use_cos, kepi, st)
                    self.barrier()

    self.S.mute = 5 not in hst
    with ExitStack() as bst:
        HB = self.sb([128, 2, D], F32, "HB", bst)
        self.bcast_load(HB, HB[:].rearrange("p o d -> p (o d)"), self.inp["hy_f_bias"][0:1].rearrange("a o d -> a (o d)"))
        self.barrier()
        for o in range(2):
            for g in geoms:
                L, nt, nf, r0 = G[g]["L"], G[g]["nt"], G[g]["nf"], G[g]["r0"]
                src = VB if o == 0 else ZB
                with ExitStack() as st:
                    kr = Ring([self.sb([128, 512], F32, "kr", st) for _ in range(2)])
                    ki = Ring([self.sb([128, 512], F32, "ki", st) for _ in range(2)])
                    t1 = Ring([self.sb([128, 512], F32, "st1", st) for _ in range(2)])
                    t2 = Ring([self.sb([128, 512], F32, "st2", st) for _ in range(2)])
                    yo = Ring([self.sb([128, 512], BF16, "syo", st) for _ in range(4)])

                    def sepi(fc, fs, dq, pr_, pi_, g=g, o=o, kr=kr, ki=ki, t1=t1, t2=t2, yo=yo):
                        a = kr.next()
                        b = ki.next()
                        rs = slice(fc * 128, fc * 128 + fs)
                        cs = slice(dq * 512, (dq + 1) * 512)
                        self.dma("act", a[0:fs, :], KRE[g].t[o, rs, cs], [KRE[g]], [a])
                        self.dma("act", b[0:fs, :], KIM[g].t[o, rs, cs], [KIM[g]], [b])
                        u1 = t1.next()
                        u2 = t2.next()
                        yr = yo.next()
                        yi = yo.next()
                        self.tt("dve", u1[0:fs, :], pr_[0:fs, :], a[0:fs, :], ALU.mult, [pr_, a], [u1])
                        self.tt("dve", u2[0:fs, :], pi_[0:fs, :], b[0:fs, :], ALU.mult, [pi_, b], [u2])
                        self.tt("dve", yr[0:fs, :], u1[0:fs, :], u2[0:fs, :], ALU.subtract, [u1, u2], [yr])
                        self.tt("dve", u1[0:fs, :], pr_[0:fs, :], b[0:fs, :], ALU.mult, [pr_, b], [u1])
                        self.tt("dve", u2[0:fs, :], pi_[0:fs, :], a[0:fs, :], ALU.mult, [pi_, a], [u2])
                        self.tt("dve", yi[0:fs, :], u1[0:fs, :], u2[0:fs, :], ALU.add, [u1, u2], [yi])
                        self.dma("pool", YRE[g].t[rs, cs], yr[0:fs, :], [yr], [YRE[g]])
                        self.dma("pool", YIM[g].t[rs, cs], yi[0:fs, :], [yi], [YIM[g]])
                    dft_fwd(g, src.t[r0:r0 + L, :], src, True, True, sepi, st)
                    self.barrier()
                with ExitStack() as st:
                    ur = Ring([self.sb([128, 512], F32, "ur", st) for _ in range(2)])
                    xr = Ring([self.sb([128, 512], F32, "xg", st) for _ in range(2)])
                    zr = Ring([self.sb([128, 512], F32, "zo", st) for _ in range(2)])
                    zb = Ring([self.sb([128, 512], BF16, "zob", st) for _ in range(2)])
                    ptp = Ring([self.ps([128, 4, 128], BF16, "iptp", st) for _ in range(2)])
                    yt = Ring([self.sb([128, 4, 128], BF16, "iyt", st) for _ in range(2)])
                    ynv = YN.t.rearrange("(k p) t -> p k t", p=128)

                    def iepi(ic, dq, p, g=g, o=o, r0=r0, ur=ur, xr=xr, zr=zr, zb=zb, ptp=ptp, yt=yt, ynv=ynv):
                        rows = slice(r0 + ic * 128, r0 + (ic + 1) * 128)
                        cs = slice(dq * 512, (dq + 1) * 512)
                        u = ur.next()
                        xg = xr.next()
                        if o == 0:
                            self.dma("act", u[:], X3.t[rows, 2 * D + dq * 512:2 * D + (dq + 1) * 512], [X3], [u])
                            self.dma("act", xg[:], X3.t[rows, dq * 512:(dq + 1) * 512], [X3], [xg])
                        else:
                            self.dma("act", u[:], Z32.t[rows, cs], [Z32], [u])
                            self.dma("act", xg[:], X3.t[rows, D + dq * 512:D + (dq + 1) * 512], [X3], [xg])
                        z = zr.next()
                        self.tt("dve", z[:], u[:], HB[:, o, cs], ALU.mult, [u, HB], [z])
                        self.tt("dve", z[:], z[:], p[:], ALU.add, [z, p], [z])
                        if o == 0:
                            self.tt("dve", z[:], z[:], xg[:], ALU.mult, [z, xg], [z])
                            zbb = zb.next()
                            self.act(zbb[:], z[:], AF.Copy, [z], [zbb])
                            self.dma("pool", Z32.t[rows, cs], z[:], [z], [Z32])
                            self.dma("pool", ZB.t[rows, cs], zbb[:], [zbb], [ZB])
                        else:
                            zbb = zb.next()
                            self.tt("dve", zbb[:], z[:], xg[:], ALU.mult, [z, xg], [zbb])
                            tp = ptp.next()
                            self.tr([(tp[:, i, :], zbb[:, i * 128:(i + 1) * 128], self.identb[:]) for i in range(4)],
                                    [zbb, self.identb], [tp])
                            y = yt.next()
                            self.act(y[:], tp[:], AF.Copy, [tp], [y])
                            c = (r0 // 128) + ic
                            self.dma("pool", ynv[:, dq * 4:(dq + 1) * 4, c * 128:(c + 1) * 128], y[:], [y], [YN])
                    dft_inv(g, iepi, st)
                    self.barrier()
    self.S.mute = False
    return YN


Prog.hyena_mixer = hyena_mixer


_PROG_CACHE = {}


def kernel(**inputs):
    n_cores = 8
    if "nc" not in _PROG_CACHE:
        _PROG_CACHE["nc"] = build_program(None)[0]
    nc = _PROG_CACHE["nc"]
    consts = host_consts_cached()
    f32 = lambda a: np.ascontiguousarray(np.asarray(a, dtype=np.float32))
    x = f32(inputs["x"])
    c = f32(inputs["c"])
    ctx = f32(inputs["ctx"])
    shared = {"c_ctx": f32(inputs["c_ctx"]).reshape(1, D)}
    for n in WEIGHT_SHAPES:
        shared[n] = f32(inputs[n])
    shared.update(consts)
    in_maps = []
    for b in range(n_cores):
        m = dict(shared)
        m["x"] = np.ascontiguousarray(x[b])
        m["c"] = np.ascontiguousarray(c[b:b + 1])
        m["ctx"] = np.ascontiguousarray(ctx[b])
        in_maps.append(m)
    res = run_bass_kernel_spmd(nc, in_maps, core_ids=list(range(n_cores)))
    return np.stack([np.asarray(r["out"], dtype=np.float32) for r in res.results], axis=0)
```

```python
import math
import os
from contextlib import ExitStack

import numpy as np
import ml_dtypes

import concourse.bass as bass
import concourse.mybir as mybir
from concourse.bass_utils import run_bass_kernel_spmd

F32 = mybir.dt.float32
BF16 = mybir.dt.bfloat16
AF = mybir.ActivationFunctionType
ALU = mybir.AluOpType
AX = mybir.AxisListType

D = 2048
SEQ = 4096
CTX = 256
NTOK = SEQ + CTX
DEPTH = 4
ALPHA = (2.0 * DEPTH) ** 0.25
LN_EPS = 1e-5
RMS_EPS = 1e-5
HID = 4 * D
DC = D // 128
MIXER = (0, 1, 2, 0)
BLOCKS = [(0, CTX, True)] + [(CTX + i * 512, 512, False) for i in range(SEQ // 512)]

D_INNER = 2 * D
SSM_HEADS = 64
SSM_G = 8
SSM_N = 128
SSM_XBC = D_INNER + 2 * SSM_G * SSM_N
SSM_IN = D_INNER + SSM_XBC + 2 * SSM_HEADS

ENGS = ("pe", "act", "dve", "pool", "sp")


class Buf:
    __slots__ = ("name", "w", "r")

    def __init__(self, name=""):
        self.name = name
        self.w = None
        self.r = []


class TT:
    def __init__(self, t, name="", nb=1):
        self.t = t
        self.b = Buf(name)
        self.bs = [Buf(name + str(i)) for i in range(nb)] if nb > 1 else [self.b]

    def __getitem__(self, k):
        return self.t[k]


class Sch:
    RINGS = {"sp": 16, "pool": 8, "act": 4}

    def __init__(self, nc):
        self.nc = nc
        self.q = {e: [] for e in ENGS}
        self.cnt = {e: 0 for e in ENGS}
        self.seen = {e: {} for e in ENGS}
        self.esem = {}
        self.dsem = []
        self.dval = []
        self.ring = {}
        self.rpos = {}
        self.mute = False

    def setup(self, stack):
        nc = self.nc
        for e in ENGS:
            self.esem[e] = stack.enter_context(nc.semaphore("s_" + e))
        for e, n in self.RINGS.items():
            self.ring[e] = []
            self.rpos[e] = 0
            for i in range(n):
                self.ring[e].append(len(self.dsem))
                self.dsem.append(stack.enter_context(nc.semaphore("d_%s%d" % (e, i))))
                self.dval.append(0)

    def _wait(self, eng, ev):
        if ev is None:
            return
        kind, a, v = ev
        if kind == "e" and a == eng and eng == "pe":
            return
        key = (kind, a)
        if self.seen[eng].get(key, 0) >= v:
            return
        self.seen[eng][key] = v
        sem = self.esem[a] if kind == "e" else self.dsem[a]
        self.q[eng].append(lambda E, sem=sem, v=v: E.wait_ge(sem, v))

    def _deps(self, eng, reads, writes):
        for b in reads:
            self._wait(eng, b.w)
        for b in writes:
            self._wait(eng, b.w)
            for ev in b.r:
                self._wait(eng, ev)

    def _commit(self, ev, reads, writes):
        for b in reads:
            b.r.append(ev)
            if len(b.r) > 48:
                last = {}
                for e in b.r:
                    k = (e[0], e[1])
                    if k not in last or last[k][2] < e[2]:
                        last[k] = e
                b.r = list(last.values())
        for b in writes:
            b.w = ev
            b.r = []

    def op(self, eng, fn, reads=(), writes=()):
        if self.mute:
            return
        self._deps(eng, reads, writes)
        self.cnt[eng] += 1
        idx = self.cnt[eng]
        sem = self.esem[eng]
        self.q[eng].append(lambda E, fn=fn, sem=sem: fn(E).then_inc(sem, 1))
        self._commit(("e", eng, idx), reads, writes)

    def group(self, eng, fns, reads=(), writes=()):
        if self.mute:
            return
        self._deps(eng, reads, writes)
        self.cnt[eng] += 1
        idx = self.cnt[eng]
        sem = self.esem[eng]
        for f in fns[:-1]:
            self.q[eng].append(lambda E, f=f: f(E))
        self.q[eng].append(lambda E, f=fns[-1], sem=sem: f(E).then_inc(sem, 1))
        self._commit(("e", eng, idx), reads, writes)

    def dma(self, eng, out, in_, reads=(), writes=(), **kw):
        if self.mute:
            return
        self._deps(eng, reads, writes)
        ring = self.ring[eng]
        si = ring[self.rpos[eng] % len(ring)]
        self.rpos[eng] += 1
        if self.dval[si] > 0:
            self._wait(eng, ("d", si, self.dval[si]))
        self.dval[si] += 16
        v = self.dval[si]
        sem = self.dsem[si]
        self.q[eng].append(
            lambda E, out=out, in_=in_, sem=sem, kw=kw: E.dma_start(out=out, in_=in_, **kw).then_inc(sem, 16))
        self._commit(("d", si, v), reads, writes)

    def barrier(self):
        for e in ENGS:
            for e2 in ENGS:
                if e2 != e and self.cnt[e2] > 0:
                    self._wait(e, ("e", e2, self.cnt[e2]))
            for si in range(len(self.dsem)):
                if self.dval[si] > 0:
                    self._wait(e, ("d", si, self.dval[si]))

    def emit(self):
        nc = self.nc
        self.barrier()
        q = self.q
        with nc.Block() as block:
            @block.tensor
            def _(E):
                for f in q["pe"]:
                    f(E)

            @block.scalar
            def _(E):
                for f in q["act"]:
                    f(E)

            @block.vector
            def _(E):
                for f in q["dve"]:
                    f(E)

            @block.gpsimd
            def _(E):
                for f in q["pool"]:
                    f(E)

            @block.sync
            def _(E):
                for f in q["sp"]:
                    f(E)


def bl(ts):
    out = []
    for t in ts:
        if t is None:
            continue
        if isinstance(t, Buf):
            out.append(t)
        elif isinstance(t, TT):
            out.extend(t.bs)
        else:
            out.extend(bl(t))
    return out


class Ring:
    def __init__(self, items):
        self.items = items
        self.i = 0

    def next(self):
        t = self.items[self.i % len(self.items)]
        self.i += 1
        return t


class KB:
    def __init__(self, nc, stack):
        self.nc = nc
        self.st = stack
        self.S = Sch(nc)
        self.S.setup(stack)
        self.uid = 0

    def nm(self, p):
        self.uid += 1
        return "%s_%d" % (p, self.uid)

    def sb(self, shape, dt, name="sb", stack=None, nb=1):
        n = self.nm(name)
        t = (stack or self.st).enter_context(self.nc.sbuf_tensor(n, list(shape), dt))
        return TT(t, n, nb)

    def ps(self, shape, dt=F32, name="ps", stack=None):
        n = self.nm(name)
        t = (stack or self.st).enter_context(self.nc.psum_tensor(n, list(shape), dt))
        return TT(t, n)

    def dram(self, shape, dt, name="dr", nb=1):
        n = self.nm(name)
        t = self.nc.dram_tensor(n, list(shape), dt).ap()
        return TT(t, n, nb)

    def act(self, out, in_, func, reads, writes, **kw):
        self.S.op("act", lambda E: E.activation(out=out, in_=in_, func=func, **kw), bl(reads), bl(writes))

    def tcopy(self, eng, out, in_, reads, writes):
        self.S.op(eng, lambda E: E.tensor_copy(out=out, in_=in_), bl(reads), bl(writes))

    def tt(self, eng, out, in0, in1, op, reads, writes):
        self.S.op(eng, lambda E: E.tensor_tensor(out=out, in0=in0, in1=in1, op=op), bl(reads), bl(writes))

    def ts(self, eng, out, in0, s1, s2, op0, op1, reads, writes):
        if s2 is None:
            self.S.op(eng, lambda E: E.tensor_scalar(out=out, in0=in0, scalar1=s1, scalar2=None, op0=op0),
                      bl(reads), bl(writes))
        else:
            self.S.op(eng, lambda E: E.tensor_scalar(out=out, in0=in0, scalar1=s1, scalar2=s2, op0=op0, op1=op1),
                      bl(reads), bl(writes))

    def stt(self, eng, out, in0, scalar, in1, op0, op1, reads, writes):
        self.S.op(eng, lambda E: E.scalar_tensor_tensor(out=out, in0=in0, scalar=scalar, in1=in1, op0=op0, op1=op1),
                  bl(reads), bl(writes))

    def memset(self, eng, ap, val, writes):
        self.S.op(eng, lambda E: E.memset(ap, val), [], bl(writes))

    def dma(self, eng, out, in_, reads, writes, **kw):
        self.S.dma(eng, out, in_, bl(reads), bl(writes), **kw)

    def mm(self, specs, reads, writes):
        fns = []
        for (o, l, r, s0, s1) in specs:
            fns.append(lambda E, o=o, l=l, r=r, s0=s0, s1=s1: E.matmul(o, lhsT=l, rhs=r, start=s0, stop=s1))
        self.S.group("pe", fns, bl(reads), bl(writes))

    def tr(self, specs, reads, writes):
        fns = []
        for (o, i, idn) in specs:
            fns.append(lambda E, o=o, i=i, idn=idn: E.transpose(out=o, in_=i, identity=idn))
        self.S.group("pe", fns, bl(reads), bl(writes))

    def barrier(self):
        self.S.barrier()


def host_consts():
    c = {}
    c["identf"] = np.eye(128, dtype=np.float32)
    c["onesf"] = np.ones((128, 128), dtype=np.float32)
    j = np.arange(128)
    c["triu"] = (j[:, None] <= j[None, :]).astype(np.float32)
    c["tril"] = (j[:, None] >= j[None, :]).astype(np.float32)
    rot = np.zeros((128, 128), np.float32)
    for i in range(64):
        rot[2 * i + 1, 2 * i] = -1.0
        rot[2 * i, 2 * i + 1] = 1.0
    c["rotT"] = rot
    t = np.arange(SEQ)
    inv = (10000.0 ** (-np.arange(16, dtype=np.float32) / 16)).astype(np.float32)
    ang = np.concatenate([(t // 64).astype(np.float32)[:, None] * inv, (t % 64).astype(np.float32)[:, None] * inv], -1)
    dd = (np.arange(128) % 64) // 2
    c["ropec"] = np.ascontiguousarray(np.cos(ang)[:, dd].T.astype(np.float32))
    c["ropes"] = np.ascontiguousarray(np.sin(ang)[:, dd].T.astype(np.float32))
    NEG = -30000.0
    am = np.zeros((3, 128, 640), np.float32)
    left = (c["triu"] - 1.0) * -NEG
    right = (c["tril"] - 1.0) * -NEG
    for v in range(3):
        am[v, :, 256:384] = left if v != 0 else NEG
        am[v, :, 512:640] = right if v != 2 else NEG
    c["amask"] = am
    bf = ml_dtypes.bfloat16
    max_decay = math.log(1e-2) / 0.3
    min_decay = math.log(1e-2) / 1.5
    deltas = np.linspace(min_decay, max_decay, D, dtype=np.float32)
    c["habsd"] = np.abs(deltas)[None, :].astype(np.float32)
    for gname, L in (("lat", SEQ), ("ctx", CTX)):
        nt = L // 128
        nf = nt + 1
        N = 2 * L
        t = np.linspace(0.0, 1.0, L, dtype=np.float32)[:, None]
        w = (np.float32(2.0 * math.pi) * np.arange(L, dtype=np.float32)[:, None] / np.float32(L)).astype(np.float32)
        f = np.linspace(1e-4, 15, 16, dtype=np.float32)[None, :]
        z = np.concatenate([t, np.cos(f * w), -np.sin(f * w)], axis=-1).astype(np.float32)
        c["hz_" + gname] = np.ascontiguousarray(z.T)
        c["hnt_" + gname] = np.ascontiguousarray((-t[:, 0]).reshape(nt, 128).T.astype(np.float32))
        a = np.arange(L, dtype=np.int64)
        fr = np.arange(nf * 128, dtype=np.int64)
        ang = (2.0 * np.pi / N) * ((a[:, None] * fr[None, :]) % N).astype(np.float64)
        valid = (fr <= L)[None, :]
        Cm = np.where(valid, np.cos(ang), 0.0)
        Sm = np.where(valid, np.sin(ang), 0.0)
        c["hC_" + gname] = np.ascontiguousarray(Cm.reshape(nt, 128, nf, 128).transpose(2, 1, 0, 3)).astype(bf)
        c["hS_" + gname] = np.ascontiguousarray(Sm.reshape(nt, 128, nf, 128).transpose(2, 1, 0, 3)).astype(bf)
        wf = np.where((fr == 0) | (fr == L), 1.0, 2.0) / N
        wf = np.where(fr <= L, wf, 0.0)
        Ci = (Cm * wf[None, :]).T
        Si = (Sm * wf[None, :]).T
        c["hCw_" + gname] = np.ascontiguousarray(Ci.reshape(nf, 128, nt, 128).transpose(2, 1, 0, 3)).astype(bf)
        c["hSw_" + gname] = np.ascontiguousarray(Si.reshape(nf, 128, nt, 128).transpose(2, 1, 0, 3)).astype(bf)
    return c


_HC_CACHE = {}


def host_consts_cached():
    if "c" not in _HC_CACHE:
        _HC_CACHE["c"] = host_consts()
    return _HC_CACHE["c"]


WEIGHT_SHAPES = {
    "ada_w": [DEPTH, D, 6 * D], "ada_b": [DEPTH, 6 * D], "ln_g": [DEPTH, 2, D], "ln_b": [DEPTH, 2, D],
    "mlp_w1": [DEPTH, D, HID], "mlp_w2": [DEPTH, HID, D],
    "ssd_w_in": [2, D, SSM_IN], "ssd_conv_w": [2, 5, SSM_XBC], "ssd_conv_b": [2, SSM_XBC],
    "ssd_dt_bias": [2, 2, 64], "ssd_a_log": [2, 2, 64], "ssd_d": [2, 64], "ssd_norm_g": [2, D_INNER],
    "ssd_w_out": [2, D_INNER, D],
    "attn_w_qkv": [1, D, 2560], "attn_sink": [1, 32], "attn_w_o": [1, D, D],
    "hy_w_in": [1, D, 3 * D], "hy_conv_w": [1, 3, 3 * D], "hy_conv_b": [1, 3 * D],
    "hy_f_w1": [1, 33, 64], "hy_f_b1": [1, 64], "hy_f_w2": [1, 64, 64], "hy_f_b2": [1, 64],
    "hy_f_w3": [1, 64, 64], "hy_f_b3": [1, 64], "hy_f_w4": [1, 64, 4 * D], "hy_f_freq": [1, 64],
    "hy_f_bias": [1, 2, D], "hy_w_out": [1, D, D],
}


class Prog(KB):
    def __init__(self, nc, stack, cfg=None):
        super().__init__(nc, stack)
        self.cfg = cfg or {}
        nc_ = nc
        self.inp = {}
        ein = lambda n, s: nc_.dram_tensor(n, list(s), F32, kind="ExternalInput").ap()
        self.inp["x"] = ein("x", [SEQ, D])
        self.inp["c"] = ein("c", [1, D])
        self.inp["ctx"] = ein("ctx", [CTX, D])
        self.inp["c_ctx"] = ein("c_ctx", [1, D])
        self.nl = self.cfg.get("nl", DEPTH)
        for n, s in WEIGHT_SHAPES.items():
            s = list(s)
            if n in ("ada_w", "ada_b", "ln_g", "ln_b", "mlp_w1", "mlp_w2"):
                s[0] = self.nl
            if n in self.cfg.get("skip_inputs", ()):
                continue
            self.inp[n] = ein(n, s)
        self.hc = host_consts_cached()
        self.blocks = [(i, BLOCKS[i]) for i in self.cfg.get('blocks', range(len(BLOCKS)))]
        for n, a in self.hc.items():
            if n.startswith("h") and self.cfg.get("no_hyena"):
                continue
            dt_ = BF16 if a.dtype == ml_dtypes.bfloat16 else F32
            self.inp[n] = nc_.dram_tensor(n, list(a.shape), dt_, kind="ExternalInput").ap()
        self.out = TT(nc_.dram_tensor("out", [SEQ, D], F32, kind="ExternalOutput").ap(), "out")
        self.XT = self.dram([D, NTOK], F32, "XT", nb=len(BLOCKS))
        self.xt_v = self.XT.t.rearrange("(k p) t -> p k t", p=128)
        self.w16 = {}
        self.pending = []
        self.tick = 0
        self.drip_every = 4
        self.setup_consts()

    def setup_consts(self):
        self.identf = self.sb([128, 128], F32, "identf")
        self.onesf = self.sb([128, 128], F32, "onesf")
        self.identb = self.sb([128, 128], BF16, "identb")
        self.dma("sp", self.identf[:], self.inp["identf"], [], [self.identf])
        self.dma("sp", self.onesf[:], self.inp["onesf"], [], [self.onesf])
        self.tcopy("dve", self.identb[:], self.identf[:], [self.identf], [self.identb])
        self.MOD = self.sb([128, DEPTH, 96, 2], F32, "MOD")
        self.LNG = self.sb([128, DEPTH * 2 * DC], F32, "LNG")
        self.LNB = self.sb([128, DEPTH * 2 * DC], F32, "LNB")

    def load_vec_fm(self, rows_ap, n, out_ap, out_tt, st):
        if not hasattr(st, "_lv"):
            st._lv = (self.sb([128, 128], F32, "lv", st), self.ps([128, 128], F32, "lvp", st))
        tmp, pst = st._lv
        self.dma("sp", tmp[0:n, :], rows_ap, [], [tmp])
        self.tr([(pst[:, 0:n], tmp[0:n, :], self.identf[0:n, 0:n])], [tmp, self.identf], [pst])
        self.tcopy("dve", out_ap, pst[:, 0:n], [pst], [out_tt])

    def stage_mod(self):
        with ExitStack() as st:
            ADAB = self.sb([128, DEPTH * 96], F32, "ADAB", st)
            ab = self.inp["ada_b"].rearrange("l (m p) -> (l m) p", p=128)
            nrow = self.nl * 96
            for r0 in range(0, nrow, 128):
                rn = min(128, nrow - r0)
                self.load_vec_fm(ab[r0:r0 + rn, :], rn, ADAB[:, r0:r0 + rn], ADAB, st)
            nr = self.nl * 32
            self.load_vec_fm(self.inp["ln_g"].rearrange("l s (k p) -> (l s k) p", p=128), nr, self.LNG[:, 0:nr], self.LNG, st)
            self.load_vec_fm(self.inp["ln_b"].rearrange("l s (k p) -> (l s k) p", p=128), nr, self.LNB[:, 0:nr], self.LNB, st)
            c32 = self.sb([32, 128], F32, "c32", st)
            self.dma("sp", c32[0:16, :], self.inp["c"].rearrange("o (k p) -> (o k) p", p=128), [], [c32])
            self.dma("sp", c32[16:32, :], self.inp["c_ctx"].rearrange("o (k p) -> (o k) p", p=128), [], [c32])
            self.act(c32[:], c32[:], AF.Silu, [c32], [c32])
            cps = self.ps([128, 32], F32, "cps", st)
            self.tr([(cps[:], c32[:], self.identf[0:32, 0:32])], [c32, self.identf], [cps])
            condT = self.sb([128, 2, 16], F32, "condT", st)
            self.tcopy("dve", condT[:].rearrange("p a k -> p (a k)"), cps[:], [cps], [condT])
            wr = Ring([self.sb([128, DC, 512], F32, "adaw", st) for _ in range(4)])
            pr = Ring([self.ps([128, 4, 2], F32, "modp", st) for _ in range(2)])
            pq = Ring([self.ps([2, 512], F32, "modq", st) for _ in range(2)])
            sqr = Ring([self.sb([2, 512], F32, "modsq", st) for _ in range(2)])
            for i in range(self.nl):
                wv = self.inp["ada_w"][i].rearrange("(k p) m -> p k m", p=128)
                for mg in range(24):
                    wt = wr.next()
                    self.dma("sp" if mg % 2 == 0 else "act", wt[:], wv[:, :, mg * 512:(mg + 1) * 512], [], [wt])
                    q = pq.next()
                    self.mm([(q[0:2, :], condT[:, :, kc], wt[:, kc, :], kc == 0, kc == DC - 1) for kc in range(DC)],
                            [wt, condT], [q])
                    sq = sqr.next()
                    self.tcopy("dve", sq[:], q[0:2, :], [q], [sq])
                    pt = pr.next()
                    self.tr([(pt[:, j, :], sq[0:2, j * 128:(j + 1) * 128], self.identf[0:2, 0:2]) for j in range(4)],
                            [sq, self.identf], [pt])
                    self.tt("dve", self.MOD[:, i, mg * 4:(mg + 1) * 4, :], pt[:],
                            ADAB[:, i * 96 + mg * 4:i * 96 + mg * 4 + 4].unsqueeze(2).broadcast_to([128, 4, 2]),
                            ALU.add, [pt, ADAB], [self.MOD])
            for i in range(self.nl):
                for grp in (1, 4):
                    v = self.MOD[:, i, grp * 16:(grp + 1) * 16, :]
                    self.ts("dve", v, v, 1.0, None, ALU.add, None, [self.MOD], [self.MOD])
                for grp in (2, 5):
                    v = self.MOD[:, i, grp * 16:(grp + 1) * 16, :]
                    self.ts("dve", v, v, 1.0 / ALPHA, None, ALU.mult, None, [self.MOD], [self.MOD])
            self.barrier()

    def precast(self, name, idx, K, M, rows=256, defer=False):
        src = self.inp[name][idx]
        dst = self.dram([K, M], BF16, "w16_" + name, nb=K // rows)
        for s in range(K // rows):
            job = (dst.t[s * rows:(s + 1) * rows, :], src[s * rows:(s + 1) * rows, :], dst.bs[s])
            if defer:
                self.pending.append(job)
            else:
                self.dma("pool", job[0], job[1], [], [job[2]])
        dst.rows = rows
        self.w16[(name, idx)] = dst
        return dst

    def drip(self, n=1):
        while n > 0 and self.pending:
            o, i, b = self.pending.pop(0)
            self.dma("pool", o, i, [], [b])
            n -= 1

    def drip_tick(self, pace_ev=None):
        self.tick += 1
        if self.pending and self.tick % self.drip_every == 0:
            if pace_ev is not None and not self.S.mute:
                self.S._wait("pool", pace_ev)
            self.drip(1)

    def stage_transpose_in(self):
        with ExitStack() as st:
            xin = Ring([self.sb([128, D], F32, "xin", st) for _ in range(2)])
            stg = Ring([self.sb([128, DC, 512], F32, "xstg", st) for _ in range(2)])
            pr = Ring([self.ps([128, 4, 128], F32, "tip", st) for _ in range(4)])
            for bi, (t0, nb, isctx) in self.blocks:
                sg = stg.next()
                for tt_ in range(nb // 128):
                    xi = xin.next()
                    if isctx:
                        src = self.inp["ctx"][tt_ * 128:(tt_ + 1) * 128, :]
                    else:
                        r0 = t0 - CTX + tt_ * 128
                        src = self.inp["x"][r0:r0 + 128, :]
                    self.dma("sp", xi[:], src, [], [xi])
                    for kg in range(4):
                        pt = pr.next()
                        self.tr([(pt[:, j, :], xi[:, (kg * 4 + j) * 128:(kg * 4 + j + 1) * 128], self.identf[:])
                                 for j in range(4)], [xi, self.identf], [pt])
                        eng = "dve" if kg % 2 == 0 else "act"
                        o = sg[:, kg * 4:(kg + 1) * 4, tt_ * 128:(tt_ + 1) * 128]
                        if eng == "dve":
                            self.tcopy("dve", o, pt[:], [pt], [sg])
                        else:
                            self.act(o, pt[:], AF.Copy, [pt], [sg])
                self.dma("sp", self.xt_v[:, :, t0:t0 + nb], sg[:, :, 0:nb], [sg], [self.XT.bs[bi]])
            self.barrier()

    def stage_transpose_out(self):
        with ExitStack() as st:
            xin = Ring([self.sb([128, DC, 512], F32, "oin", st) for _ in range(2)])
            stg = Ring([self.sb([128, D], F32, "ostg", st) for _ in range(2)])
            pr = Ring([self.ps([128, 4, 128], F32, "top", st) for _ in range(4)])
            for bi, (t0, nb, isctx) in self.blocks:
                if isctx:
                    continue
                xi = xin.next()
                self.dma("sp", xi[:, :, 0:nb], self.xt_v[:, :, t0:t0 + nb], [self.XT.bs[bi]], [xi])
                for tt_ in range(nb // 128):
                    sg = stg.next()
                    for kg in range(4):
                        pt = pr.next()
                        self.tr([(pt[:, j, :], xi[:, kg * 4 + j, tt_ * 128:(tt_ + 1) * 128], self.identf[:])
                                 for j in range(4)], [xi, self.identf], [pt])
                        o = sg[:, kg * 512:(kg + 1) * 512]
                        if kg % 2 == 0:
                            self.tcopy("dve", o, pt[:].rearrange("p a b -> p (a b)"), [pt], [sg])
                        else:
                            self.act(o, pt[:].rearrange("p a b -> p (a b)"), AF.Copy, [pt], [sg])
                    r0 = t0 - CTX + tt_ * 128
                    self.dma("sp", self.out.t[r0:r0 + 128, :], sg[:], [sg], [self.out])
            self.barrier()

    def wview(self, W, kt, m0, mw):
        return W.t.rearrange("(kk p) m -> p kk m", p=128)[:, kt * 16:(kt + 1) * 16, m0:m0 + mw]

    def wstrips(self, W, kt):
        n = 2048 // W.rows
        return W.bs[kt * n:(kt + 1) * n]

    def linear_fm(self, W, K, M, rhs_fn, rhs_reads, nb, epi, wring, pring, mstart=0):
        KT = K // 2048
        m0 = mstart
        while m0 < M:
            mw = min(256, M - m0)
            nj = mw // 128
            banks = [pring.next() for _ in range(nj)]
            for kt in range(KT):
                wt = wring.next()
                self.dma("sp", wt[:, :, 0:mw], self.wview(W, kt, m0, mw), self.wstrips(W, kt), [wt])
                for j in range(nj):
                    specs = [(banks[j][:, 0:nb], wt[:, kc, j * 128:(j + 1) * 128], rhs_fn(kt * 16 + kc),
                              kt == 0 and kc == 0, kt == KT - 1 and kc == 15) for kc in range(16)]
                    self.mm(specs, [wt] + rhs_reads, [banks[j]])
            for j in range(nj):
                epi((m0 - mstart) // 128 + j, banks[j])
            m0 += mw
            self.drip_tick(banks[-1].b.w)

    def linear_tm(self, W, K, M, lhs_fn, lhs_reads, nb, epi, wring, pring, mstart=0, mwmax=256):
        KT = K // 2048
        ntt = nb // 128
        m0 = mstart
        while m0 < M:
            mw = min(mwmax, M - m0)
            banks = [pring.next() for _ in range(ntt)]
            for kt in range(KT):
                wt = wring.next()
                self.dma("sp", wt[:, :, 0:mw], self.wview(W, kt, m0, mw), self.wstrips(W, kt), [wt])
                for tt_ in range(ntt):
                    specs = [(banks[tt_][:, 0:mw], lhs_fn(kt * 16 + kc, tt_), wt[:, kc, 0:mw],
                              kt == 0 and kc == 0, kt == KT - 1 and kc == 15) for kc in range(16)]
                    self.mm(specs, [wt] + lhs_reads, [banks[tt_]])
            for tt_ in range(ntt):
                epi(m0 - mstart, mw, tt_, banks[tt_])
            m0 += mw

    def ln_epi(self, li, which, col, XTb, nb, S1, S2, rsq_ring):
        gbase = (2 if which == 0 else 5) * 16

        def epi(m, bank):
            xs = XTb[:, m, 0:nb]
            self.stt("dve", xs, bank[:, 0:nb], self.MOD[:, li, gbase + m, col:col + 1], xs, ALU.mult, ALU.add,
                     [bank, self.MOD, XTb], [XTb])
            rs = rsq_ring.next()
            self.act(rs[:, 0:nb], xs, AF.Square, [XTb], [rs])
            self.mm([(S1[:, 0:nb], self.onesf[:], xs, m == 0, m == DC - 1)], [self.onesf, XTb], [S1])
            self.mm([(S2[:, 0:nb], self.onesf[:], rs[:, 0:nb], m == 0, m == DC - 1)], [self.onesf, rs], [S2])
        return epi

    def ln_finish(self, li, which, col, XTb, nb, S1, S2, mean, rstd, tmp, U=None, unext=None):
        self.act(mean[:, 0:nb], S1[:, 0:nb], AF.Copy, [S1], [mean], scale=1.0 / D)
        self.act(rstd[:, 0:nb], S2[:, 0:nb], AF.Copy, [S2], [rstd], scale=1.0 / D)
        self.tt("dve", tmp[:, 0:nb], mean[:, 0:nb], mean[:, 0:nb], ALU.mult, [mean], [tmp])
        self.tt("dve", rstd[:, 0:nb], rstd[:, 0:nb], tmp[:, 0:nb], ALU.subtract, [rstd, tmp], [rstd])
        self.ts("dve", rstd[:, 0:nb], rstd[:, 0:nb], LN_EPS / (ALPHA * ALPHA), None, ALU.add, None, [rstd], [rstd])
        self.act(rstd[:, 0:nb], rstd[:, 0:nb], AF.Ln, [rstd], [rstd])
        self.act(rstd[:, 0:nb], rstd[:, 0:nb], AF.Exp, [rstd], [rstd], scale=-0.5)
        xa = XTb[:, :, 0:nb]
        self.tt("dve", xa, xa, mean[:, 0:nb].unsqueeze(1).broadcast_to([128, DC, nb]), ALU.subtract, [XTb, mean], [XTb])
        self.tt("dve", xa, xa, rstd[:, 0:nb].unsqueeze(1).broadcast_to([128, DC, nb]), ALU.mult, [XTb, rstd], [XTb])
        lc = (li * 2 + which) * DC
        for m in range(DC):
            xs = XTb[:, m, 0:nb]
            self.ts("dve", xs, xs, self.LNG[:, lc + m:lc + m + 1], self.LNB[:, lc + m:lc + m + 1], ALU.mult, ALU.add,
                    [XTb, self.LNG, self.LNB], [XTb])
            if U is not None:
                l2, shb, scb = unext
                self.act(U[:, m, 0:nb], xs, AF.Identity, [XTb, self.MOD], [U],
                         scale=self.MOD[:, l2, scb * 16 + m, col:col + 1], bias=self.MOD[:, l2, shb * 16 + m, col:col + 1])

    def stage_post(self, li, YN, KY, Wout, W1, W2, last=False):
        with ExitStack() as st:
            XTb = self.sb([128, DC, 512], F32, "XTb", st)
            U = self.sb([128, DC, 512], BF16, "U", st)
            BIG = self.sb([128, HID // 128, 512], BF16, "BIG", st)
            wring = Ring([self.sb([128, 16, 256], BF16, "wt", st) for _ in range(4)])
            pring = Ring([self.ps([128, 512], F32, "mmp", st) for _ in range(4)])
            S1 = self.ps([128, 512], F32, "S1", st)
            S2 = self.ps([128, 512], F32, "S2", st)
            rsq = Ring([self.sb([128, 512], F32, "rsq", st) for _ in range(2)])
            rel = Ring([self.sb([128, 512], F32, "rel", st) for _ in range(2)])
            mean = self.sb([128, 512], F32, "mean", st)
            rstd = self.sb([128, 512], F32, "rstd", st)
            tmp = self.sb([128, 512], F32, "lntmp", st)
            ynv = YN.t.rearrange("(kk p) t -> p kk t", p=128)
            for bi, (t0, nb, isctx) in self.blocks:
                if last and isctx:
                    continue
                col = 1 if isctx else 0
                self.dma("sp", BIG[:, 0:KY // 128, 0:nb], ynv[:, :, t0:t0 + nb], [YN], [BIG])
                self.dma("pool", XTb[:, :, 0:nb], self.xt_v[:, :, t0:t0 + nb], [self.XT.bs[bi]], [XTb])
                self.linear_fm(Wout, KY, D, lambda kc: BIG[:, kc, 0:nb], [BIG], nb,
                               self.ln_epi(li, 0, col, XTb, nb, S1, S2, rsq), wring, pring)
                self.ln_finish(li, 0, col, XTb, nb, S1, S2, mean, rstd, tmp, U, (li, 3, 4))

                def relu2(m, bank):
                    r = rel.next()
                    self.act(r[:, 0:nb], bank[:, 0:nb], AF.Relu, [bank], [r])
                    self.tt("dve", BIG[:, m, 0:nb], r[:, 0:nb], r[:, 0:nb], ALU.mult, [r], [BIG])
                self.linear_fm(W1, D, HID, lambda kc: U[:, kc, 0:nb], [U], nb, relu2, wring, pring)
                self.linear_fm(W2, HID, D, lambda kc: BIG[:, kc, 0:nb], [BIG], nb,
                               self.ln_epi(li, 1, col, XTb, nb, S1, S2, rsq), wring, pring)
                self.ln_finish(li, 1, col, XTb, nb, S1, S2, mean, rstd, tmp)
                self.dma("pool", self.xt_v[:, :, t0:t0 + nb], XTb[:, :, 0:nb], [XTb], [self.XT.bs[bi]])
            self.barrier()

    def build_U(self, li, col, XTb, U, nb, shg, scg):
        for m in range(DC):
            self.act(U[:, m, 0:nb], XTb[:, m, 0:nb], AF.Identity, [XTb, self.MOD], [U],
                     scale=self.MOD[:, li, scg * 16 + m, col:col + 1], bias=self.MOD[:, li, shg * 16 + m, col:col + 1])

    def bcast_load(self, dst_tt, dst_ap, src_row_ap):
        self.dma("pool", dst_ap, src_row_ap.partition_broadcast(128), [], [dst_tt])

    def ssd_mixer(self, li, j, Win, ctx_out):
        NCH = NTOK // 128
        YN = self.dram([D_INNER, NTOK], BF16, "YN")
        RAWT = self.dram([SSM_XBC, NTOK], F32, "RAWT")
        ZTOK = self.dram([NTOK, D_INNER], BF16, "ZTOK")
        DTRAW = self.dram([NTOK, 128], F32, "DTRAW")
        XSTOK = self.dram([NTOK, 5120], BF16, "XSTOK")
        BCT = self.dram([2048, NTOK], BF16, "BCT")
        CUMT = self.dram([128, NTOK], F32, "CUMT")
        STD = self.dram([2, NCH, 128, D_INNER], BF16, "STD")
        chunks = sorted(set(c for _, (t0, nb, _) in self.blocks for c in range(t0 // 128, (t0 + nb) // 128)))

        stages = self.cfg.get("ssd_stages", (1, 2, 3, 4, 5))
        self.S.mute = 1 not in stages
        with ExitStack() as st:
            XTb = self.sb([128, DC, 512], F32, "XTb", st)
            U = self.sb([128, DC, 512], BF16, "U", st)
            wring = Ring([self.sb([128, 16, 256], BF16, "wt", st) for _ in range(4)])
            pring = Ring([self.ps([128, 512], F32, "mmp", st) for _ in range(4)])
            zst = Ring([self.sb([128, 512], BF16, "zst", st) for _ in range(4)])
            wring2 = Ring([self.sb([128, 16, 512], BF16, "wt2", st) for _ in range(3)])
            rst = Ring([self.sb([128, 512], F32, "rst", st) for _ in range(4)])
            for bi, (t0, nb, isctx) in self.blocks:
                col = 1 if isctx else 0
                self.dma("sp", XTb[:, :, 0:nb], self.xt_v[:, :, t0:t0 + nb], [self.XT.bs[bi]], [XTb])
                self.build_U(li, col, XTb, U, nb, 0, 1)

                def zepi(m0r, mw, tt_, bank):
                    z = zst.next()
                    self.act(z[:, 0:mw], bank[:, 0:mw], AF.Silu, [bank], [z])
                    self.dma("pool", ZTOK.t[t0 + tt_ * 128:t0 + (tt_ + 1) * 128, m0r:m0r + mw], z[:, 0:mw], [z], [ZTOK])
                self.linear_tm(Win, D, D_INNER, lambda kc, tt_: U[:, kc, tt_ * 128:(tt_ + 1) * 128], [U], nb, zepi,
                               wring2, pring, mstart=0, mwmax=512)

                def xepi(m, bank):
                    r = rst.next()
                    if m % 2 == 0:
                        self.tcopy("dve", r[:, 0:nb], bank[:, 0:nb], [bank], [r])
                    else:
                        self.act(r[:, 0:nb], bank[:, 0:nb], AF.Copy, [bank], [r])
                    self.dma("pool", RAWT.t[m * 128:(m + 1) * 128, t0:t0 + nb], r[:, 0:nb], [r], [RAWT])
                self.linear_fm(Win, D, D_INNER + SSM_XBC, lambda kc: U[:, kc, 0:nb], [U], nb, xepi, wring, pring,
                               mstart=D_INNER)

                def depi(m0r, mw, tt_, bank):
                    r = rst.next()
                    self.tcopy("dve", r[:, 0:mw], bank[:, 0:mw], [bank], [r])
                    self.dma("pool", DTRAW.t[t0 + tt_ * 128:t0 + (tt_ + 1) * 128, :], r[:, 0:mw], [r], [DTRAW])
                self.linear_tm(Win, D, SSM_IN, lambda kc, tt_: U[:, kc, tt_ * 128:(tt_ + 1) * 128], [U], nb, depi,
                               wring, pring, mstart=D_INNER + SSM_XBC)
            self.barrier()
        if self.cfg.get("ssd_stop") == 1:
            return YN

        self.S.mute = 2 not in stages
        with ExitStack() as st:
            CW = self.sb([128, 240], F32, "CW", st)
            CBv = self.sb([128, 48], F32, "CBv", st)
            cwv = self.inp["ssd_conv_w"][j].rearrange("k (m p) -> (k m) p", p=128)
            self.load_vec_fm(cwv[0:128, :], 128, CW[:, 0:128], CW, st)
            self.load_vec_fm(cwv[128:240, :], 112, CW[:, 128:240], CW, st)
            self.load_vec_fm(self.inp["ssd_conv_b"][j:j + 1, :].rearrange("o (m p) -> (o m) p", p=128), 48, CBv[:], CBv, st)
            PADW = 4 + CTX + 4 + SEQ
            raws = [self.sb([128, PADW], F32, "raw", st) for _ in range(2)]
            for r in raws:
                self.memset("dve", r[:], 0.0, [r])
            rawr = Ring(raws)
            accr = Ring([self.sb([128, NTOK], F32, "acc", st) for _ in range(2)])
            xbr = Ring([self.sb([128, NTOK], BF16, "xb", st) for _ in range(2)])
            tpr = Ring([self.ps([128, 8, 128], BF16, "ctp", st) for _ in range(2)])
            tsr = Ring([self.sb([128, NCH, 128], BF16, "cts", st) for _ in range(2)])
            xsv = XSTOK.t.rearrange("(c p) f -> p c f", p=128)
            for m2 in range(0, 48, 2):
                pair = []
                for m in (m2, m2 + 1):
                    raw = rawr.next()
                    self.dma("sp", raw[:, 2:2 + CTX], RAWT.t[m * 128:(m + 1) * 128, 0:CTX], [RAWT], [raw])
                    self.dma("sp", raw[:, CTX + 6:CTX + 6 + SEQ], RAWT.t[m * 128:(m + 1) * 128, CTX:NTOK], [RAWT], [raw])
                    pair.append((m, raw, accr.next()))
                for (oi, oo, n) in ((0, 0, CTX), (CTX + 4, CTX, SEQ)):
                    for (m, raw, acc) in pair:
                        self.ts("dve", acc[:, oo:oo + n], raw[:, oi:oi + n], CW[:, m:m + 1], None, ALU.mult, None,
                                [raw, CW], [acc])
                    for k in range(1, 5):
                        for (m, raw, acc) in pair:
                            self.stt("dve", acc[:, oo:oo + n], raw[:, oi + k:oi + k + n], CW[:, k * 48 + m:k * 48 + m + 1],
                                     acc[:, oo:oo + n], ALU.mult, ALU.add, [raw, CW, acc], [acc])
                for (m, raw, acc) in pair:
                    xb = xbr.next()
                    self.act(xb[:], acc[:], AF.Silu, [acc, CBv], [xb], bias=CBv[:, m:m + 1])
                    if m >= 32:
                        self.dma("pool", BCT.t[(m - 32) * 128:(m - 31) * 128, :], xb[:], [xb], [BCT])
                    if m < 40:
                        tsb = tsr.next()
                        for c0 in range(0, NCH, 8):
                            cn = min(8, NCH - c0)
                            tp = tpr.next()
                            self.tr([(tp[:, i, :], xb[:, (c0 + i) * 128:(c0 + i + 1) * 128], self.identb[:]) for i in range(cn)],
                                    [xb, self.identb], [tp])
                            self.tcopy("dve", tsb[:, c0:c0 + cn, :], tp[:, 0:cn, :], [tp], [tsb])
                        for c0 in range(0, NCH, 9):
                            cn = min(9, NCH - c0)
                            self.dma("pool", xsv[:, c0:c0 + cn, m * 128:(m + 1) * 128], tsb[:, c0:c0 + cn, :], [tsb], [XSTOK])
            self.barrier()
        if self.cfg.get("ssd_stop") == 2:
            return YN

        self.S.mute = 3 not in stages
        with ExitStack() as sst:
            DT = self.sb([128, NCH, 128], F32, "DT", sst)
            CUM = self.sb([128, NCH, 128], F32, "CUM", sst)
            triu = self.sb([128, 128], F32, "triu", sst)
            tril = self.sb([128, 128], F32, "tril", sst)
            self.dma("sp", triu[:], self.inp["triu"], [], [triu])
            self.dma("sp", tril[:], self.inp["tril"], [], [tril])
            DSK = self.sb([128, 64], F32, "DSK", sst)
            self.bcast_load(DSK, DSK[:], self.inp["ssd_d"][j:j + 1, :])
            NGfm = self.sb([128, 32], F32, "NGfm", sst)
            with ExitStack() as st:
                self.load_vec_fm(self.inp["ssd_norm_g"][j:j + 1, :].rearrange("o (m p) -> (o m) p", p=128), 32,
                                 NGfm[:], NGfm, st)
                self.barrier()
            with ExitStack() as s2:
                Wd = self.sb([128, NCH, 128], F32, "Wd", s2)
                DEC = self.sb([128, NCH, 128], F32, "DEC", s2)
                with ExitStack() as st:
                    DTB = self.sb([128, 128], F32, "DTB", st)
                    AB = self.sb([128, 128], F32, "AB", st)
                    self.bcast_load(DTB, DTB[:], self.inp["ssd_dt_bias"][j:j + 1].rearrange("o a h -> o (a h)"))
                    self.bcast_load(AB, AB[:], self.inp["ssd_a_log"][j:j + 1].rearrange("o a h -> o (a h)"))
                    self.act(AB[:], AB[:], AF.Exp, [AB], [AB])
                    self.ts("dve", AB[:], AB[:], -1.0, None, ALU.mult, None, [AB], [AB])
                    import os
                    CUT = int(os.environ.get("S3CUT", "0"))
                    if CUT == 1:
                        self.S.mute = True
                    dtv = DTRAW.t.rearrange("(c p) h -> p c h", p=128)
                    for c0 in range(0, NCH, 8):
                        cn = min(8, NCH - c0)
                        self.dma("sp", DT[:, c0:c0 + cn, :], dtv[:, c0:c0 + cn, :], [DTRAW], [DT])
                    self.tt("dve", DT[:], DT[:], DTB[:].unsqueeze(1).broadcast_to([128, NCH, 128]), ALU.add, [DT, DTB], [DT])
                    self.act(DT[:], DT[:], AF.Exp, [DT], [DT])
                    self.act(DT[:], DT[:], AF.Ln, [DT], [DT], bias=1.0)
                    if CUT == 2:
                        self.S.mute = True
                    dtA = self.sb([128, NCH, 128], F32, "dtA", st)
                    self.tt("dve", dtA[:], DT[:], AB[:].unsqueeze(1).broadcast_to([128, NCH, 128]), ALU.mult, [DT, AB], [dtA])
                    CTs = self.sb([128, NTOK], F32, "CTs", st)
                    pcr = Ring([self.ps([128, 128], F32, "pc", st) for _ in range(2)])
                    ptr_ = Ring([self.ps([128, 128], F32, "pt", st) for _ in range(2)])
                    pxr = Ring([self.ps([128, 128], F32, "px", st) for _ in range(2)])
                    for c in range(NCH):
                        pc = pcr.next()
                        self.mm([(pc[:, 0:64], triu[:], dtA[:, c, 0:64], True, True)], [triu, dtA], [pc])
                        self.mm([(pc[:, 64:128], tril[:], dtA[:, c, 64:128], True, True)], [tril, dtA], [pc])
                        self.tcopy("dve", CUM[:, c, :], pc[:], [pc], [CUM])
                        if CUT == 4:
                            continue
                        pt = ptr_.next()
                        self.mm([(pt[:], self.onesf[:], dtA[:, c, :], True, True)], [self.onesf, dtA], [pt])
                        if CUT != 7:
                            self.tcopy("dve", DEC[:, c, :], pt[:], [pt], [DEC])
                        if CUT != 6:
                            self.tt("dve", Wd[:, c, :], pt[:], CUM[:, c, :], ALU.subtract, [pt, CUM], [Wd])
                        if CUT in (6, 7, 8):
                            continue
                        if CUT == 5:
                            continue
                        px = pxr.next()
                        self.tr([(px[:], CUM[:, c, :], self.identf[:])], [CUM, self.identf], [px])
                        self.act(CTs[:, c * 128:(c + 1) * 128], px[:], AF.Copy, [px], [CTs])
                    if CUT == 3:
                        self.S.mute = True
                    self.act(DEC[:], DEC[:], AF.Exp, [DEC], [DEC])
                    self.act(Wd[:], Wd[:], AF.Exp, [Wd], [Wd])
                    self.tt("dve", Wd[:], Wd[:], DT[:], ALU.mult, [Wd, DT], [Wd])
                    self.dma("sp", CUMT.t, CTs[:], [CTs], [CUMT])
                    self.barrier()
                if self.cfg.get("ssd_stop") == 3:
                    return YN

                self.S.mute = 4 not in stages
                with ExitStack() as st:
                    Sst = [self.sb([128, D_INNER], F32, "Sst", st) for _ in range(2)]
                    for s_ in Sst:
                        self.memset("dve", s_[:], 0.0, [s_])
                    xr = Ring([self.sb([128, 5120], BF16, "Xs", st) for _ in range(3)])
                    xwr = Ring([self.sb([128, D_INNER], BF16, "xw", st) for _ in range(2)])
                    sbr = Ring([self.sb([128, D_INNER], BF16, "sbf", st) for _ in range(2)])
                    pnr = Ring([self.ps([128, 512], F32, "pn", st) for _ in range(4)])
                    order = [list(range(NCH)), [1, 0] + list(range(NCH - 1, 1, -1))]
                    for step in range(NCH):
                        for dr in (0, 1):
                            c = order[dr][step]
                            X = xr.next()
                            self.dma("sp", X[:], XSTOK.t[c * 128:(c + 1) * 128, :], [XSTOK], [X])
                            xw = xwr.next()
                            self.tt("dve", xw[:].rearrange("p (h q) -> p h q", q=64),
                                    X[:, 0:D_INNER].rearrange("p (h q) -> p h q", q=64),
                                    Wd[:, c, dr * 64:(dr + 1) * 64].unsqueeze(2).broadcast_to([128, 64, 64]), ALU.mult,
                                    [X, Wd], [xw])
                            sbf = sbr.next()
                            self.act(sbf[:], Sst[dr][:], AF.Copy, [Sst[dr]], [sbf])
                            self.dma("pool", STD.t[dr, c], sbf[:], [sbf], [STD])
                            for g in range(8):
                                pn = pnr.next()
                                self.mm([(pn[:], X[:, D_INNER + g * 128:D_INNER + (g + 1) * 128],
                                          xw[:, g * 512:(g + 1) * 512], True, True)], [X, xw], [pn])
                                sv = Sst[dr][:, g * 512:(g + 1) * 512].rearrange("p (h q) -> p h q", q=64)
                                self.tt("dve", sv, sv,
                                        DEC[:, c, dr * 64 + g * 8:dr * 64 + g * 8 + 8].unsqueeze(2).broadcast_to([128, 8, 64]),
                                        ALU.mult, [Sst[dr], DEC], [Sst[dr]])
                                self.tt("dve", Sst[dr][:, g * 512:(g + 1) * 512], Sst[dr][:, g * 512:(g + 1) * 512], pn[:],
                                        ALU.add, [Sst[dr], pn], [Sst[dr]])
                    self.barrier()
                if self.cfg.get("ssd_stop") == 4:
                    return YN

            self.S.mute = 5 not in stages
            with ExitStack() as st:
                xr = Ring([self.sb([128, 5120], BF16, "Xs", st) for _ in range(2)])
                bcr = Ring([self.sb([128, 16, 128], BF16, "bct", st) for _ in range(2)])
                XDT = [self.sb([128, D_INNER], BF16, "xdt", st) for _ in range(2)]
                XD = self.sb([128, D_INNER], BF16, "xd", st)
                Zt = self.sb([128, D_INNER], BF16, "zt", st)
                YS = self.sb([128, D_INNER], F32, "ys", st)
                yn = self.sb([128, D_INNER], BF16, "yn", st)
                junk = self.sb([128, 512], F32, "junk", st)
                YNT = self.sb([128, 32, 128], BF16, "ynt", st)
                STt = [self.sb([128, D_INNER], BF16, "stt", st) for _ in range(2)]
                EC = self.sb([128, 128], F32, "ec", st)
                ss = self.sb([128, 8], F32, "ss", st)
                cbr = Ring([self.sb([128, 8, 128], F32, "cbt", st) for _ in range(4)])
                argr = Ring([self.sb([128, 8, 128], F32, "arg", st) for _ in range(4)])
                mtr = Ring([self.sb([128, 8, 128], BF16, "mt", st) for _ in range(3)])
                cbmr = Ring([self.sb([128, 128], F32, "cbm", st) for _ in range(4)])
                y2r = Ring([self.sb([128, 512], BF16, "y2s", st) for _ in range(3)])
                pcb = Ring([self.ps([128, 128], F32, "pcb", st) for _ in range(2)])
                pyr = Ring([self.ps([128, 512], F32, "py", st) for _ in range(2)])
                pcs = Ring([self.ps([128, 512], F32, "pcs", st) for _ in range(2)])
                ptp = Ring([self.ps([128, 8, 128], BF16, "ptp", st) for _ in range(2)])
                masks = [triu, tril]
                bcv = BCT.t.rearrange("(k p) t -> p k t", p=128)
                ynv = YN.t.rearrange("(k p) t -> p k t", p=128)
                for c in chunks:
                    if c < 2 and not ctx_out:
                        continue
                    X = xr.next()
                    self.dma("sp", X[:], XSTOK.t[c * 128:(c + 1) * 128, :], [XSTOK], [X])
                    bct = bcr.next()
                    self.dma("sp", bct[:], bcv[:, :, c * 128:(c + 1) * 128], [BCT], [bct])
                    self.dma("sp", Zt[:], ZTOK.t[c * 128:(c + 1) * 128, :], [ZTOK], [Zt])
                    for dr in (0, 1):
                        self.dma("sp", STt[dr][:], STD.t[dr, c], [STD], [STt[dr]])
                        self.tt("dve", XDT[dr][:].rearrange("p (h q) -> p h q", q=64),
                                X[:, 0:D_INNER].rearrange("p (h q) -> p h q", q=64),
                                DT[:, c, dr * 64:(dr + 1) * 64].unsqueeze(2).broadcast_to([128, 64, 64]), ALU.mult,
                                [X, DT], [XDT[dr]])
                    self.tt("dve", XD[:].rearrange("p (h q) -> p h q", q=64),
                            X[:, 0:D_INNER].rearrange("p (h q) -> p h q", q=64),
                            DSK[:].unsqueeze(2).broadcast_to([128, 64, 64]), ALU.mult, [X, DSK], [XD])
                    self.act(EC[:], CUM[:, c, :], AF.Exp, [CUM], [EC])
                    def part_a(g):
                        pc = pcb.next()
                        self.mm([(pc[:], bct[:, g, :], bct[:, 8 + g, :], True, True)], [bct], [pc])
                        res = []
                        for dr in (0, 1):
                            cbm = cbmr.next()
                            self.tt("dve", cbm[:], pc[:], masks[dr][:], ALU.mult, [pc, masks[dr]], [cbm])
                            cbt = cbr.next()
                            r0 = dr * 64 + g * 8
                            self.dma("pool", cbt[:], CUMT.t[r0:r0 + 8, c * 128:(c + 1) * 128].partition_broadcast(128), [CUMT], [cbt])
                            arg = argr.next()
                            self.tt("dve", arg[:], cbt[:], CUM[:, c, r0:r0 + 8].unsqueeze(2).broadcast_to([128, 8, 128]),
                                    ALU.subtract, [cbt, CUM], [arg])
                            res.append((cbm, arg, r0))
                        for (cbm, arg, r0) in res:
                            self.act(arg[:], arg[:], AF.Exp, [arg], [arg])
                        return g, res

                    def part_b(g, res):
                        py = pyr.next()
                        self.mm([(py[:], self.identb[:], XD[:, g * 512:(g + 1) * 512], True, False)], [self.identb, XD], [py])
                        for dr in (0, 1):
                            cbm, arg, r0 = res[dr]
                            mt = mtr.next()
                            self.stt("dve", mt[:], arg[:], 1.0, cbm[:].unsqueeze(1).broadcast_to([128, 8, 128]),
                                     ALU.min, ALU.mult, [arg, cbm], [mt])
                            self.mm([(py[:, h * 64:(h + 1) * 64], mt[:, h, :],
                                      XDT[dr][:, (g * 8 + h) * 64:(g * 8 + h + 1) * 64], False, False) for h in range(8)],
                                    [mt, XDT[dr]], [py])
                            pq = pcs.next()
                            self.mm([(pq[:], bct[:, 8 + g, :], STt[dr][:, g * 512:(g + 1) * 512], True, True)],
                                    [bct, STt[dr]], [pq])
                            y2 = y2r.next()
                            self.tt("dve", y2[:].rearrange("p (h q) -> p h q", q=64),
                                    pq[:].rearrange("p (h q) -> p h q", q=64),
                                    EC[:, r0:r0 + 8].unsqueeze(2).broadcast_to([128, 8, 64]), ALU.mult, [pq, EC], [y2])
                            self.mm([(py[:], self.identb[:], y2[:], False, dr == 1)], [self.identb, y2], [py])
                        self.tt("dve", YS[:, g * 512:(g + 1) * 512], py[:], Zt[:, g * 512:(g + 1) * 512], ALU.mult,
                                [py, Zt], [YS])
                        self.act(junk[:], YS[:, g * 512:(g + 1) * 512], AF.Square, [YS], [junk, ss], accum_out=ss[:, g:g + 1])

                    pend = None
                    for g in range(8):
                        cur = part_a(g)
                        if pend is not None:
                            part_b(*pend)
                        pend = cur
                    part_b(*pend)
                    self.ts("dve", ss[:], ss[:], 1.0 / 512, RMS_EPS, ALU.mult, ALU.add, [ss], [ss])
                    self.act(ss[:], ss[:], AF.Ln, [ss], [ss])
                    self.act(ss[:], ss[:], AF.Exp, [ss], [ss], scale=-0.5)
                    for g in range(8):
                        self.act(yn[:, g * 512:(g + 1) * 512], YS[:, g * 512:(g + 1) * 512], AF.Copy, [YS, ss], [yn],
                                 scale=ss[:, g:g + 1])
                    for k0 in range(0, 32, 8):
                        tp = ptp.next()
                        self.tr([(tp[:, i, :], yn[:, (k0 + i) * 128:(k0 + i + 1) * 128], self.identb[:]) for i in range(8)],
                                [yn, self.identb], [tp])
                        self.tt("dve", YNT[:, k0:k0 + 8, :], tp[:], NGfm[:, k0:k0 + 8].unsqueeze(2).broadcast_to([128, 8, 128]),
                                ALU.mult, [tp, NGfm], [YNT])
                    self.dma("sp", ynv[:, :, c * 128:(c + 1) * 128], YNT[:], [YNT], [YN])
                self.barrier()
        self.S.mute = False
        return YN

    def attn_mixer(self, li, Wqkv, ctx_out):
        NCH = NTOK // 128
        YN = self.dram([D, NTOK], BF16, "AYN")
        QT = self.dram([D, NTOK], BF16, "QT")
        KT2 = self.dram([4, 128, NTOK + 128], BF16, "KT2")
        VT = self.dram([NTOK + 128, 256], BF16, "VT")
        SCALE = 0.125
        with ExitStack() as st:
            XTb = self.sb([128, DC, 512], F32, "XTb", st)
            U = self.sb([128, DC, 512], BF16, "U", st)
            wring = Ring([self.sb([128, 16, 256], BF16, "wt", st) for _ in range(4)])
            pring = Ring([self.ps([128, 512], F32, "mmp", st) for _ in range(4)])
            psw = Ring([self.ps([128, 512], F32, "psw", st) for _ in range(2)])
            rotT = self.sb([128, 128], F32, "rotT", st)
            self.dma("sp", rotT[:], self.inp["rotT"], [], [rotT])
            COS = self.sb([128, 512], F32, "COS", st)
            SIN = self.sb([128, 512], F32, "SIN", st)
            x32r = Ring([self.sb([128, 512], F32, "x32", st) for _ in range(3)])
            t1r = Ring([self.sb([128, 512], F32, "t1", st) for _ in range(2)])
            t2r = Ring([self.sb([128, 512], F32, "t2", st) for _ in range(2)])
            qor = Ring([self.sb([128, 512], BF16, "qo", st) for _ in range(3)])
            vor = Ring([self.sb([128, 256], BF16, "vo", st) for _ in range(3)])
            zer = self.sb([128, 256], BF16, "zer", st)
            self.memset("dve", zer[:], 0.0, [zer])
            for kv in range(4):
                self.dma("sp", KT2.t[kv, :, NTOK:NTOK + 128], zer[:, 0:128], [zer], [KT2])
            self.dma("sp", VT.t[NTOK:NTOK + 128, :], zer[:], [zer], [VT])
            for bi, (t0, nb, isctx) in self.blocks:
                col = 1 if isctx else 0
                self.dma("sp", XTb[:, :, 0:nb], self.xt_v[:, :, t0:t0 + nb], [self.XT.bs[bi]], [XTb])
                self.build_U(li, col, XTb, U, nb, 0, 1)
                if not isctx:
                    self.dma("sp", COS[:, 0:nb], self.inp["ropec"][:, t0 - CTX:t0 - CTX + nb], [], [COS])
                    self.dma("sp", SIN[:, 0:nb], self.inp["ropes"][:, t0 - CTX:t0 - CTX + nb], [], [SIN])

                def qkepi(m, bank):
                    qo = qor.next()
                    if isctx:
                        self.act(qo[:, 0:nb], bank[:, 0:nb], AF.Copy, [bank], [qo])
                    else:
                        x32 = x32r.next()
                        self.act(x32[:, 0:nb], bank[:, 0:nb], AF.Copy, [bank], [x32])
                        pw = psw.next()
                        self.mm([(pw[:, 0:nb], rotT[:], x32[:, 0:nb], True, True)], [rotT, x32], [pw])
                        t1 = t1r.next()
                        t2 = t2r.next()
                        self.tt("dve", t1[:, 0:nb], x32[:, 0:nb], COS[:, 0:nb], ALU.mult, [x32, COS], [t1])
                        self.tt("dve", t2[:, 0:nb], pw[:, 0:nb], SIN[:, 0:nb], ALU.mult, [pw, SIN], [t2])
                        self.tt("dve", qo[:, 0:nb], t1[:, 0:nb], t2[:, 0:nb], ALU.add, [t1, t2], [qo])
                    if m < 16:
                        self.dma("pool", QT.t[m * 128:(m + 1) * 128, t0:t0 + nb], qo[:, 0:nb], [qo], [QT])
                    else:
                        for hh in range(2):
                            kv = (m - 16) * 2 + hh
                            for dup in range(2):
                                self.dma("pool", KT2.t[kv, dup * 64:(dup + 1) * 64, t0:t0 + nb],
                                         qo[hh * 64:(hh + 1) * 64, 0:nb], [qo], [KT2])
                self.linear_fm(Wqkv, D, 2304, lambda kc: U[:, kc, 0:nb], [U], nb, qkepi, wring, pring, mstart=0)

                def vepi(m0r, mw, tt_, bank):
                    vo = vor.next()
                    self.act(vo[:, 0:mw], bank[:, 0:mw], AF.Copy, [bank], [vo])
                    self.dma("pool", VT.t[t0 + tt_ * 128:t0 + (tt_ + 1) * 128, :], vo[:, 0:mw], [vo], [VT])
                self.linear_tm(Wqkv, D, 2560, lambda kc, tt_: U[:, kc, tt_ * 128:(tt_ + 1) * 128], [U], nb, vepi,
                               wring, pring, mstart=2304)
            self.barrier()

        with ExitStack() as st:
            SINK = self.sb([128, 32], F32, "SINK", st)
            self.bcast_load(SINK, SINK[:], self.inp["attn_sink"][0:1, :])
            SINKS = self.sb([128, 32], F32, "SINKS", st)
            self.ts("dve", SINKS[:], SINK[:], 1.0 / SCALE, None, ALU.mult, None, [SINK], [SINKS])
            AM = self.sb([128, 3, 640], F32, "AM", st)
            self.dma("sp", AM[:], self.inp["amask"].rearrange("v p s -> p v s"), [], [AM])
            KC = self.sb([128, 4, 256], BF16, "KC", st)
            self.dma("sp", KC[:], KT2.t[:, :, 0:256].rearrange("k p t -> p k t"), [KT2], [KC])
            VC = self.sb([128, 2, 256], BF16, "VC", st)
            self.dma("sp", VC[:], VT.t[0:256, :].rearrange("(c p) f -> p c f", p=128), [VT], [VC])
            qr = Ring([self.sb([128, 16, 128], BF16, "Qb", st) for _ in range(2)])
            klr = Ring([self.sb([128, 4, 384], BF16, "Kl", st) for _ in range(2)])
            vlr = Ring([self.sb([128, 3, 256], BF16, "Vl", st) for _ in range(2)])
            ssr = Ring([self.sb([128, 640], F32, "Ssb", st) for _ in range(3)])
            ppr = Ring([self.sb([128, 640], BF16, "Pb", st) for _ in range(3)])
            ptr_ = Ring([self.sb([128, 5, 128], BF16, "PT", st) for _ in range(3)])
            smr = Ring([self.sb([128, 8], F32, "sm", st) for _ in range(4)])
            Ot = self.sb([128, D], BF16, "Ot", st)
            OT = self.sb([128, 16, 128], BF16, "OTt", st)
            psA = Ring([self.ps([128, 512], F32, "psA", st) for _ in range(2)])
            psB = Ring([self.ps([128, 128], F32, "psB", st) for _ in range(2)])
            psT = Ring([self.ps([128, 5, 128], BF16, "psT", st) for _ in range(2)])
            psO = Ring([self.ps([128, 64], F32, "psO", st) for _ in range(2)])
            qtv = QT.t.rearrange("(k p) t -> p k t", p=128)
            ynv = YN.t.rearrange("(k p) t -> p k t", p=128)
            vtv = VT.t.rearrange("(c p) f -> p c f", p=128)
            chunks = sorted(set(c for _, (t0, nb, _) in self.blocks for c in range(t0 // 128, (t0 + nb) // 128)))
            for c in chunks:
                isctx = c < 2
                if isctx and not ctx_out:
                    continue
                Qb = qr.next()
                self.dma("sp", Qb[:], qtv[:, :, c * 128:(c + 1) * 128], [QT], [Qb])
                if not isctx:
                    Kl = klr.next()
                    self.dma("sp", Kl[:], KT2.t[:, :, (c - 1) * 128:(c + 2) * 128].rearrange("k p t -> p k t"), [KT2], [Kl])
                    Vl = vlr.next()
                    self.dma("sp", Vl[:], vtv[:, c - 1:c + 2, :], [VT], [Vl])
                    W = 640
                    mv = 0 if c == 2 else (2 if c == NCH - 1 else 1)
                else:
                    W = 256
                nkc = W // 128
                def part_a(h):
                    kv = h // 8
                    pb = (h % 2) * 64
                    qap = Qb[pb:pb + 64, h // 2, :]
                    pa = psA.next()
                    self.mm([(pa[:, 0:256], qap, KC[pb:pb + 64, kv, :], True, True)], [Qb, KC], [pa])
                    Ssb = ssr.next()
                    if not isctx:
                        self.mm([(pa[:, 256:512], qap, Kl[pb:pb + 64, kv, 0:256], True, True)], [Qb, Kl], [pa])
                        pbk = psB.next()
                        self.mm([(pbk[:], qap, Kl[pb:pb + 64, kv, 256:384], True, True)], [Qb, Kl], [pbk])
                        self.tt("dve", Ssb[:, 0:512], pa[:], AM[:, mv, 0:512], ALU.add, [pa, AM], [Ssb])
                        self.tt("dve", Ssb[:, 512:640], pbk[:], AM[:, mv, 512:640], ALU.add, [pbk, AM], [Ssb])
                    else:
                        self.tcopy("dve", Ssb[:, 0:256], pa[:, 0:256], [pa], [Ssb])
                    sm = smr.next()
                    self.S.op("dve", lambda E, o=sm[:, 0:1], i=Ssb[:, 0:W]: E.reduce_max(out=o, in_=i, axis=AX.X),
                              bl([Ssb]), bl([sm]))
                    self.tt("dve", sm[:, 1:2], sm[:, 0:1], SINKS[:, h:h + 1], ALU.max, [sm, SINKS], [sm])
                    self.ts("dve", sm[:, 1:2], sm[:, 1:2], -SCALE, None, ALU.mult, None, [sm], [sm])
                    Pb = ppr.next()
                    self.act(Pb[:, 0:W], Ssb[:, 0:W], AF.Exp, [Ssb, sm], [Pb, sm], scale=SCALE, bias=sm[:, 1:2],
                             accum_out=sm[:, 2:3])
                    self.act(sm[:, 3:4], SINK[:, h:h + 1], AF.Exp, [SINK, sm], [sm], bias=sm[:, 1:2])
                    self.tt("dve", sm[:, 4:5], sm[:, 2:3], sm[:, 3:4], ALU.add, [sm], [sm])
                    self.S.op("dve", lambda E, o=sm[:, 4:5], i=sm[:, 4:5]: E.reciprocal(out=o, in_=i), bl([sm]), bl([sm]))
                    return (h, kv, Pb, sm)

                def part_b(h, kv, Pb, sm):
                    pt = psT.next()
                    self.tr([(pt[:, k, :], Pb[:, k * 128:(k + 1) * 128], self.identb[:]) for k in range(nkc)],
                            [Pb, self.identb], [pt])
                    PT = ptr_.next()
                    self.act(PT[:, 0:nkc, :], pt[:, 0:nkc, :], AF.Copy, [pt], [PT])
                    po = psO.next()
                    specs = []
                    for k in range(nkc):
                        vsrc = VC[:, k, kv * 64:(kv + 1) * 64] if k < 2 else Vl[:, k - 2, kv * 64:(kv + 1) * 64]
                        specs.append((po[:], PT[:, k, :], vsrc, k == 0, k == nkc - 1))
                    self.mm(specs, [PT, VC] + ([Vl] if not isctx else []), [po])
                    self.act(Ot[:, h * 64:(h + 1) * 64], po[:], AF.Copy, [po, sm], [Ot], scale=sm[:, 4:5])

                pend = None
                for h in range(32):
                    cur = part_a(h)
                    if pend is not None:
                        part_b(*pend)
                    pend = cur
                part_b(*pend)
                for k0 in range(0, 16, 5):
                    kn = min(5, 16 - k0)
                    pt = psT.next()
                    self.tr([(pt[:, i, :], Ot[:, (k0 + i) * 128:(k0 + i + 1) * 128], self.identb[:]) for i in range(kn)],
                            [Ot, self.identb], [pt])
                    self.tcopy("dve", OT[:, k0:k0 + kn, :], pt[:, 0:kn, :], [pt], [OT])
                self.dma("sp", ynv[:, :, c * 128:(c + 1) * 128], OT[:], [OT], [YN])
            self.barrier()
        return YN

    def precast_layer(self, li, kind, j, defer=False):
        w = {}
        if kind == 0:
            w["in"] = self.precast("ssd_w_in", j, D, SSM_IN, defer=defer)
            w["out"] = self.precast("ssd_w_out", j, D_INNER, D, defer=defer)
        elif kind == 1:
            w["in"] = self.precast("attn_w_qkv", j, D, 2560, defer=defer)
            w["out"] = self.precast("attn_w_o", j, D, D, defer=defer)
        else:
            w["in"] = self.precast("hy_w_in", j, D, 3 * D, defer=defer)
            w["out"] = self.precast("hy_w_out", j, D, D, defer=defer)
        w["w1"] = self.precast("mlp_w1", li, D, HID, defer=defer)
        w["w2"] = self.precast("mlp_w2", li, HID, D, defer=defer)
        return w

    def run_all(self):
        layers = self.cfg.get("layers", [(0, 0), (1, 0), (2, 0), (0, 1)])
        full = "layers" not in self.cfg
        w = self.precast_layer(0, *layers[0])
        self.stage_mod()
        self.stage_transpose_in()
        for li, (kind, j) in enumerate(layers):
            last = full and li == len(layers) - 1
            if kind == 0:
                YN = self.ssd_mixer(li, j, w["in"], not last)
                KY = D_INNER
            elif kind == 1:
                YN = self.attn_mixer(li, w["in"], not last)
                KY = D
            else:
                YN = self.hyena_mixer(li, w["in"], not last)
                KY = D
            wn = self.precast_layer(li + 1, *layers[li + 1], defer=True) if li + 1 < len(layers) else None
            self.stage_post(li, YN, KY, w["out"], w["w1"], w["w2"], last)
            self.drip(10 ** 6)
            w = wn
        self.stage_transpose_out()


def build_program(cfg=None):
    nc = bass.Bass("TRN2", target_bir_lowering=False)
    with ExitStack() as st:
        P = Prog(nc, st, cfg)
        P.run_all()
        if cfg and cfg.get("dump_xt"):
            dbg = nc.dram_tensor("dbg_xt", [D, NTOK], F32, kind="ExternalOutput").ap()
            P.dma("sp", dbg, P.XT.t, [P.XT], [])
        P.S.emit()
    return nc, P


TWO_PI = 2.0 * math.pi
MAGIC = 12582912.0


def _hy_geoms():
    return {"lat": dict(L=SEQ, nt=SEQ // 128, nf=SEQ // 128 + 1, r0=CTX),
            "ctx": dict(L=CTX, nt=CTX // 128, nf=CTX // 128 + 1, r0=0)}


def hyena_mixer(self, li, Win, ctx_out):
    NCH = NTOK // 128
    G = _hy_geoms()
    geoms = ["lat"] + (["ctx"] if ctx_out else [])
    YN = self.dram([D, NTOK], BF16, "HYN")
    RAWT = self.dram([3 * D, NTOK], F32, "HRAWT")
    X3 = self.dram([NTOK, 3 * D], F32, "HX3")
    VB = self.dram([NTOK, D], BF16, "HVB")
    Z32 = self.dram([NTOK, D], F32, "HZ32")
    ZB = self.dram([NTOK, D], BF16, "HZB")
    FA = {g: self.dram([2, G[g]["L"], D], BF16, "HFA" + g) for g in geoms}
    FB = {g: self.dram([2, G[g]["L"], D], BF16, "HFB" + g) for g in geoms}
    KRE = {g: self.dram([2, G[g]["nf"] * 128, D], F32, "HKRE" + g) for g in geoms}
    KIM = {g: self.dram([2, G[g]["nf"] * 128, D], F32, "HKIM" + g) for g in geoms}
    YRE = {g: self.dram([G[g]["nf"] * 128, D], BF16, "HYRE" + g) for g in geoms}
    YIM = {g: self.dram([G[g]["nf"] * 128, D], BF16, "HYIM" + g) for g in geoms}

    hst = self.cfg.get('hy_stages', (1, 2, 3, 4, 5))
    self.S.mute = 1 not in hst
    with ExitStack() as st:
        XTb = self.sb([128, DC, 512], F32, "XTb", st)
        U = self.sb([128, DC, 512], BF16, "U", st)
        wring = Ring([self.sb([128, 16, 256], BF16, "wt", st) for _ in range(4)])
        pring = Ring([self.ps([128, 512], F32, "mmp", st) for _ in range(4)])
        rst = Ring([self.sb([128, 512], F32, "rst", st) for _ in range(4)])
        for bi, (t0, nb, isctx) in self.blocks:
            if isctx and not ctx_out:
                continue
            col = 1 if isctx else 0
            self.dma("sp", XTb[:, :, 0:nb], self.xt_v[:, :, t0:t0 + nb], [self.XT.bs[bi]], [XTb])
            self.build_U(li, col, XTb, U, nb, 0, 1)

            def xepi(m, bank):
                r = rst.next()
                if m % 2 == 0:
                    self.tcopy("dve", r[:, 0:nb], bank[:, 0:nb], [bank], [r])
                else:
                    self.act(r[:, 0:nb], bank[:, 0:nb], AF.Copy, [bank], [r])
                self.dma("pool", RAWT.t[m * 128:(m + 1) * 128, t0:t0 + nb], r[:, 0:nb], [r], [RAWT])
            self.linear_fm(Win, D, 3 * D, lambda kc: U[:, kc, 0:nb], [U], nb, xepi, wring, pring)
        self.barrier()

    self.S.mute = 2 not in hst
    with ExitStack() as st:
        CW = self.sb([128, 144], F32, "HCW", st)
        CBv = self.sb([128, 48], F32, "HCB", st)
        cwv = self.inp["hy_conv_w"][0].rearrange("k (m p) -> (k m) p", p=128)
        self.load_vec_fm(cwv[0:128, :], 128, CW[:, 0:128], CW, st)
        self.load_vec_fm(cwv[128:144, :], 16, CW[:, 128:144], CW, st)
        self.load_vec_fm(self.inp["hy_conv_b"][0:1, :].rearrange("o (m p) -> (o m) p", p=128), 48, CBv[:], CBv, st)
        PADW = 2 + CTX + 2 + SEQ
        raws = [self.sb([128, PADW], F32, "hraw", st) for _ in range(2)]
        for r in raws:
            self.memset("dve", r[:], 0.0, [r])
        rawr = Ring(raws)
        accr = Ring([self.sb([128, NTOK], F32, "hacc", st) for _ in range(2)])
        tpr = Ring([self.ps([128, 4, 128], F32, "htp", st) for _ in range(4)])
        tsr = Ring([self.sb([128, NCH, 128], F32, "hts", st) for _ in range(2)])
        tbr = Ring([self.sb([128, NCH, 128], BF16, "htb", st) for _ in range(2)])
        x3v = X3.t.rearrange("(c p) f -> p c f", p=128)
        vbv = VB.t.rearrange("(c p) f -> p c f", p=128)
        c_lo = 0 if ctx_out else 2
        for m in range(48):
            raw = rawr.next()
            if ctx_out:
                self.dma("sp", raw[:, 1:1 + CTX], RAWT.t[m * 128:(m + 1) * 128, 0:CTX], [RAWT], [raw])
            self.dma("sp", raw[:, CTX + 3:CTX + 3 + SEQ], RAWT.t[m * 128:(m + 1) * 128, CTX:NTOK], [RAWT], [raw])
            acc = accr.next()
            segs = ([(0, 0, CTX)] if ctx_out else []) + [(CTX + 2, CTX, SEQ)]
            for (oi, oo, n) in segs:
                self.ts("dve", acc[:, oo:oo + n], raw[:, oi:oi + n], CW[:, m:m + 1], CBv[:, m:m + 1], ALU.mult, ALU.add,
                        [raw, CW, CBv], [acc])
                for k in range(1, 3):
                    self.stt("dve", acc[:, oo:oo + n], raw[:, oi + k:oi + k + n], CW[:, k * 48 + m:k * 48 + m + 1],
                             acc[:, oo:oo + n], ALU.mult, ALU.add, [raw, CW, acc], [acc])
            tsb = tsr.next()
            tbb = tbr.next() if m >= 32 else None
            for c0 in range(c_lo, NCH, 4):
                cn = min(4, NCH - c0)
                tp = tpr.next()
                self.tr([(tp[:, i, :], acc[:, (c0 + i) * 128:(c0 + i + 1) * 128], self.identf[:]) for i in range(cn)],
                        [acc, self.identf], [tp])
                self.tcopy("dve", tsb[:, c0:c0 + cn, :], tp[:, 0:cn, :], [tp], [tsb])
                if tbb is not None:
                    self.tcopy("dve", tbb[:, c0:c0 + cn, :], tp[:, 0:cn, :], [tp], [tbb])
            for c0 in range(c_lo, NCH, 8):
                cn = min(8, NCH - c0)
                self.dma("pool", x3v[:, c0:c0 + cn, m * 128:(m + 1) * 128], tsb[:, c0:c0 + cn, :], [tsb], [X3])
                if tbb is not None:
                    self.dma("pool", vbv[:, c0:c0 + cn, (m - 32) * 128:(m - 31) * 128], tbb[:, c0:c0 + cn, :], [tbb], [VB])
        self.barrier()

    self.S.mute = 3 not in hst
    with ExitStack() as st:
        ABSD = self.sb([128, D], F32, "ABSD", st)
        self.bcast_load(ABSD, ABSD[:], self.inp["habsd"])
        W1 = self.sb([33, 64], F32, "HW1", st)
        W2 = self.sb([64, 64], F32, "HW2", st)
        W3 = self.sb([64, 64], F32, "HW3", st)
        W4 = self.sb([64, 4 * D], F32, "HW4", st)
        self.dma("sp", W1[:], self.inp["hy_f_w1"][0], [], [W1])
        self.dma("sp", W2[:], self.inp["hy_f_w2"][0], [], [W2])
        self.dma("sp", W3[:], self.inp["hy_f_w3"][0], [], [W3])
        self.dma("sp", W4[:], self.inp["hy_f_w4"][0], [], [W4])
        FRB = self.sb([64, 4], F32, "FRB", st)
        tmp4 = self.sb([4, 64], F32, "tmp4", st)
        self.dma("sp", tmp4[0:1, :], self.inp["hy_f_freq"][0:1, :], [], [tmp4])
        self.dma("sp", tmp4[1:2, :], self.inp["hy_f_b1"][0:1, :], [], [tmp4])
        self.dma("sp", tmp4[2:3, :], self.inp["hy_f_b2"][0:1, :], [], [tmp4])
        self.dma("sp", tmp4[3:4, :], self.inp["hy_f_b3"][0:1, :], [], [tmp4])
        p4 = self.ps([64, 4], F32, "p4", st)
        self.tr([(p4[:], tmp4[:], self.identf[0:4, 0:4])], [tmp4, self.identf], [p4])
        self.tcopy("dve", FRB[:], p4[:], [p4], [FRB])
        self.tt("dve", FRB[:, 1:4], FRB[:, 1:4], FRB[:, 0:1].broadcast_to([64, 3]), ALU.mult, [FRB], [FRB])
        pm = Ring([self.ps([128, 512], F32, "hpm", st) for _ in range(4)])
        ar = Ring([self.sb([64, 512], F32, "har", st) for _ in range(2)])
        a2 = Ring([self.sb([64, 512], F32, "ha2", st) for _ in range(2)])
        WIN = self.sb([128, D], F32, "WIN", st)
        hw = Ring([self.sb([128, 512], F32, "hw", st) for _ in range(4)])
        abr = Ring([self.sb([128, 512], BF16, "hab", st) for _ in range(4)])
        for g in geoms:
            L, nt = G[g]["L"], G[g]["nt"]
            zT = self.sb([33, L], F32, "zT" + g, st)
            self.dma("sp", zT[:], self.inp["hz_" + g], [], [zT])
            NT = self.sb([128, nt], F32, "NT" + g, st)
            self.dma("sp", NT[:], self.inp["hnt_" + g], [], [NT])
            Hs = [self.sb([64, L], F32, "H%d%s" % (i, g), st) for i in range(2)]
            src, srcK, Wl = zT, 33, [W1, W2, W3]
            for layer in range(3):
                dst = Hs[layer % 2]
                nbk = min(512, L)
                for cb in range(L // nbk):
                    p = pm.next()
                    self.mm([(p[0:64, 0:nbk], Wl[layer][0:srcK, :], src[0:srcK, cb * nbk:(cb + 1) * nbk], True, True)],
                            [Wl[layer], src], [p])
                    a = ar.next()
                    self.ts("dve", a[:, 0:nbk], p[0:64, 0:nbk], FRB[:, 0:1], FRB[:, layer + 1:layer + 2], ALU.mult, ALU.add,
                            [p, FRB], [a])
                    b = a2.next()
                    self.ts("dve", b[:, 0:nbk], a[:, 0:nbk], 1.0 / TWO_PI, MAGIC, ALU.mult, ALU.add, [a], [b])
                    self.ts("dve", b[:, 0:nbk], b[:, 0:nbk], -MAGIC, -TWO_PI, ALU.add, ALU.mult, [b], [b])
                    self.tt("dve", a[:, 0:nbk], a[:, 0:nbk], b[:, 0:nbk], ALU.add, [a, b], [a])
                    self.ts("dve", a[:, 0:nbk], a[:, 0:nbk], math.pi, -math.pi, ALU.min, ALU.max, [a], [a])
                    self.act(dst[:, cb * nbk:(cb + 1) * nbk], a[:, 0:nbk], AF.Sin, [a], [dst])
                src, srcK = dst, 64
            H3 = src
            for lc in range(nt):
                self.act(WIN[:], ABSD[:], AF.Exp, [ABSD, NT], [WIN], scale=NT[:, lc:lc + 1])
                for o in range(2):
                    for db in range(4):
                        hws = []
                        for dr in range(2):
                            cb = o * 8 + dr * 4 + db
                            p = pm.next()
                            self.mm([(p[:], H3[:, lc * 128:(lc + 1) * 128], W4[:, cb * 512:(cb + 1) * 512], True, True)],
                                    [H3, W4], [p])
                            h = hw.next()
                            self.tt("dve", h[:], p[:], WIN[:, db * 512:(db + 1) * 512], ALU.mult, [p, WIN], [h])
                            hws.append(h)
                        if lc == 0:
                            self.memset("dve", hws[1][0:1, :], 0.0, [hws[1]])
                        A = abr.next()
                        B = abr.next()
                        self.tt("dve", A[:], hws[0][:], hws[1][:], ALU.add, hws, [A])
                        self.tt("dve", B[:], hws[0][:], hws[1][:], ALU.subtract, hws, [B])
                        self.dma("pool", FA[g].t[o, lc * 128:(lc + 1) * 128, db * 512:(db + 1) * 512], A[:], [A], [FA[g]])
                        self.dma("pool", FB[g].t[o, lc * 128:(lc + 1) * 128, db * 512:(db + 1) * 512], B[:], [B], [FB[g]])
        self.barrier()

    def dft_fwd(g, src_ap, src_tt, use_cos, use_sin, epi, st):
        L, nt, nf = G[g]["L"], G[g]["nt"], G[g]["nf"]
        X = self.sb([128, nt, 1024], BF16, "dfX", st)
        cr = Ring([self.sb([128, nt, 128], BF16, "dfC", st) for _ in range(2)])
        sr = Ring([self.sb([128, nt, 128], BF16, "dfS", st) for _ in range(2)])
        pre = Ring([self.ps([128, 512], F32, "dfpr", st) for _ in range(4)])
        pim = Ring([self.ps([128, 512], F32, "dfpi", st) for _ in range(4)])
        sv = src_ap.rearrange("(c p) d -> p c d", p=128)
        for dh in range(2):
            for c0 in range(0, nt, 8):
                cn = min(8, nt - c0)
                self.dma("sp", X[:, c0:c0 + cn, :], sv[:, c0:c0 + cn, dh * 1024:(dh + 1) * 1024], [src_tt], [X])
            for fc in range(nf):
                fs = 128 if fc < nf - 1 else 1
                Ct = St = None
                if use_cos:
                    Ct = cr.next()
                    self.dma("sp", Ct[:], self.inp["hC_" + g][fc], [], [Ct])
                if use_sin:
                    St = sr.next()
                    self.dma("sp", St[:], self.inp["hS_" + g][fc], [], [St])
                for d2 in range(2):
                    dq = dh * 2 + d2
                    xs_ = slice(d2 * 512, (d2 + 1) * 512)
                    pr_ = pi_ = None
                    if use_cos:
                        pr_ = pre.next()
                        self.mm([(pr_[0:fs, :], Ct[:, ac, 0:fs], X[:, ac, xs_], ac == 0, ac == nt - 1) for ac in range(nt)],
                                [Ct, X], [pr_])
                    if use_sin:
                        pi_ = pim.next()
                        self.mm([(pi_[0:fs, :], St[:, ac, 0:fs], X[:, ac, xs_], ac == 0, ac == nt - 1) for ac in range(nt)],
                                [St, X], [pi_])
                    epi(fc, fs, dq, pr_, pi_)

    def dft_inv(g, epi, st):
        L, nt, nf = G[g]["L"], G[g]["nt"], G[g]["nf"]
        YR = self.sb([128, nf, 1024], BF16, "diR", st)
        YI = self.sb([128, nf, 1024], BF16, "diI", st)
        cr = Ring([self.sb([128, nf, 128], BF16, "diC", st) for _ in range(2)])
        sr = Ring([self.sb([128, nf, 128], BF16, "diS", st) for _ in range(2)])
        py = Ring([self.ps([128, 512], F32, "dipy", st) for _ in range(4)])
        yrv = YRE[g].t.rearrange("(c p) d -> p c d", p=128)
        yiv = YIM[g].t.rearrange("(c p) d -> p c d", p=128)
        for dh in range(2):
            for c0 in range(0, nf, 8):
                cn = min(8, nf - c0)
                self.dma("sp", YR[:, c0:c0 + cn, :], yrv[:, c0:c0 + cn, dh * 1024:(dh + 1) * 1024], [YRE[g]], [YR])
                self.dma("sp", YI[:, c0:c0 + cn, :], yiv[:, c0:c0 + cn, dh * 1024:(dh + 1) * 1024], [YIM[g]], [YI])
            for ic in range(nt):
                Ct = cr.next()
                St = sr.next()
                self.dma("sp", Ct[:], self.inp["hCw_" + g][ic], [], [Ct])
                self.dma("sp", St[:], self.inp["hSw_" + g][ic], [], [St])
                for d2 in range(2):
                    dq = dh * 2 + d2
                    xs_ = slice(d2 * 512, (d2 + 1) * 512)
                    p = py.next()
                    specs = []
                    for fc in range(nf):
                        fs = 128 if fc < nf - 1 else 1
                        specs.append((p[:], Ct[0:fs, fc, :], YR[0:fs, fc, xs_], fc == 0, False))
                        specs.append((p[:], St[0:fs, fc, :], YI[0:fs, fc, xs_], False, fc == nf - 1))
                    self.mm(specs, [Ct, St, YR, YI], [p])
                    epi(ic, dq, p)

    self.S.mute = 4 not in hst
    for g in geoms:
        for o in range(2):
            for (use_cos, src, dstK) in ((True, FA[g], KRE[g]), (False, FB[g], KIM[g])):
                with ExitStack() as st:
                    kst = Ring([self.sb([128, 512], F32, "kst", st) for _ in range(3)])

                    def kepi(fc, fs, dq, pr_, pi_, dstK=dstK, o=o, kst=kst):
                        p = pr_ if pr_ is not None else pi_
                        k = kst.next()
                        self.act(k[0:fs, :], p[0:fs, :], AF.Copy, [p], [k])
                        self.dma("pool", dstK.t[o, fc * 128:fc * 128 + fs, dq * 512:(dq + 1) * 512], k[0:fs, :], [k], [dstK])
                    dft_fwd(g, src.t[o], src, use_cos, not use_cos, kepi, st)
                    self.barrier()

    self.S.mute = 5 not in hst
    with ExitStack() as bst:
        HB = self.sb([128, 2, D], F32, "HB", bst)
        self.bcast_load(HB, HB[:].rearrange("p o d -> p (o d)"), self.inp["hy_f_bias"][0:1].rearrange("a o d -> a (o d)"))
        self.barrier()
        for o in range(2):
            for g in geoms:
                L, nt, nf, r0 = G[g]["L"], G[g]["nt"], G[g]["nf"], G[g]["r0"]
                src = VB if o == 0 else ZB
                with ExitStack() as st:
                    kr = Ring([self.sb([128, 512], F32, "kr", st) for _ in range(2)])
                    ki = Ring([self.sb([128, 512], F32, "ki", st) for _ in range(2)])
                    t1 = Ring([self.sb([128, 512], F32, "st1", st) for _ in range(2)])
                    t2 = Ring([self.sb([128, 512], F32, "st2", st) for _ in range(2)])
                    yo = Ring([self.sb([128, 512], BF16, "syo", st) for _ in range(4)])

                    def sepi(fc, fs, dq, pr_, pi_, g=g, o=o, kr=kr, ki=ki, t1=t1, t2=t2, yo=yo):
                        a = kr.next()
                        b = ki.next()
                        rs = slice(fc * 128, fc * 128 + fs)
                        cs = slice(dq * 512, (dq + 1) * 512)
                        self.dma("act", a[0:fs, :], KRE[g].t[o, rs, cs], [KRE[g]], [a])
                        self.dma("act", b[0:fs, :], KIM[g].t[o, rs, cs], [KIM[g]], [b])
                        u1 = t1.next()
                        u2 = t2.next()
                        yr = yo.next()
                        yi = yo.next()
                        self.tt("dve", u1[0:fs, :], pr_[0:fs, :], a[0:fs, :], ALU.mult, [pr_, a], [u1])
                        self.tt("dve", u2[0:fs, :], pi_[0:fs, :], b[0:fs, :], ALU.mult, [pi_, b], [u2])
                        self.tt("dve", yr[0:fs, :], u1[0:fs, :], u2[0:fs, :], ALU.subtract, [u1, u2], [yr])
                        self.tt("dve", u1[0:fs, :], pr_[0:fs, :], b[0:fs, :], ALU.mult, [pr_, b], [u1])
                        self.tt("dve", u2[0:fs, :], pi_[0:fs, :], a[0:fs, :], ALU.mult, [pi_, a], [u2])
                        self.tt("dve", yi[0:fs, :], u1[0:fs, :], u2[0:fs, :], ALU.add, [u1, u2], [yi])
                        self.dma("pool", YRE[g].t[rs, cs], yr[0:fs, :], [yr], [YRE[g]])
                        self.dma("pool", YIM[g].t[rs, cs], yi[0:fs, :], [yi], [YIM[g]])
                    dft_fwd(g, src.t[r0:r0 + L, :], src, True, True, sepi, st)
                    self.barrier()
                with ExitStack() as st:
                    ur = Ring([self.sb([128, 512], F32, "ur", st) for _ in range(2)])
                    xr = Ring([self.sb([128, 512], F32, "xg", st) for _ in range(2)])
                    zr = Ring([self.sb([128, 512], F32, "zo", st) for _ in range(2)])
                    zb = Ring([self.sb([128, 512], BF16, "zob", st) for _ in range(2)])
                    ptp = Ring([self.ps([128, 4, 128], BF16, "iptp", st) for _ in range(2)])
                    yt = Ring([self.sb([128, 4, 128], BF16, "iyt", st) for _ in range(2)])
                    ynv = YN.t.rearrange("(k p) t -> p k t", p=128)

                    def iepi(ic, dq, p, g=g, o=o, r0=r0, ur=ur, xr=xr, zr=zr, zb=zb, ptp=ptp, yt=yt, ynv=ynv):
                        rows = slice(r0 + ic * 128, r0 + (ic + 1) * 128)
                        cs = slice(dq * 512, (dq + 1) * 512)
                        u = ur.next()
                        xg = xr.next()
                        if o == 0:
                            self.dma("act", u[:], X3.t[rows, 2 * D + dq * 512:2 * D + (dq + 1) * 512], [X3], [u])
                            self.dma("act", xg[:], X3.t[rows, dq * 512:(dq + 1) * 512], [X3], [xg])
                        else:
                            self.dma("act", u[:], Z32.t[rows, cs], [Z32], [u])
                            self.dma("act", xg[:], X3.t[rows, D + dq * 512:D + (dq + 1) * 512], [X3], [xg])
                        z = zr.next()
                        self.tt("dve", z[:], u[:], HB[:, o, cs], ALU.mult, [u, HB], [z])
                        self.tt("dve", z[:], z[:], p[:], ALU.add, [z, p], [z])
                        if o == 0:
                            self.tt("dve", z[:], z[:], xg[:], ALU.mult, [z, xg], [z])
                            zbb = zb.next()
                            self.act(zbb[:], z[:], AF.Copy, [z], [zbb])
                            self.dma("pool", Z32.t[rows, cs], z[:], [z], [Z32])
                            self.dma("pool", ZB.t[rows, cs], zbb[:], [zbb], [ZB])
                        else:
                            zbb = zb.next()
                            self.tt("dve", zbb[:], z[:], xg[:], ALU.mult, [z, xg], [zbb])
                            tp = ptp.next()
                            self.tr([(tp[:, i, :], zbb[:, i * 128:(i + 1) * 128], self.identb[:]) for i in range(4)],
                                    [zbb, self.identb], [tp])
                            y = yt.next()
                            self.act(y[:], tp[:], AF.Copy, [tp], [y])
                            c = (r0 // 128) + ic
                            self.dma("pool", ynv[:, dq * 4:(dq + 1) * 4, c * 128:(c + 1) * 128], y[:], [y], [YN])
                    dft_inv(g, iepi, st)
                    self.barrier()
    self.S.mute = False
    return YN


Prog.hyena_mixer = hyena_mixer


_PROG_CACHE = {}


def kernel(**inputs):
    n_cores = 8
    if "nc" not in _PROG_CACHE:
        _PROG_CACHE["nc"] = build_program(None)[0]
    nc = _PROG_CACHE["nc"]
    consts = host_consts_cached()
    f32 = lambda a: np.ascontiguousarray(np.asarray(a, dtype=np.float32))
    x = f32(inputs["x"])
    c = f32(inputs["c"])
    ctx = f32(inputs["ctx"])
    shared = {"c_ctx": f32(inputs["c_ctx"]).reshape(1, D)}
    for n in WEIGHT_SHAPES:
        shared[n] = f32(inputs[n])
    shared.update(consts)
    in_maps = []
    for b in range(n_cores):
        m = dict(shared)
        m["x"] = np.ascontiguousarray(x[b])
        m["c"] = np.ascontiguousarray(c[b:b + 1])
        m["ctx"] = np.ascontiguousarray(ctx[b])
        in_maps.append(m)
    res = run_bass_kernel_spmd(nc, in_maps, core_ids=list(range(n_cores)))
    return np.stack([np.asarray(r["out"], dtype=np.float32) for r in res.results], axis=0)
```
